# Optimizing a Trainium2 kernel written in Bass

```python
import jax, jax.numpy as jnp
from jax import lax
import numpy as np

D_MODEL = 2048
BATCH = 4
SEQ = 4096
DEPTH = 2

FOX_HEADS = 8
FOX_HD = 128
FOX_WIDTH = FOX_HEADS * FOX_HD
Q_BLOCK = 128
MLSTM_HEADS = 4
MLSTM_DQK = 128
MLSTM_DV = 256
MLSTM_QK_WIDTH = MLSTM_HEADS * MLSTM_DQK
MLSTM_V_WIDTH = MLSTM_HEADS * MLSTM_DV
MLSTM_CHUNK = 64
D_FF = 5632
CONV_W = 3
EPS = 1e-6

OFF_FQ = 0
OFF_FK = OFF_FQ + FOX_WIDTH
OFF_FV = OFF_FK + FOX_WIDTH
OFF_FF = OFF_FV + FOX_WIDTH
OFF_MQ = OFF_FF + FOX_HEADS
OFF_MK = OFF_MQ + MLSTM_QK_WIDTH
OFF_MV = OFF_MK + MLSTM_QK_WIDTH
OFF_MO = OFF_MV + MLSTM_V_WIDTH
OFF_MI = OFF_MO + MLSTM_V_WIDTH
OFF_MF = OFF_MI + MLSTM_HEADS
OFF_GA = OFF_MF + MLSTM_HEADS
OFF_GM = OFF_GA + D_MODEL
D_IN = OFF_GM + D_MODEL

kernel_name = "hybrid_fox_mlstm_convffn"


def rmsnorm(x, g):
    x32 = x.astype(jnp.float32)
    y = x32 * lax.rsqrt(jnp.mean(x32 * x32, axis=-1, keepdims=True) + EPS)
    return (y * g.astype(jnp.float32)).astype(x.dtype)


def forgetting_attention(q, k, v, log_f):
    B, S, H, D = q.shape
    F = jnp.cumsum(log_f, axis=1).transpose(0, 2, 1)
    scale = D ** -0.5
    outs = []
    for blk in range(S // Q_BLOCK):
        q0, q1 = blk * Q_BLOCK, (blk + 1) * Q_BLOCK
        qb = q[:, q0:q1].astype(jnp.float32)
        kb = k[:, :q1].astype(jnp.float32)
        vb = v[:, :q1]
        logits = jnp.einsum('bqhd,bkhd->bhqk', qb, kb) * scale
        logits = logits + F[:, :, q0:q1, None] - F[:, :, None, :q1]
        q_pos = q0 + jnp.arange(Q_BLOCK)
        k_pos = jnp.arange(q1)
        causal = k_pos[None, :] <= q_pos[:, None]
        logits = jnp.where(causal, logits, -jnp.inf)
        p = jax.nn.softmax(logits, axis=-1)
        outs.append(jnp.einsum('bhqk,bkhd->bqhd', p.astype(vb.dtype), vb))
    return jnp.concatenate(outs, axis=1)


def mlstm_chunkwise(q, k, v, i_pre, log_f):
    B, S, H, DK = q.shape
    DV = v.shape[-1]
    L = MLSTM_CHUNK
    NC = S // L
    f32 = jnp.float32

    def chunks(a):
        return a.astype(f32).reshape(B, NC, L, H, a.shape[-1]).transpose(1, 0, 3, 2, 4)

    def gchunks(a):
        return a.astype(f32).reshape(B, NC, L, H).transpose(1, 0, 3, 2)

    qc = chunks(q) * (DK ** -0.5)
    kc, vc = chunks(k), chunks(v)
    ic, fc = gchunks(i_pre), gchunks(log_f)
    causal = jnp.tril(jnp.ones((L, L), dtype=bool))

    def step(carry, inp):
        C, n, m = carry
        qt, kt, vt, it, ft = inp
        b = jnp.cumsum(ft, axis=-1)
        dlog = b[..., :, None] - b[..., None, :] + it[..., None, :]
        dlog = jnp.where(causal, dlog, -jnp.inf)
        inter = b + m[..., None]
        m_row = jnp.maximum(inter, dlog.max(axis=-1))
        s = jnp.einsum('bhtd,bhsd->bhts', qt, kt) * jnp.exp(dlog - m_row[..., None])
        inter_w = jnp.exp(inter - m_row)
        num = inter_w[..., None] * jnp.einsum('bhtd,bhde->bhte', qt, C) + jnp.einsum('bhts,bhse->bhte', s, vt)
        den = inter_w * jnp.einsum('bhtd,bhd->bht', qt, n) + s.sum(axis=-1)
        h = num / jnp.maximum(jnp.abs(den), jnp.exp(-m_row))[..., None]
        bL = b[..., -1]
        w_log = bL[..., None] - b + it
        m_new = jnp.maximum(bL + m, w_log.max(axis=-1))
        decay = jnp.exp(bL + m - m_new)
        ws = jnp.exp(w_log - m_new[..., None])
        C_new = decay[..., None, None] * C + jnp.einsum('bhs,bhsd,bhse->bhde', ws, kt, vt)
        n_new = decay[..., None] * n + jnp.einsum('bhs,bhsd->bhd', ws, kt)
        return (C_new, n_new, m_new), h

    init = (jnp.zeros((B, H, DK, DV), f32), jnp.zeros((B, H, DK), f32), jnp.zeros((B, H), f32))
    _, hs = lax.scan(step, init, (qc, kc, vc, ic, fc))
    return hs.transpose(1, 0, 3, 2, 4).reshape(B, S, H, DV).astype(v.dtype)


def causal_dwconv(u, w, b):
    S = u.shape[1]
    up = jnp.pad(u, ((0, 0), (CONV_W - 1, 0), (0, 0)))
    out = b
    for j in range(CONV_W):
        out = out + up[:, j:j + S] * w[j]
    return out


def setup_inputs(seed: int = 0) -> dict:
    key = jax.random.key(seed)
    ks = jax.random.split(key, 20)
    nrm = lambda k, shape, s: jax.random.normal(k, shape, jnp.float32) * s
    L = DEPTH
    return {
        "x": nrm(ks[0], (BATCH, SEQ, D_MODEL), 1.0),
        "norm1_g": 1.0 + nrm(ks[1], (L, D_MODEL), 0.1),
        "w_in": nrm(ks[2], (L, D_MODEL, D_IN), D_MODEL ** -0.5),
        "fox_f_bias": jnp.linspace(1.0, 4.0, FOX_HEADS, dtype=jnp.float32)[None] + nrm(ks[3], (L, FOX_HEADS), 0.1),
        "q_norm_g": 1.0 + nrm(ks[4], (L, FOX_HD), 0.1),
        "k_norm_g": 1.0 + nrm(ks[5], (L, FOX_HD), 0.1),
        "m_i_bias": nrm(ks[6], (L, MLSTM_HEADS), 0.1),
        "m_f_bias": jnp.linspace(3.0, 6.0, MLSTM_HEADS, dtype=jnp.float32)[None] + nrm(ks[7], (L, MLSTM_HEADS), 0.1),
        "m_norm_g": 1.0 + nrm(ks[8], (L, MLSTM_HEADS, MLSTM_DV), 0.1),
        "w_proj_a": nrm(ks[9], (L, FOX_WIDTH, D_MODEL), FOX_WIDTH ** -0.5),
        "w_proj_m": nrm(ks[10], (L, MLSTM_V_WIDTH, D_MODEL), MLSTM_V_WIDTH ** -0.5),
        "w_out": nrm(ks[11], (L, D_MODEL, D_MODEL), D_MODEL ** -0.5),
        "norm2_g": 1.0 + nrm(ks[12], (L, D_MODEL), 0.1),
        "w_up": nrm(ks[13], (L, D_MODEL, 2 * D_FF), D_MODEL ** -0.5),
        "conv_w": nrm(ks[14], (L, CONV_W, 2 * D_FF), CONV_W ** -0.5),
        "conv_b": nrm(ks[15], (L, 2 * D_FF), 0.02),
        "w_down": nrm(ks[16], (L, D_FF, D_MODEL), D_FF ** -0.5),
    }


def reference(x, norm1_g, w_in, fox_f_bias, q_norm_g, k_norm_g, m_i_bias, m_f_bias,
              m_norm_g, w_proj_a, w_proj_m, w_out, norm2_g, w_up, conv_w, conv_b, w_down):
    B, S, _ = x.shape

    for l in range(DEPTH):
        h = rmsnorm(x, norm1_g[l])
        z = jnp.einsum('bsd,de->bse', h, w_in[l])

        fq = rmsnorm(z[..., OFF_FQ:OFF_FK].reshape(B, S, FOX_HEADS, FOX_HD), q_norm_g[l])
        fk = rmsnorm(z[..., OFF_FK:OFF_FV].reshape(B, S, FOX_HEADS, FOX_HD), k_norm_g[l])
        fv = z[..., OFF_FV:OFF_FF].reshape(B, S, FOX_HEADS, FOX_HD)
        f_logf = jax.nn.log_sigmoid((z[..., OFF_FF:OFF_MQ] + fox_f_bias[l]).astype(jnp.float32))
        a_out = forgetting_attention(fq, fk, fv, f_logf).reshape(B, S, FOX_WIDTH)

        mq = z[..., OFF_MQ:OFF_MK].reshape(B, S, MLSTM_HEADS, MLSTM_DQK)
        mk = z[..., OFF_MK:OFF_MV].reshape(B, S, MLSTM_HEADS, MLSTM_DQK)
        mv = z[..., OFF_MV:OFF_MO].reshape(B, S, MLSTM_HEADS, MLSTM_DV)
        m_o = jax.nn.sigmoid(z[..., OFF_MO:OFF_MI]).reshape(B, S, MLSTM_HEADS, MLSTM_DV)
        m_ipre = (z[..., OFF_MI:OFF_MF] + m_i_bias[l]).astype(jnp.float32)
        m_logf = jax.nn.log_sigmoid((z[..., OFF_MF:OFF_GA] + m_f_bias[l]).astype(jnp.float32))
        hm = mlstm_chunkwise(mq, mk, mv, m_ipre, m_logf)
        m_out = (rmsnorm(hm, m_norm_g[l]) * m_o).reshape(B, S, MLSTM_V_WIDTH)

        g_a = jax.nn.sigmoid(z[..., OFF_GA:OFF_GM])
        g_m = jax.nn.sigmoid(z[..., OFF_GM:D_IN])
        merged = (g_a * jnp.einsum('bse,ed->bsd', a_out, w_proj_a[l])
                  + g_m * jnp.einsum('bse,ed->bsd', m_out, w_proj_m[l]))
        x = x + jnp.einsum('bsd,de->bse', merged, w_out[l])

        h2 = rmsnorm(x, norm2_g[l])
        u = causal_dwconv(jnp.einsum('bsd,df->bsf', h2, w_up[l]), conv_w[l], conv_b[l])
        u_gate, u_val = u[..., :D_FF], u[..., D_FF:]
        x = x + jnp.einsum('bsf,fd->bsd', jax.nn.silu(u_gate) * u_val, w_down[l])
    return x
```

```python
import math
from contextlib import ExitStack

import numpy as np
import concourse.bass as bass
import concourse.mybir as mybir
from concourse.bass_utils import run_bass_kernel_spmd

F32 = mybir.dt.float32
BF16 = mybir.dt.bfloat16
U8 = mybir.dt.uint8
AF = mybir.ActivationFunctionType
ALU = mybir.AluOpType
AX = mybir.AxisListType

D = 2048
DIN = 10256
DFF = 5632
NL = 2
NH = 8
MH = 4
EPS = 1e-6
KC = D // 128
FC = DFF // 128
ENG = ("pe", "act", "dve", "pool", "sp")


class Buf:
    def __init__(self, name):
        self.name = name
        self.writers = []
        self.readers = []
        self.sem = None
        self.nd = 0


class Op:
    pass


class Prog:
    def __init__(self, nc, strict=("act", "dve", "pool")):
        self.nc = nc
        self.ops = {e: [] for e in ENG}
        self.bufs = []
        self.strict = strict

    def buf(self, name):
        for b in self.bufs:
            if b.name == name:
                return b
        b = Buf(name)
        self.bufs.append(b)
        return b

    def add(self, eng, fn, r=(), w=(), dma=False):
        op = Op()
        op.eng, op.fn, op.dma, op.signal = eng, fn, dma, False
        op.deps_eng, op.deps_sem = {}, {}

        def dep(d):
            if d.dma:
                k = id(d.dst)
                v = op.deps_sem.get(k)
                if v is None or v[1] < d.count:
                    op.deps_sem[k] = (d.dst, d.count)
            else:
                if d.eng == eng and eng not in self.strict:
                    return
                if op.deps_eng.get(d.eng, -1) < d.seq:
                    op.deps_eng[d.eng] = d.seq

        for b in r:
            for d in b.writers:
                dep(d)
        for b in w:
            for d in b.readers:
                dep(d)
        for b in r:
            b.readers.append(op)
        for b in w:
            if b.readers:
                b.writers = [op]
                b.readers = []
            else:
                b.writers.append(op)
        op.seq = len(self.ops[eng])
        self.ops[eng].append(op)
        if dma:
            op.dst = w[0]
            op.dst.nd += 1
            op.count = 16 * op.dst.nd
        return op

    def barrier(self):
        lasts = {}
        for e in ENG:
            if e == "sp":
                continue
            for op in reversed(self.ops[e]):
                if op.fn is not None:
                    lasts[e] = op.seq
                    break
        sems = {id(b): (b, 16 * b.nd) for b in self.bufs if b.nd > 0}
        for e in ENG:
            op = Op()
            op.eng, op.fn, op.dma, op.signal = e, None, False, False
            op.deps_eng = {e2: s for e2, s in lasts.items() if e2 != e}
            op.deps_sem = dict(sems)
            op.seq = len(self.ops[e])
            self.ops[e].append(op)
        for b in self.bufs:
            b.writers = []
            b.readers = []

    def emit(self):
        nc = self.nc
        for e in ENG:
            for op in self.ops[e]:
                for pe_, s in op.deps_eng.items():
                    self.ops[pe_][s].signal = True
        for e in ENG:
            c = 0
            for op in self.ops[e]:
                if op.signal:
                    c += 1
                op.sig = c
        with ExitStack() as es:
            esem = {e: es.enter_context(nc.semaphore("S_" + e)) for e in ENG}
            for b in self.bufs:
                if b.nd > 0:
                    b.sem = es.enter_context(nc.semaphore("D_" + b.name))
            block = es.enter_context(nc.Block())

            def run(e, eng):
                waited = {}
                for op in self.ops[e]:
                    for pe_, s in op.deps_eng.items():
                        v = self.ops[pe_][s].sig
                        key = ("e", pe_)
                        if waited.get(key, 0) < v:
                            eng.wait_ge(esem[pe_], v)
                            waited[key] = v
                    for (dst, cnt) in op.deps_sem.values():
                        key = ("d", id(dst))
                        if waited.get(key, 0) < cnt:
                            eng.wait_ge(dst.sem, cnt)
                            waited[key] = cnt
                    if op.fn is not None:
                        inst = op.fn(eng)
                        if op.dma:
                            inst.then_inc(op.dst.sem, 16)
                        elif op.signal:
                            inst.then_inc(esem[e], 1)

            @block.tensor
            def _(eng):
                run("pe", eng)

            @block.scalar
            def _(eng):
                run("act", eng)

            @block.vector
            def _(eng):
                run("dve", eng)

            @block.gpsimd
            def _(eng):
                run("pool", eng)

            @block.sync
            def _(eng):
                run("sp", eng)


class Arena:
    def __init__(self, handle, size):
        self.h, self.size, self.off = handle, size, 0

    def reset(self, keep=0):
        self.off = keep

    def alloc(self, shape, dtype):
        esz = 2 if dtype == BF16 else 4
        n = 1
        for s in shape[1:]:
            n *= s
        nbytes = (n * esz + 63) // 64 * 64
        assert self.off + nbytes <= self.size, ("SBUF arena overflow", self.off, nbytes)
        v = self.h[0:shape[0], self.off:self.off + n * esz].bitcast(dtype)
        self.off += nbytes
        if len(shape) == 3:
            v = v.rearrange("p (a b) -> p a b", a=shape[1])
        elif len(shape) == 4:
            v = v.rearrange("p (a b c) -> p a b c", a=shape[1], b=shape[2])
        return v


def build(S=4096, n_layers=NL, debug=(), stop_after=None):
    nc = bass.Bass("TRN2", target_bir_lowering=False)
    P = Prog(nc)
    NT = S // 128
    T = S // 2
    HT = T // 128
    TG = T // 512

    def dram(name, shape, dt, kind="Internal"):
        if name in debug:
            kind = "ExternalOutput"
        return nc.dram_tensor(name, list(shape), dt, kind=kind).ap()

    x_in = dram("x", [S, D], F32, "ExternalInput")
    w_in = dram("w_in", [NL, D, DIN], F32, "ExternalInput")
    w_pa = dram("w_pa", [NL, 1024, D], F32, "ExternalInput")
    w_pm = dram("w_pm", [NL, 1024, D], F32, "ExternalInput")
    w_out = dram("w_out", [NL, D, D], F32, "ExternalInput")
    w_up = dram("w_up", [NL, D, 2 * DFF], F32, "ExternalInput")
    w_down = dram("w_down", [NL, DFF, D], F32, "ExternalInput")
    g1_d = dram("g1", [NL, D], F32, "ExternalInput")
    g2_d = dram("g2", [NL, D], F32, "ExternalInput")
    gbias_d = dram("gbias", [NL, 128, 16], F32, "ExternalInput")
    qkg_d = dram("qkg", [NL, 128, 2], F32, "ExternalInput")
    mng_d = dram("mng", [NL, 128, 1024], F32, "ExternalInput")
    cw_d = dram("cw", [NL, 128, 2 * FC, 3], F32, "ExternalInput")
    cb_d = dram("cb", [NL, 128, 2 * FC], F32, "ExternalInput")
    consts_d = dram("consts", [128, 3, 128], F32, "ExternalInput")
    y_out = dram("y", [S, D], F32, "ExternalOutput")

    qT_s = dram("qT_s", [NH, 128, S], BF16)
    kT_s = dram("kT_s", [NH, 128, S], BF16)
    vp_s = dram("vp_s", [NH, 128, NT, 129], BF16)
    tot_s = dram("tot_s", [NT, 12], F32)
    mqT_s = dram("mqT_s", [MH, 128, S], BF16)
    mkT_s = dram("mkT_s", [MH, 128, S], BF16)
    mk_s = dram("mk_s", [NT, 128, 512], BF16)
    mv_s = dram("mv_s", [NT, 128, MH, 257], BF16)
    mo_s = dram("mo_s", [NT, 128, 1024], BF16)
    gaT_s = dram("gaT_s", [KC, 128, S], BF16)
    gmT_s = dram("gmT_s", [KC, 128, S], BF16)
    aoT_s = dram("aoT_s", [NH, 128, S], BF16)
    moT_s = dram("moT_s", [NH, 128, S], BF16)
    x_mid = dram("x_mid", [S, D], F32)
    x_l1 = dram("x_l1", [S, D], F32)
    actT_s = dram("actT_s", [FC, 128, S], BF16)

    B_x_in = P.buf("x_in")
    B = {n: P.buf(n) for n in ("qT_s", "kT_s", "vp_s", "tot_s", "mqT_s", "mkT_s", "mk_s", "mv_s",
                               "mo_s", "gaT_s", "gmT_s", "aoT_s", "moT_s", "x_mid", "x_l1", "actT_s", "y")}
    B_w = P.buf("weights")

    with ExitStack() as es:
        ARENA_BYTES = 204 * 1024
        arena_h = es.enter_context(nc.sbuf_tensor("arena", [128, ARENA_BYTES], U8))
        A = Arena(arena_h, ARENA_BYTES)
        psum = [es.enter_context(nc.psum_tensor("ps%d" % i, [128, 512], F32)) for i in range(8)]
        Bps = [P.buf("ps%d" % i) for i in range(8)]

        def ps_bf(i):
            return psum[i][:, :].bitcast(BF16)

        cst = A.alloc([128, 3, 128], F32)
        ident_b = A.alloc([128, 128], BF16)
        ones_b = A.alloc([128, 128], BF16)
        tri_b = A.alloc([128, 128], BF16)
        B_cst = P.buf("cst")
        P.add("sp", lambda e: e.dma_start(out=cst, in_=consts_d), r=[B_w], w=[B_cst], dma=True)
        P.add("dve", lambda e: e.tensor_copy(out=ident_b, in_=cst[:, 0, :]), r=[B_cst], w=[B_cst])
        P.add("dve", lambda e: e.tensor_copy(out=tri_b, in_=cst[:, 1, :]), r=[B_cst], w=[B_cst])
        P.add("dve", lambda e: e.tensor_copy(out=ones_b, in_=cst[:, 2, :]), r=[B_cst], w=[B_cst])
        tri_f = cst[:, 1, :]
        ones_f = cst[:, 2, :]
        KEEP = A.off
        P.barrier()

        class WTile:
            def __init__(self, ws, srcs, ncols):
                self.ws, self.srcs, self.ncols = ws, srcs, ncols
                slot = ws.wi % len(ws.wb)
                ws.wi += 1
                self.wb, self.Bwb = ws.wb[slot], ws.Bwb[slot]
                self.pieces = [(k0, min(ws.nk, k0 + ws.kp)) for k0 in range(0, ws.nk, ws.kp)]
                self.stg_of = {}
                self.nd = 0
                self.ncst = 0

            def _dma(self):
                ws = self.ws
                k0, k1 = self.pieces[self.nd]
                si = ws.si % len(ws.stg)
                ws.si += 1
                stg, Bstg = ws.stg[si], ws.Bstg[si]
                self.stg_of[self.nd] = (stg, Bstg)
                self.nd += 1
                for (c0, c1, fn) in self.srcs:
                    P.add("sp", lambda e, stg=stg, k0=k0, k1=k1, c0=c0, c1=c1, fn=fn: e.dma_start(
                        out=stg[:, 0:k1 - k0, c0:c1], in_=fn(k0, k1)), r=[B_w], w=[Bstg], dma=True)

            def step(self):
                if self.ncst >= len(self.pieces):
                    return False
                while self.nd < min(len(self.pieces), self.ncst + 2):
                    self._dma()
                k0, k1 = self.pieces[self.ncst]
                stg, Bstg = self.stg_of.pop(self.ncst)
                self.ncst += 1
                wb, ncols = self.wb, self.ncols
                P.add("pool", lambda e, stg=stg, k0=k0, k1=k1: e.tensor_copy(
                    out=wb[:, k0:k1, 0:ncols], in_=stg[:, 0:k1 - k0, 0:ncols]), r=[Bstg], w=[self.Bwb])
                return True

            def finish(self):
                while self.step():
                    pass
                return self.wb, self.Bwb

        class WStream:
            def __init__(self, nk, ncols, nslots=2, kp=4, nstg=4, name="w"):
                self.nk, self.ncols, self.kp = nk, ncols, kp
                self.stg = [A.alloc([128, kp, ncols], F32) for _ in range(nstg)]
                self.Bstg = [P.buf("%s_stg%d" % (name, i)) for i in range(nstg)]
                self.wb = [A.alloc([128, nk, ncols], BF16) for _ in range(nslots)]
                self.Bwb = [P.buf("%s_wb%d" % (name, i)) for i in range(nslots)]
                self.si = 0
                self.wi = 0

            def tile(self, src_fn=None, srcs=None, ncols=None):
                ncols = ncols or self.ncols
                if srcs is None:
                    srcs = [(0, ncols, src_fn)]
                return WTile(self, srcs, ncols)

        def pipelined(ws, tiles_srcs, nunits, first=None):
            cur = first
            if cur is None:
                cur = ws.tile(**tiles_srcs[0])
                cur.finish()
            for n in range(len(tiles_srcs)):
                nxt = ws.tile(**tiles_srcs[n + 1]) if n + 1 < len(tiles_srcs) else None
                npieces = len(nxt.pieces) if nxt is not None else 0
                state = {"done": 0}

                def tick(u, nxt=nxt, npieces=npieces, state=state):
                    if nxt is None:
                        return
                    want = min(npieces, (u + 1) * npieces // nunits + 1)
                    while state["done"] < want and nxt.step():
                        state["done"] += 1
                yield n, cur.wb, cur.Bwb, tick
                if nxt is not None:
                    nxt.finish()
                cur = nxt

        def mm_acc(ps_ap, Bp, pairs, rbufs):
            n = len(pairs)
            for i, (l, r_) in enumerate(pairs):
                P.add("pe", lambda e, l=l, r_=r_, i=i: e.matmul(ps_ap, l, r_, start=(i == 0), stop=(i == n - 1)),
                      r=rbufs, w=[Bp])

        def phase_norm(src_ap, Bsrc, r0, hT, BhT, g_row):
            gbc = A.alloc([128, D], F32)
            Bgbc = P.buf("gbc")
            P.add("sp", lambda e: e.dma_start(out=gbc, in_=g_row.partition_broadcast(128)), r=[B_w], w=[Bgbc], dma=True)
            xt = [A.alloc([128, D], F32) for _ in range(2)]
            Bxt = [P.buf("xt%d" % i) for i in range(2)]
            hn = [A.alloc([128, D], BF16) for _ in range(2)]
            Bhn = [P.buf("hn%d" % i) for i in range(2)]
            junk = A.alloc([128, D], BF16)
            Bjunk = P.buf("junk")
            st = A.alloc([128, 8], F32)
            Bst = P.buf("st")
            for i in range(HT):
                s = i % 2
                rows = slice(r0 + i * 128, r0 + (i + 1) * 128)
                P.add("sp", lambda e, s=s, rows=rows: e.dma_start(out=xt[s], in_=src_ap[rows, :]),
                      r=[Bsrc], w=[Bxt[s]], dma=True)
                P.add("act", lambda e, s=s: e.activation(out=junk, in_=xt[s], func=AF.Square, accum_out=st[:, 0:1]),
                      r=[Bxt[s]], w=[Bjunk, Bst])
                P.add("act", lambda e: e.activation(out=st[:, 1:2], in_=st[:, 0:1], func=AF.Sqrt,
                                                    scale=1.0 / D, bias=eps_t[:, 0:1]), r=[Bst], w=[Bst])
                P.add("dve", lambda e: e.reciprocal(out=st[:, 2:3], in_=st[:, 1:2]), r=[Bst], w=[Bst])
                P.add("dve", lambda e, s=s: e.scalar_tensor_tensor(out=hn[s], in0=xt[s], scalar=st[:, 2:3], in1=gbc,
                                                                   op0=ALU.mult, op1=ALU.mult),
                      r=[Bst, Bxt[s], Bgbc], w=[Bhn[s]])
                for q in range(4):
                    pb = 4 + (q % 2)
                    pv = ps_bf(pb)
                    for c4 in range(4):
                        c = q * 4 + c4
                        P.add("pe", lambda e, s=s, c=c, c4=c4, pv=pv: e.transpose(
                            out=pv[:, c4 * 128:(c4 + 1) * 128], in_=hn[s][:, c * 128:(c + 1) * 128],
                            identity=ident_b), r=[Bhn[s], B_cst], w=[Bps[pb]])
                    ce = "act" if q % 2 == 0 else "dve"
                    dst = hT[:, q * 4:(q + 1) * 4, i * 128:(i + 1) * 128]
                    srcv = pv[:, 0:512].rearrange("p (a b) -> p a b", a=4)
                    if ce == "act":
                        P.add("act", lambda e, dst=dst, srcv=srcv: e.copy(out=dst, in_=srcv), r=[Bps[pb]], w=[BhT[i]])
                    else:
                        P.add("dve", lambda e, dst=dst, srcv=srcv: e.tensor_copy(out=dst, in_=srcv),
                              r=[Bps[pb]], w=[BhT[i]])
                yield

        eps_t = None

        def phase1(l, j, src_ap, Bsrc):
            A.reset(KEEP2)
            r0 = j * T
            hT = A.alloc([128, KC, T], BF16)
            BhT = [P.buf("hT%d" % i) for i in range(HT)]
            mark = A.off
            gb = A.alloc([128, 16], F32)
            qkg = A.alloc([128, 4], F32)
            Bpar = P.buf("par1")
            P.add("sp", lambda e: e.dma_start(out=gb, in_=gbias_d[l]), r=[B_w], w=[Bpar], dma=True)
            P.add("sp", lambda e: e.dma_start(out=qkg[:, 0:2], in_=qkg_d[l]), r=[B_w], w=[Bpar], dma=True)
            P.add("dve", lambda e: e.scalar_tensor_tensor(out=qkg[:, 2:3], in0=qkg[:, 0:1], scalar=float(128 ** -0.5),
                                                          in1=qkg[:, 1:2], op0=ALU.mult, op1=ALU.mult),
                  r=[Bpar], w=[Bpar])
            E_all = A.alloc([128, HT, 8], F32)
            QS_all = A.alloc([128, HT, 4], F32)
            KS_all = A.alloc([128, HT, 4], F32)
            Bgate = P.buf("gates")
            ws = WStream(KC, 512, name="win")
            wsm_stg = A.alloc([128, KC, 16], F32)
            wsm = A.alloc([128, KC, 16], BF16)
            Bwsm_stg, Bwsm = P.buf("wsm_stg"), P.buf("wsm")
            NST = 3
            st_bf = [A.alloc([128, 4, 132], BF16) for _ in range(NST)]
            Bst_bf = [P.buf("st_bf%d" % i) for i in range(NST)]
            st2_bf = [A.alloc([128, 512], BF16) for _ in range(NST)]
            Bst2_bf = [P.buf("st2_bf%d" % i) for i in range(NST)]
            mvst = [A.alloc([128, 2, 257], BF16) for _ in range(2)]
            Bmvst = [P.buf("mvst%d" % i) for i in range(2)]
            tmp_f = [A.alloc([128, 512], F32) for _ in range(2)]
            Btmp_f = [P.buf("tmp_f%d" % i) for i in range(2)]
            sm = A.alloc([128, 64], F32)
            Bsm = P.buf("sm")
            tsb = A.alloc([128, 16], F32)
            Btsb = P.buf("tsb")
            for s in range(2):
                P.add("pool", lambda e, s=s: e.memset(mvst[s][:, :, 256:257], 1.0), w=[Bmvst[s]])
            cnt = {"st": 0, "st2": 0, "mv": 0, "tf": 0, "ps": 0}

            def nxt(k, n):
                v = cnt[k] % n
                cnt[k] += 1
                return v

            groups = []
            groups += [("fv", 2048 + 512 * g, g) for g in range(2)]
            groups += [("mq", 3080, 0), ("mk", 3592, 0)]
            groups += [("mv", 4104 + 512 * g, g) for g in range(2)]
            groups += [("mo", 5128 + 512 * g, g) for g in range(2)]
            groups += [("fq", 512 * g, g) for g in range(2)]
            groups += [("fk", 1024 + 512 * g, g) for g in range(2)]
            groups += [("ga", 6160 + 512 * g, g) for g in range(4)]
            groups += [("gm", 8208 + 512 * g, g) for g in range(4)]

            def sm_src(c0, c1, lo, hi):
                return w_in[l, :, lo:hi].rearrange("(c p) n -> p c n", p=128)
            with nc.allow_non_contiguous_dma(reason="tiny gate columns"):
                for (dlo, lo, hi) in ((0, 3072, 3080), (8, 6156, 6160), (12, 6152, 6156)):
                    P.add("sp", lambda e, dlo=dlo, lo=lo, hi=hi: e.dma_start(
                        out=wsm_stg[:, :, dlo:dlo + hi - lo],
                        in_=w_in[l, :, lo:hi].rearrange("(c p) n -> p c n", p=128)),
                        r=[B_w], w=[Bwsm_stg], dma=True)
            for k in range(KC):
                P.add("dve", lambda e, k=k: e.tensor_copy(out=wsm[:, k, :], in_=wsm_stg[:, k, :]),
                      r=[Bwsm_stg], w=[Bwsm])
            gsrcs = [dict(src_fn=lambda k0, k1, c0=c0: w_in[l, k0 * 128:k1 * 128, c0:c0 + 512].rearrange(
                "(c p) n -> p c n", p=128)) for (kind, c0, g) in groups]
            first_tile = ws.tile(**gsrcs[0])
            first_tile.finish()
            norm_it = phase_norm(src_ap, Bsrc, r0, hT, BhT, g1_d[l:l + 1, :])
            zb = sm[:, 0:16]
            e1 = sm[:, 16:32]
            sp_ = sm[:, 32:48]
            t4 = sm[:, 48:52]

            def smA(i):
                tok = slice(i * 128, (i + 1) * 128)
                mm_acc(psum[7][:, 0:16], Bps[7], [(hT[:, c, tok], wsm[:, c, :]) for c in range(KC)], [BhT[i], Bwsm])
                P.add("dve", lambda e: e.tensor_tensor(out=zb, in0=psum[7][:, 0:16], in1=gb, op=ALU.add),
                      r=[Bps[7], Bpar], w=[Bsm])
                P.add("act", lambda e: e.activation(out=e1, in_=zb, func=AF.Exp, scale=-1.0), r=[Bsm], w=[Bsm])
                P.add("act", lambda e: e.activation(out=sp_, in_=e1, func=AF.Ln, bias=one_t[:, 0:1]), r=[Bsm], w=[Bsm])

            def smB(i):
                gi = j * HT + i
                P.add("pe", lambda e: e.matmul(psum[7][:, 32:44], tri_f, sp_[:, 0:12], start=True, stop=True),
                      r=[Bsm, B_cst], w=[Bps[7]])
                P.add("pe", lambda e: e.matmul(psum[7][:, 64:76], ones_f, sp_[:, 0:12], start=True, stop=True),
                      r=[Bsm, B_cst], w=[Bps[7]])
                cum = psum[7][:, 32:44]
                P.add("act", lambda e, i=i: e.activation(out=E_all[:, i, :], in_=cum[:, 0:8], func=AF.Exp),
                      r=[Bps[7]], w=[Bgate])
                P.add("act", lambda e, i=i: e.activation(out=QS_all[:, i, :], in_=cum[:, 8:12], func=AF.Exp,
                                                         scale=-1.0, bias=lnsc_t[:, 0:1]), r=[Bps[7]], w=[Bgate])
                P.add("dve", lambda e: e.tensor_tensor(out=t4, in0=cum[:, 8:12], in1=zb[:, 12:16], op=ALU.add),
                      r=[Bps[7], Bsm], w=[Bsm])
                P.add("act", lambda e, i=i: e.activation(out=KS_all[:, i, :], in_=t4, func=AF.Exp),
                      r=[Bsm], w=[Bgate])
                P.add("act", lambda e: e.copy(out=tsb[:, 0:12], in_=psum[7][:, 64:76]), r=[Bps[7]], w=[Btsb])
                P.add("sp", lambda e, gi=gi: e.dma_start(out=tot_s[gi:gi + 1, :], in_=tsb[0:1, 0:12]),
                      r=[Btsb], w=[B["tot_s"]], dma=True)

            for gn, wb, Bwb, tick in pipelined(ws, gsrcs, 16, first=first_tile):
                kind, c0, g = groups[gn]
                if kind in ("fv", "mq", "mk", "mv", "mo"):
                  def tm_unit(i, wb=wb, Bwb=Bwb, kind=kind, g=g):
                    if True:
                        gi = j * HT + i
                        tok = slice(i * 128, (i + 1) * 128)
                        pb = nxt("ps", 4)
                        ps = psum[pb]
                        mm_acc(ps[:, :], Bps[pb], [(hT[:, c, tok], wb[:, c, :]) for c in range(KC)], [BhT[i], Bwb])
                        if kind == "fv":
                            s = nxt("st", NST)
                            h0 = g * 4
                            P.add("dve", lambda e, s=s, ps=ps, i=i, h0=h0: e.tensor_tensor(
                                out=st_bf[s][:, :, 0:128], in0=ps[:, :].rearrange("p (h d) -> p h d", h=4),
                                in1=E_all[:, i, h0:h0 + 4].unsqueeze(2).to_broadcast([128, 4, 128]), op=ALU.mult),
                                r=[Bps[pb], Bgate], w=[Bst_bf[s]])
                            P.add("dve", lambda e, s=s, i=i, h0=h0: e.tensor_copy(
                                out=st_bf[s][:, :, 128:129], in_=E_all[:, i, h0:h0 + 4].unsqueeze(2)),
                                r=[Bgate], w=[Bst_bf[s]])
                            P.add("sp", lambda e, s=s, h0=h0, gi=gi: e.dma_start(
                                out=vp_s[h0:h0 + 4, :, gi, :].rearrange("h p c -> p h c"),
                                in_=st_bf[s][:, :, 0:129]), r=[Bst_bf[s]], w=[B["vp_s"]], dma=True)
                        elif kind in ("mq", "mk"):
                            s = nxt("st2", NST)
                            SC = QS_all if kind == "mq" else KS_all
                            P.add("dve", lambda e, s=s, ps=ps, i=i, SC=SC: e.tensor_tensor(
                                out=st2_bf[s][:, :].rearrange("p (h d) -> p h d", h=4),
                                in0=ps[:, :].rearrange("p (h d) -> p h d", h=4),
                                in1=SC[:, i, 0:4].unsqueeze(2).to_broadcast([128, 4, 128]), op=ALU.mult),
                                r=[Bps[pb], Bgate], w=[Bst2_bf[s]])
                            if kind == "mk":
                                P.add("sp", lambda e, s=s, gi=gi: e.dma_start(out=mk_s[gi], in_=st2_bf[s]),
                                      r=[Bst2_bf[s]], w=[B["mk_s"]], dma=True)
                            tb = 4 + (i % 2)
                            pv = ps_bf(tb)
                            for hh in range(4):
                                P.add("pe", lambda e, s=s, hh=hh, pv=pv: e.transpose(
                                    out=pv[:, hh * 128:(hh + 1) * 128], in_=st2_bf[s][:, hh * 128:(hh + 1) * 128],
                                    identity=ident_b), r=[Bst2_bf[s], B_cst], w=[Bps[tb]])
                            s2 = nxt("st2", NST)
                            P.add("act", lambda e, s2=s2, pv=pv: e.copy(out=st2_bf[s2], in_=pv[:, 0:512]),
                                  r=[Bps[tb]], w=[Bst2_bf[s2]])
                            dstT = mqT_s if kind == "mq" else mkT_s
                            Bd = B["mqT_s"] if kind == "mq" else B["mkT_s"]
                            P.add("sp", lambda e, s2=s2, dstT=dstT, gi=gi: e.dma_start(
                                out=dstT[:, :, gi * 128:(gi + 1) * 128].rearrange("h p t -> p h t"),
                                in_=st2_bf[s2][:, :].rearrange("p (h t) -> p h t", h=4)),
                                r=[Bst2_bf[s2]], w=[Bd], dma=True)
                        elif kind == "mv":
                            s = nxt("mv", 2)
                            P.add("act", lambda e, s=s, ps=ps: e.copy(
                                out=mvst[s][:, :, 0:256], in_=ps[:, :].rearrange("p (h d) -> p h d", h=2)),
                                r=[Bps[pb]], w=[Bmvst[s]])
                            P.add("sp", lambda e, s=s, gi=gi, g=g: e.dma_start(
                                out=mv_s[gi, :, 2 * g:2 * g + 2, :], in_=mvst[s]),
                                r=[Bmvst[s]], w=[B["mv_s"]], dma=True)
                        else:
                            s = nxt("st2", NST)
                            P.add("act", lambda e, s=s, ps=ps: e.activation(out=st2_bf[s], in_=ps[:, :],
                                                                            func=AF.Sigmoid),
                                  r=[Bps[pb]], w=[Bst2_bf[s]])
                            P.add("sp", lambda e, s=s, gi=gi, g=g: e.dma_start(
                                out=mo_s[gi, :, 512 * g:512 * (g + 1)], in_=st2_bf[s]),
                                r=[Bst2_bf[s]], w=[B["mo_s"]], dma=True)
                  if gn == 0:
                      for i in range(HT):
                          next(norm_it, None)
                          smA(i)
                          if i >= 1:
                              tick(i - 1)
                              tm_unit(i - 1)
                          smB(i)
                      tick(HT - 1)
                      tm_unit(HT - 1)
                  else:
                      for i in range(HT):
                          tick(i)
                          tm_unit(i)
                else:
                    for q in range(4):
                        ch = g * 4 + q
                        for tg in range(TG):
                            tick(q * TG + tg)
                            tks = slice(tg * 512, (tg + 1) * 512)
                            gt0 = r0 + tg * 512
                            pb = nxt("ps", 4)
                            ps = psum[pb]
                            mm_acc(ps[:, :], Bps[pb], [(wb[:, c, q * 128:(q + 1) * 128], hT[:, c, tks])
                                                       for c in range(KC)], BhT[4 * tg:4 * tg + 4] + [Bwb])
                            if kind in ("ga", "gm"):
                                s = nxt("st2", NST)
                                P.add("act", lambda e, s=s, ps=ps: e.activation(out=st2_bf[s], in_=ps[:, :],
                                                                                func=AF.Sigmoid),
                                      r=[Bps[pb]], w=[Bst2_bf[s]])
                                dstT = gaT_s if kind == "ga" else gmT_s
                                Bd = B["gaT_s"] if kind == "ga" else B["gmT_s"]
                                P.add("sp", lambda e, s=s, dstT=dstT, ch=ch, gt0=gt0: e.dma_start(
                                    out=dstT[ch, :, gt0:gt0 + 512], in_=st2_bf[s]),
                                    r=[Bst2_bf[s]], w=[Bd], dma=True)
                            else:
                                s = nxt("st2", NST)
                                tf = nxt("tf", 2)
                                P.add("act", lambda e, s=s, ps=ps: e.activation(out=st2_bf[s], in_=ps[:, :],
                                                                                func=AF.Square),
                                      r=[Bps[pb]], w=[Bst2_bf[s]])
                                P.add("pe", lambda e, s=s: e.matmul(psum[6][:, :], ones_b, st2_bf[s],
                                                                    start=True, stop=True),
                                      r=[Bst2_bf[s], B_cst], w=[Bps[6]])
                                P.add("act", lambda e, tf=tf: e.activation(out=tmp_f[tf], in_=psum[6][:, :],
                                                                           func=AF.Sqrt, scale=1.0 / 128,
                                                                           bias=eps_t[:, 0:1]),
                                      r=[Bps[6]], w=[Btmp_f[tf]])
                                P.add("dve", lambda e, tf=tf: e.reciprocal(out=tmp_f[tf], in_=tmp_f[tf]),
                                      r=[Btmp_f[tf]], w=[Btmp_f[tf]])
                                s2 = nxt("st2", NST)
                                if kind == "fq":
                                    P.add("dve", lambda e, s2=s2, ps=ps, tf=tf: e.scalar_tensor_tensor(
                                        out=st2_bf[s2], in0=ps[:, :], scalar=qkg[:, 2:3], in1=tmp_f[tf],
                                        op0=ALU.mult, op1=ALU.mult), r=[Bps[pb], Btmp_f[tf], Bpar], w=[Bst2_bf[s2]])
                                else:
                                    P.add("dve", lambda e, s2=s2, ps=ps, tf=tf: e.tensor_tensor(
                                        out=st2_bf[s2], in0=ps[:, :], in1=tmp_f[tf], op=ALU.mult),
                                        r=[Bps[pb], Btmp_f[tf]], w=[Bst2_bf[s2]])
                                dstT = qT_s if kind == "fq" else kT_s
                                Bd = B["qT_s"] if kind == "fq" else B["kT_s"]
                                P.add("sp", lambda e, s2=s2, dstT=dstT, ch=ch, gt0=gt0: e.dma_start(
                                    out=dstT[ch, :, gt0:gt0 + 512], in_=st2_bf[s2]),
                                    r=[Bst2_bf[s2]], w=[Bd], dma=True)
            P.barrier()


        def phase2(l):
            A.reset(KEEP2)
            QG = S // 512
            totb = A.alloc([128, NT, 12], F32)
            fcn = A.alloc([128, NT, 8], F32)
            Btot = P.buf("totb")
            qkg = A.alloc([128, 4], F32)
            bnd = A.alloc([128, 4], F32)
            Bpar = P.buf("par2")
            P.add("sp", lambda e: e.dma_start(out=totb, in_=tot_s.partition_broadcast(128)),
                  r=[B["tot_s"]], w=[Btot], dma=True)
            P.add("sp", lambda e: e.dma_start(out=qkg[:, 0:2], in_=qkg_d[l]), r=[B_w], w=[Bpar], dma=True)
            P.add("dve", lambda e: e.scalar_tensor_tensor(out=qkg[:, 2:3], in0=qkg[:, 0:1], scalar=float(128 ** -0.5),
                                                          in1=qkg[:, 1:2], op0=ALU.mult, op1=ALU.mult),
                  r=[Bpar], w=[Bpar])
            P.add("pe", lambda e: e.transpose(out=psum[6][0:1, 0:128], in_=qkg[:, 2:3], identity=cst[:, 0, :]),
                  r=[Bpar, B_cst], w=[Bps[6]])
            P.add("dve", lambda e: e.tensor_reduce(out=bnd[0:1, 0:1], in_=psum[6][0:1, 0:128], axis=AX.X, op=ALU.max,
                                                   apply_absolute_value=True), r=[Bps[6]], w=[Bpar])
            P.add("dve", lambda e: e.tensor_scalar(out=bnd[0:1, 1:2], in0=bnd[0:1, 0:1], scalar1=128.0, scalar2=None,
                                                   op0=ALU.mult), r=[Bpar], w=[Bpar])
            P.add("pe", lambda e: e.matmul(psum[6][:, 128:129], ones_f[0:1, :], bnd[0:1, 1:2], start=True, stop=True),
                  r=[Bpar, B_cst], w=[Bps[6]])
            P.add("dve", lambda e: e.tensor_copy(out=bnd[:, 2:3], in_=psum[6][:, 128:129]), r=[Bps[6]], w=[Bpar])
            P.add("pool", lambda e: e.memset(fcn[:, 0, :], 0.0), w=[Btot])
            for i in range(1, NT):
                P.add("dve", lambda e, i=i: e.tensor_tensor(out=fcn[:, i, :], in0=fcn[:, i - 1, :],
                                                            in1=totb[:, i - 1, 0:8], op=ALU.add), r=[Btot], w=[Btot])
            NS = 2
            kT = [A.alloc([128, S], BF16) for _ in range(NS)]
            qT = [A.alloc([128, S], BF16) for _ in range(NS)]
            vp = [A.alloc([128, NT, 129], BF16) for _ in range(NS)]
            Dt = [A.alloc([128, NT, NT], F32) for _ in range(NS)]
            Dhi = [A.alloc([128, NT, NT], BF16) for _ in range(NS)]
            Dlo = [A.alloc([128, NT, NT], F32) for _ in range(NS)]
            DX = [A.alloc([128, NT, NT], BF16) for _ in range(NS)]
            BkT = [P.buf("a_kT%d" % i) for i in range(NS)]
            BqT = [P.buf("a_qT%d" % i) for i in range(NS)]
            Bvp = [P.buf("a_vp%d" % i) for i in range(NS)]
            BDt = [P.buf("a_D%d" % i) for i in range(NS)]
            NPT = 3
            pt = [A.alloc([128, 512], BF16) for _ in range(NPT)]
            Bpt = [P.buf("a_pt%d" % i) for i in range(NPT)]
            ao = [A.alloc([128, 512], BF16) for _ in range(2)]
            Bao = [P.buf("a_ao%d" % i) for i in range(2)]
            aoT = [A.alloc([128, 512], BF16) for _ in range(2)]
            BaoT = [P.buf("a_aoT%d" % i) for i in range(2)]
            rc = A.alloc([128, 8], F32)
            Brc = P.buf("a_rc")
            ptc = 0
            stc = 0
            for h in range(NH):
                s = h % NS
                P.add("sp", lambda e, s=s, h=h: e.dma_start(out=kT[s], in_=kT_s[h]), r=[B["kT_s"]], w=[BkT[s]], dma=True)
                P.add("sp", lambda e, s=s, h=h: e.dma_start(out=qT[s], in_=qT_s[h]), r=[B["qT_s"]], w=[BqT[s]], dma=True)
                P.add("sp", lambda e, s=s, h=h: e.dma_start(out=vp[s], in_=vp_s[h]), r=[B["vp_s"]], w=[Bvp[s]], dma=True)
                P.add("dve", lambda e, s=s, h=h: e.scalar_tensor_tensor(
                    out=Dt[s], in0=fcn[:, :, h].unsqueeze(1).to_broadcast([128, NT, NT]), scalar=bnd[:, 2:3],
                    in1=fcn[:, :, h].unsqueeze(2).to_broadcast([128, NT, NT]), op0=ALU.subtract, op1=ALU.subtract),
                    r=[Btot, Bpar], w=[BDt[s]])
                P.add("dve", lambda e, s=s: e.tensor_copy(out=Dhi[s], in_=Dt[s]), r=[BDt[s]], w=[BDt[s]])
                P.add("dve", lambda e, s=s: e.tensor_tensor(out=Dlo[s], in0=Dt[s], in1=Dhi[s], op=ALU.subtract),
                      r=[BDt[s]], w=[BDt[s]])
                P.add("dve", lambda e, s=s: e.tensor_scalar(out=Dlo[s], in0=Dlo[s], scalar1=cst[:, 0, 1:2],
                                                            scalar2=None, op0=ALU.mult), r=[BDt[s], B_cst], w=[BDt[s]])
                P.add("dve", lambda e, s=s: e.scalar_tensor_tensor(
                    out=DX[s], in0=Dhi[s], scalar=cst[:, 0, 0:1], in1=Dlo[s], op0=ALU.mult, op1=ALU.add),
                    r=[BDt[s], B_cst], w=[BDt[s]])
                steps = [(g, j) for g in range(QG) for j in range(4 * g + 4)]

                def emit_qk(st):
                    g, j = steps[st]
                    i0 = 4 * g
                    ilo = max(i0, j)
                    ncol = (i0 + 4 - ilo) * 128
                    qlo = ilo * 128
                    sb = 4 + (st % 2)
                    P.add("pe", lambda e, s=s, j=j, sb=sb, qlo=qlo, ncol=ncol: e.matmul(
                        psum[sb][:, 0:ncol], kT[s][:, j * 128:(j + 1) * 128], qT[s][:, qlo:qlo + ncol],
                        start=True, stop=False), r=[BkT[s], BqT[s]], w=[Bps[sb]])
                    nseg = i0 + 4 - ilo
                    P.add("pe", lambda e, s=s, j=j, sb=sb, ilo=ilo, nseg=nseg, ncol=ncol: e.matmul(
                        psum[sb][:, 0:ncol], ones_b,
                        DX[s][:, ilo:ilo + nseg, j:j + 1].to_broadcast([128, nseg, 128]),
                        start=False, stop=True), r=[BDt[s], B_cst], w=[Bps[sb]])

                emit_qk(0)
                for st, (g, j) in enumerate(steps):
                    if True:
                        i0 = 4 * g
                        ilo = max(i0, j)
                        sb = 4 + (st % 2)
                        p = ptc % NPT
                        ptc += 1
                        if st + 1 < len(steps):
                            emit_qk(st + 1)
                        ncol = (i0 + 4 - ilo) * 128
                        P.add("act", lambda e, p=p, sb=sb, ncol=ncol: e.activation(
                            out=pt[p][:, 0:ncol], in_=psum[sb][:, 0:ncol], func=AF.Exp),
                            r=[Bps[sb]], w=[Bpt[p]])
                        for i in range(ilo, i0 + 4):
                            c0 = (i - ilo) * 128
                            if i == j:
                                P.add("dve", lambda e, p=p, c0=c0: e.tensor_tensor(
                                    out=pt[p][:, c0:c0 + 128], in0=pt[p][:, c0:c0 + 128], in1=tri_b, op=ALU.mult),
                                    r=[Bpt[p], B_cst], w=[Bpt[p]])
                        for i in range(ilo, i0 + 4):
                            c0 = (i - ilo) * 128
                            ab = i - i0
                            P.add("pe", lambda e, s=s, p=p, c0=c0, ab=ab, i=i, j=j: e.matmul(
                                psum[ab][:, 0:129], pt[p][:, c0:c0 + 128], vp[s][:, j, :],
                                start=(j == 0), stop=(j == i)), r=[Bpt[p], Bvp[s]], w=[Bps[ab]])
                    if j != 4 * g + 3:
                        continue
                    a = stc % 2
                    stc += 1
                    for m in range(4):
                        P.add("dve", lambda e, m=m: e.reciprocal(out=rc[:, m:m + 1], in_=psum[m][:, 128:129]),
                              r=[Bps[m]], w=[Brc])
                        P.add("dve", lambda e, m=m, a=a: e.tensor_scalar(
                            out=ao[a][:, m * 128:(m + 1) * 128], in0=psum[m][:, 0:128], scalar1=rc[:, m:m + 1],
                            scalar2=None, op0=ALU.mult), r=[Bps[m], Brc], w=[Bao[a]])
                    pv = ps_bf(6)
                    for m in range(4):
                        P.add("pe", lambda e, m=m, a=a, pv=pv: e.transpose(
                            out=pv[:, m * 128:(m + 1) * 128], in_=ao[a][:, m * 128:(m + 1) * 128], identity=ident_b),
                            r=[Bao[a], B_cst], w=[Bps[6]])
                    P.add("dve", lambda e, a=a, pv=pv: e.tensor_copy(out=aoT[a], in_=pv[:, 0:512]),
                          r=[Bps[6]], w=[BaoT[a]])
                    P.add("sp", lambda e, a=a, h=h, g=g: e.dma_start(out=aoT_s[h, :, g * 512:(g + 1) * 512],
                                                                     in_=aoT[a]),
                          r=[BaoT[a]], w=[B["aoT_s"]], dma=True)
            P.barrier()

        def phase3(l):
            A.reset(KEEP2)
            totb = A.alloc([128, NT, 12], F32)
            eL = A.alloc([128, NT, 4], F32)
            Btot = P.buf("m_tot")
            gbc = A.alloc([128, 1024], F32)
            Bg = P.buf("m_g")
            P.add("sp", lambda e: e.dma_start(out=totb, in_=tot_s.partition_broadcast(128)),
                  r=[B["tot_s"]], w=[Btot], dma=True)
            P.add("sp", lambda e: e.dma_start(out=gbc, in_=mng_d[l]), r=[B_w], w=[Bg], dma=True)
            P.add("act", lambda e: e.activation(out=eL, in_=totb[:, :, 8:12], func=AF.Exp, scale=-1.0),
                  r=[Btot], w=[Btot])
            C32 = A.alloc([128, MH, 257], F32)
            Cb = A.alloc([128, MH, 257], BF16)
            BC32, BCb = P.buf("m_C32"), P.buf("m_Cb")
            NS = 2
            qT4 = [A.alloc([128, MH, 128], BF16) for _ in range(NS)]
            kT4 = [A.alloc([128, MH, 128], BF16) for _ in range(NS)]
            k4 = [A.alloc([128, 512], BF16) for _ in range(NS)]
            v4 = [A.alloc([128, MH, 257], BF16) for _ in range(NS)]
            o4 = [A.alloc([128, 1024], BF16) for _ in range(NS)]
            Bin = [P.buf("m_in%d" % i) for i in range(NS)]
            wt = [A.alloc([128, MH, 128], BF16) for _ in range(2)]
            Bwt = [P.buf("m_wt%d" % i) for i in range(2)]
            tmpU = [A.alloc([128, 257], F32) for _ in range(2)]
            BtmpU = [P.buf("m_tmpU%d" % i) for i in range(2)]
            sm = A.alloc([128, 16], F32)
            Bsm = P.buf("m_sm")
            junk = A.alloc([128, 256], BF16)
            Bjunk = P.buf("m_junk")
            mo = [A.alloc([128, 1024], BF16) for _ in range(2)]
            Bmo = [P.buf("m_mo%d" % i) for i in range(2)]
            moT = [A.alloc([128, 8, 128], BF16) for _ in range(2)]
            BmoT = [P.buf("m_moT%d" % i) for i in range(2)]
            uc = 0
            BC32h = [P.buf("m_C32_%d" % h) for h in range(MH)]
            BCbh = [P.buf("m_Cb_%d" % h) for h in range(MH)]
            for h in range(MH):
                P.add("pool", lambda e, h=h: e.memset(C32[:, h, :], 0.0), w=[BC32h[h]])
                P.add("pool", lambda e, h=h: e.memset(Cb[:, h, :], 0.0), w=[BCbh[h]])
            dn = A.alloc([128, 32], F32)
            Bdn, Bss = P.buf("m_dn"), P.buf("m_ss")
            t2 = [A.alloc([128, 256], F32) for _ in range(4)]
            Bt2 = [P.buf("m_t2%d" % i) for i in range(4)]
            pending = []
            for c in range(NT):
                s = c % NS
                tok = slice(c * 128, (c + 1) * 128)
                P.add("sp", lambda e, s=s, tok=tok: e.dma_start(
                    out=qT4[s], in_=mqT_s[:, :, tok].rearrange("h p t -> p h t")), r=[B["mqT_s"]], w=[Bin[s]], dma=True)
                P.add("sp", lambda e, s=s, tok=tok: e.dma_start(
                    out=kT4[s], in_=mkT_s[:, :, tok].rearrange("h p t -> p h t")), r=[B["mkT_s"]], w=[Bin[s]], dma=True)
                P.add("sp", lambda e, s=s, c=c: e.dma_start(out=k4[s], in_=mk_s[c]), r=[B["mk_s"]], w=[Bin[s]], dma=True)
                P.add("sp", lambda e, s=s, c=c: e.dma_start(out=v4[s], in_=mv_s[c]), r=[B["mv_s"]], w=[Bin[s]], dma=True)
                P.add("sp", lambda e, s=s, c=c: e.dma_start(out=o4[s], in_=mo_s[c]), r=[B["mo_s"]], w=[Bin[s]], dma=True)
                w_ = c % 2
                for h in range(MH):
                    P.add("pe", lambda e, s=s, h=h: e.matmul(psum[4][:, h * 128:(h + 1) * 128], kT4[s][:, h, :],
                                                             qT4[s][:, h, :], start=True, stop=True),
                          r=[Bin[s]], w=[Bps[4]])
                P.add("dve", lambda e, w_=w_: e.tensor_tensor(
                    out=wt[w_], in0=psum[4][:, :].rearrange("p (h t) -> p h t", h=MH),
                    in1=tri_b.unsqueeze(1).to_broadcast([128, MH, 128]), op=ALU.mult),
                    r=[Bps[4], B_cst], w=[Bwt[w_]])
                for h in range(MH):
                    hb = h // 2
                    hc = (h % 2) * 256
                    P.add("pe", lambda e, s=s, h=h, w_=w_, hb=hb, hc=hc: e.matmul(
                        psum[hb][:, hc:hc + 256], wt[w_][:, h, :], v4[s][:, h, 0:256], start=True, stop=False),
                        r=[Bwt[w_], Bin[s]], w=[Bps[hb]])
                    P.add("pe", lambda e, s=s, h=h, hb=hb, hc=hc: e.matmul(
                        psum[hb][:, hc:hc + 256], qT4[s][:, h, :], Cb[:, h, 0:256], start=False, stop=True),
                        r=[Bin[s], BCbh[h]], w=[Bps[hb]])
                for h in range(MH):
                    P.add("pe", lambda e, s=s, h=h, w_=w_: e.matmul(
                        psum[2][:, h:h + 1], wt[w_][:, h, :], v4[s][:, h, 256:257], start=True, stop=False),
                        r=[Bwt[w_], Bin[s]], w=[Bps[2]])
                    P.add("pe", lambda e, s=s, h=h: e.matmul(
                        psum[2][:, h:h + 1], qT4[s][:, h, :], Cb[:, h, 256:257], start=False, stop=True),
                        r=[Bin[s], BCbh[h]], w=[Bps[2]])
                for h in range(MH):
                    ub = 5 + (uc % 2)
                    u = uc % 2
                    uc += 1
                    P.add("pe", lambda e, s=s, h=h, ub=ub: e.matmul(psum[ub][:, 0:257], k4[s][:, h * 128:(h + 1) * 128],
                                                                    v4[s][:, h, :], start=True, stop=True),
                          r=[Bin[s]], w=[Bps[ub]])
                    P.add("dve", lambda e, u=u, ub=ub, h=h: e.tensor_tensor(
                        out=tmpU[u], in0=psum[ub][:, 0:257], in1=C32[:, h, :], op=ALU.add),
                        r=[Bps[ub], BC32h[h]], w=[BtmpU[u]])
                    P.add("act", lambda e, u=u, c=c, h=h: e.activation(
                        out=Cb[:, h, :], in_=tmpU[u], func=AF.Copy, scale=eL[:, c, h:h + 1]),
                        r=[BtmpU[u], Btot], w=[BCbh[h]])
                    P.add("dve", lambda e, u=u, c=c, h=h: e.tensor_scalar(
                        out=C32[:, h, :], in0=tmpU[u], scalar1=eL[:, c, h:h + 1], scalar2=None, op0=ALU.mult),
                        r=[BtmpU[u], Btot], w=[BC32h[h]])
                for fn in pending:
                    fn()
                pending = []
                P.add("dve", lambda e: e.tensor_reduce(out=dn[:, 0:4], in_=psum[2][:, 0:4].unsqueeze(2), axis=AX.X,
                                                       op=ALU.max, apply_absolute_value=True), r=[Bps[2]], w=[Bdn])
                P.add("dve", lambda e: e.tensor_scalar(out=dn[:, 4:8], in0=dn[:, 0:4], scalar1=1.0, scalar2=None,
                                                       op0=ALU.max), r=[Bdn], w=[Bdn])
                P.add("dve", lambda e: e.reciprocal(out=dn[:, 8:12], in_=dn[:, 4:8]), r=[Bdn], w=[Bdn])
                for h in range(MH):
                    hb = h // 2
                    hc = (h % 2) * 256
                    P.add("act", lambda e, h=h, hb=hb, hc=hc: e.activation(
                        out=junk, in_=psum[hb][:, hc:hc + 256], func=AF.Square, scale=dn[:, 8 + h:9 + h],
                        accum_out=dn[:, 12 + h:13 + h]), r=[Bps[hb], Bdn], w=[Bjunk, Bss])
                P.add("act", lambda e: e.activation(out=dn[:, 16:20], in_=dn[:, 12:16], func=AF.Sqrt, scale=1.0 / 256,
                                                    bias=eps_t[:, 0:1]), r=[Bss], w=[Bss])
                P.add("dve", lambda e: e.reciprocal(out=dn[:, 20:24], in_=dn[:, 16:20]), r=[Bss], w=[Bss])
                P.add("dve", lambda e: e.tensor_tensor(out=dn[:, 24:28], in0=dn[:, 20:24], in1=dn[:, 8:12], op=ALU.mult),
                      r=[Bss, Bdn], w=[Bss])
                for h in range(MH):
                    hb = h // 2
                    hc = (h % 2) * 256
                    P.add("dve", lambda e, h=h, hb=hb, hc=hc: e.scalar_tensor_tensor(
                        out=t2[h], in0=psum[hb][:, hc:hc + 256], scalar=dn[:, 24 + h:25 + h],
                        in1=gbc[:, h * 256:(h + 1) * 256], op0=ALU.mult, op1=ALU.mult),
                        r=[Bps[hb], Bss, Bg], w=[Bt2[h]])
                    P.add("pool", lambda e, h=h, w_=w_, s=s: e.tensor_tensor(
                        out=mo[w_][:, h * 256:(h + 1) * 256], in0=t2[h], in1=o4[s][:, h * 256:(h + 1) * 256],
                        op=ALU.mult), r=[Bt2[h], Bin[s]], w=[Bmo[w_]])

                def do_transposes(w_=w_, tok=tok):
                    pv = ps_bf(7)
                    for q in range(8):
                        P.add("pe", lambda e, q=q, pv=pv: e.transpose(
                            out=pv[:, q * 128:(q + 1) * 128], in_=mo[w_][:, q * 128:(q + 1) * 128], identity=ident_b),
                            r=[Bmo[w_], B_cst], w=[Bps[7]])
                    P.add("act", lambda e, pv=pv: e.copy(out=moT[w_], in_=pv[:, 0:1024].rearrange(
                        "p (q t) -> p q t", q=8)), r=[Bps[7]], w=[BmoT[w_]])
                    P.add("sp", lambda e: e.dma_start(
                        out=moT_s[:, :, tok].rearrange("q p t -> p q t"), in_=moT[w_]),
                        r=[BmoT[w_]], w=[B["moT_s"]], dma=True)
                pending.append(do_transposes)
            for fn in pending:
                fn()
            P.barrier()


        def phase4(l, j, src_ap, Bsrc):
            A.reset(KEEP2)
            r0 = j * T
            mT = A.alloc([128, KC, T], BF16)
            BmT = P.buf("mT")
            mark = A.off
            TH = T // 2
            aoT = A.alloc([128, NH, TH], BF16)
            moT = A.alloc([128, NH, TH], BF16)
            Bao, Bmo = P.buf("p4_ao"), P.buf("p4_mo")
            ws = WStream(8, 512, nslots=4, name="p4w")
            gat = [A.alloc([128, TH], BF16) for _ in range(2)]
            gmt = [A.alloc([128, TH], BF16) for _ in range(2)]
            Bgat = [P.buf("p4_ga%d" % i) for i in range(2)]
            Bgmt = [P.buf("p4_gm%d" % i) for i in range(2)]
            t1 = [A.alloc([128, 512], F32) for _ in range(2)]
            t2 = [A.alloc([128, 512], F32) for _ in range(2)]
            Bt1 = [P.buf("p4_t1%d" % i) for i in range(2)]
            Bt2 = [P.buf("p4_t2%d" % i) for i in range(2)]
            seq = [(th, n) for th in range(2) for n in range(4)]

            def mk(idx):
                th, n = seq[idx]
                ta = ws.tile(src_fn=lambda k0, k1, n=n: w_pa[l, k0 * 128:k1 * 128, n * 512:(n + 1) * 512].rearrange(
                    "(c p) n -> p c n", p=128))
                tm = ws.tile(src_fn=lambda k0, k1, n=n: w_pm[l, k0 * 128:k1 * 128, n * 512:(n + 1) * 512].rearrange(
                    "(c p) n -> p c n", p=128))
                return ta, tm

            def load_gates(th, dc):
                gs = dc % 2
                c0 = r0 + th * TH
                P.add("sp", lambda e, gs=gs, dc=dc, c0=c0: e.dma_start(out=gat[gs], in_=gaT_s[dc, :, c0:c0 + TH]),
                      r=[B["gaT_s"]], w=[Bgat[gs]], dma=True)
                P.add("sp", lambda e, gs=gs, dc=dc, c0=c0: e.dma_start(out=gmt[gs], in_=gmT_s[dc, :, c0:c0 + TH]),
                      r=[B["gmT_s"]], w=[Bgmt[gs]], dma=True)

            cur = mk(0)
            cur[0].finish()
            cur[1].finish()
            load_gates(0, 0)
            k = 0
            for idx, (th, n) in enumerate(seq):
                if n == 0:
                    c0 = r0 + th * TH
                    P.add("sp", lambda e, c0=c0: e.dma_start(
                        out=aoT, in_=aoT_s[:, :, c0:c0 + TH].rearrange("h p t -> p h t")),
                        r=[B["aoT_s"]], w=[Bao], dma=True)
                    P.add("sp", lambda e, c0=c0: e.dma_start(
                        out=moT, in_=moT_s[:, :, c0:c0 + TH].rearrange("h p t -> p h t")),
                        r=[B["moT_s"]], w=[Bmo], dma=True)
                nxt_t = mk(idx + 1) if idx + 1 < len(seq) else None
                wa, Bwa = cur[0].wb, cur[0].Bwb
                wm, Bwm = cur[1].wb, cur[1].Bwb
                un = 0
                for q in range(4):
                    dc = n * 4 + q
                    gs = dc % 2
                    if q < 3:
                        load_gates(th, dc + 1)
                    elif idx + 1 < len(seq):
                        load_gates(seq[idx + 1][0], seq[idx + 1][1] * 4)
                    for tg in range(TH // 512):
                        if nxt_t is not None and un % 2 == 0:
                            nxt_t[(un // 2) % 2].step()
                        un += 1
                        tks = slice(tg * 512, (tg + 1) * 512)
                        otk = slice(th * TH + tg * 512, th * TH + (tg + 1) * 512)
                        pa = (k % 3)
                        pm = 3 + (k % 3)
                        u = k % 2
                        k += 1
                        mm_acc(psum[pa][:, :], Bps[pa], [(wa[:, c, q * 128:(q + 1) * 128], aoT[:, c, tks])
                                                         for c in range(8)], [Bwa, Bao])
                        mm_acc(psum[pm][:, :], Bps[pm], [(wm[:, c, q * 128:(q + 1) * 128], moT[:, c, tks])
                                                         for c in range(8)], [Bwm, Bmo])
                        P.add("dve", lambda e, pa=pa, u=u, gs=gs, tks=tks: e.tensor_tensor(
                            out=t1[u], in0=psum[pa][:, :], in1=gat[gs][:, tks], op=ALU.mult),
                            r=[Bps[pa], Bgat[gs]], w=[Bt1[u]])
                        P.add("dve", lambda e, pm=pm, u=u, gs=gs, tks=tks: e.tensor_tensor(
                            out=t2[u], in0=psum[pm][:, :], in1=gmt[gs][:, tks], op=ALU.mult),
                            r=[Bps[pm], Bgmt[gs]], w=[Bt2[u]])
                        P.add("dve", lambda e, u=u, dc=dc, otk=otk: e.tensor_tensor(
                            out=mT[:, dc, otk], in0=t1[u], in1=t2[u], op=ALU.add),
                            r=[Bt1[u], Bt2[u]], w=[BmT])
                if nxt_t is not None:
                    nxt_t[0].finish()
                    nxt_t[1].finish()
                cur = nxt_t
            P.barrier()
            A.reset(mark)
            ws = WStream(KC, 512, name="p4o")
            xp = [A.alloc([128, 512], F32) for _ in range(3)]
            Bxp = [P.buf("p4_xp%d" % i) for i in range(3)]
            xo = [A.alloc([128, 512], F32) for _ in range(3)]
            Bxo = [P.buf("p4_xo%d" % i) for i in range(3)]
            osrcs = [dict(src_fn=lambda k0, k1, n=n: w_out[l, k0 * 128:k1 * 128, n * 512:(n + 1) * 512].rearrange(
                "(c p) n -> p c n", p=128)) for n in range(4)]

            def ld_x(kk):
                n, i = divmod(kk, HT)
                u = kk % 3
                rows = slice(r0 + i * 128, r0 + (i + 1) * 128)
                cols = slice(n * 512, (n + 1) * 512)
                P.add("sp", lambda e, u=u, rows=rows, cols=cols: e.dma_start(out=xp[u], in_=src_ap[rows, cols]),
                      r=[Bsrc], w=[Bxp[u]], dma=True)

            ld_x(0)
            k = 0
            for n, wb, Bwb, tick in pipelined(ws, osrcs, HT):
                cols = slice(n * 512, (n + 1) * 512)
                for i in range(HT):
                    tick(i)
                    rows = slice(r0 + i * 128, r0 + (i + 1) * 128)
                    tok = slice(i * 128, (i + 1) * 128)
                    pb = k % 4
                    u = k % 3
                    if k + 1 < 4 * HT:
                        ld_x(k + 1)
                    k += 1
                    mm_acc(psum[pb][:, :], Bps[pb], [(mT[:, c, tok], wb[:, c, :]) for c in range(KC)], [BmT, Bwb])
                    P.add("dve", lambda e, u=u, pb=pb: e.tensor_tensor(out=xo[u], in0=psum[pb][:, :], in1=xp[u],
                                                                        op=ALU.add), r=[Bps[pb], Bxp[u]], w=[Bxo[u]])
                    P.add("sp", lambda e, u=u, rows=rows, cols=cols: e.dma_start(out=x_mid[rows, cols], in_=xo[u]),
                          r=[Bxo[u]], w=[B["x_mid"]], dma=True)
            P.barrier()

        def phase5(l, j, dst_ap, Bdst):
            A.reset(KEEP2)
            r0 = j * T
            h2T = A.alloc([128, KC, T], BF16)
            Bh2T = [P.buf("h2T%d" % i) for i in range(HT)]
            mark = A.off
            cw = A.alloc([128, 2 * FC, 3], F32)
            cb = A.alloc([128, 2 * FC], F32)
            Bcw = P.buf("p5_cw")
            P.add("sp", lambda e: e.dma_start(out=cw, in_=cw_d[l]), r=[B_w], w=[Bcw], dma=True)
            P.add("sp", lambda e: e.dma_start(out=cb, in_=cb_d[l]), r=[B_w], w=[Bcw], dma=True)
            if j == 0:
                P.add("pool", lambda e: e.memset(hist, 0.0), w=[Bhist])
            ws = WStream(KC, 256, name="p5u")
            av = [[A.alloc([128, 512], F32) for _ in range(3)] for _ in range(2)]
            Bav = [[P.buf("p5_a%d%d" % (x, i)) for i in range(3)] for x in range(2)]
            sg = [A.alloc([128, 512], F32) for _ in range(3)]
            Bsg = [P.buf("p5_sg%d" % i) for i in range(3)]
            ast = [A.alloc([128, 512], BF16) for _ in range(3)]
            Bast = [P.buf("p5_ast%d" % i) for i in range(3)]
            k = 0
            usrcs = [dict(srcs=[
                (0, 128, lambda k0, k1, fc=fc: w_up[l, k0 * 128:k1 * 128, fc * 128:(fc + 1) * 128].rearrange(
                    "(c p) n -> p c n", p=128)),
                (128, 256, lambda k0, k1, fc=fc: w_up[l, k0 * 128:k1 * 128, DFF + fc * 128:DFF + (fc + 1) * 128].rearrange(
                    "(c p) n -> p c n", p=128))]) for fc in range(FC)]
            first_tile = ws.tile(**usrcs[0])
            first_tile.finish()
            norm_it = phase_norm(x_mid, B["x_mid"], r0, h2T, Bh2T, g2_d[l:l + 1, :])
            for fc, wb, Bwb, tick in pipelined(ws, usrcs, TG, first=first_tile):
                for tg in range(TG):
                    if fc == 0:
                        for _ in range(4):
                            next(norm_it, None)
                    tick(tg)
                    tks = slice(tg * 512, (tg + 1) * 512)
                    u = k % 3
                    k3 = k
                    k += 1
                    first = (j == 0 and tg == 0)
                    for x in range(2):
                        pb = 3 * x + (k3 % 3)
                        fcx = x * FC + fc
                        a = av[x][u]
                        Ba = Bav[x][u]
                        mm_acc(psum[pb][:, :], Bps[pb], [(wb[:, c, x * 128:(x + 1) * 128], h2T[:, c, tks])
                                                         for c in range(KC)], [Bwb] + Bh2T[4 * tg:4 * tg + 4])
                        P.add("act", lambda e, a=a, pb=pb, fcx=fcx: e.activation(
                            out=a, in_=psum[pb][:, :], func=AF.Identity, scale=cw[:, fcx, 2:3],
                            bias=cb[:, fcx:fcx + 1]), r=[Bps[pb], Bcw], w=[Ba])
                        P.add("dve", lambda e, a=a, pb=pb, fcx=fcx: e.scalar_tensor_tensor(
                            out=a[:, 1:512], in0=psum[pb][:, 0:511], scalar=cw[:, fcx, 1:2], in1=a[:, 1:512],
                            op0=ALU.mult, op1=ALU.add), r=[Bps[pb], Bcw, Ba], w=[Ba])
                        P.add("dve", lambda e, a=a, pb=pb, fcx=fcx: e.scalar_tensor_tensor(
                            out=a[:, 2:512], in0=psum[pb][:, 0:510], scalar=cw[:, fcx, 0:1], in1=a[:, 2:512],
                            op0=ALU.mult, op1=ALU.add), r=[Bps[pb], Bcw, Ba], w=[Ba])
                        if not first:
                            P.add("dve", lambda e, a=a, fcx=fcx: e.scalar_tensor_tensor(
                                out=a[:, 0:1], in0=hist[:, fcx, 1:2], scalar=cw[:, fcx, 1:2], in1=a[:, 0:1],
                                op0=ALU.mult, op1=ALU.add), r=[Bhist, Bcw, Ba], w=[Ba])
                            P.add("dve", lambda e, a=a, fcx=fcx: e.scalar_tensor_tensor(
                                out=a[:, 0:2], in0=hist[:, fcx, 0:2], scalar=cw[:, fcx, 0:1], in1=a[:, 0:2],
                                op0=ALU.mult, op1=ALU.add), r=[Bhist, Bcw, Ba], w=[Ba])
                        P.add("dve", lambda e, pb=pb, fcx=fcx: e.tensor_copy(out=hist[:, fcx, 0:2],
                                                                              in_=psum[pb][:, 510:512]),
                              r=[Bps[pb], Bhist], w=[Bhist])
                    P.add("act", lambda e, u=u: e.activation(out=sg[u], in_=av[0][u], func=AF.Silu),
                          r=[Bav[0][u]], w=[Bsg[u]])
                    o = k % 3
                    P.add("pool", lambda e, u=u, o=o: e.tensor_tensor(out=ast[o], in0=sg[u], in1=av[1][u], op=ALU.mult),
                          r=[Bsg[u], Bav[1][u]], w=[Bast[o]])
                    P.add("sp", lambda e, o=o, fc=fc, tg=tg: e.dma_start(
                        out=actT_s[fc, :, r0 + tg * 512:r0 + (tg + 1) * 512], in_=ast[o]),
                        r=[Bast[o]], w=[B["actT_s"]], dma=True)
            P.barrier()
            A.reset(KEEP2)
            ws = WStream(FC, 512, name="p5d")
            at = [A.alloc([128, FC, 128], BF16) for _ in range(2)]
            Bat = [P.buf("p5_at%d" % i) for i in range(2)]
            xp = [A.alloc([128, 512], F32) for _ in range(3)]
            Bxp = [P.buf("p5_xp%d" % i) for i in range(3)]
            xo = [A.alloc([128, 512], F32) for _ in range(3)]
            Bxo = [P.buf("p5_xo%d" % i) for i in range(3)]
            dsrcs = [dict(src_fn=lambda k0, k1, n=n: w_down[l, k0 * 128:k1 * 128, n * 512:(n + 1) * 512].rearrange(
                "(c p) n -> p c n", p=128)) for n in range(4)]

            def ld_in(kk):
                n, i = divmod(kk, HT)
                u = kk % 3
                a = kk % 2
                rows = slice(r0 + i * 128, r0 + (i + 1) * 128)
                cols = slice(n * 512, (n + 1) * 512)
                P.add("sp", lambda e, a=a, rows=rows: e.dma_start(
                    out=at[a], in_=actT_s[:, :, rows].rearrange("f p t -> p f t")),
                    r=[B["actT_s"]], w=[Bat[a]], dma=True)
                P.add("sp", lambda e, u=u, rows=rows, cols=cols: e.dma_start(out=xp[u], in_=x_mid[rows, cols]),
                      r=[B["x_mid"]], w=[Bxp[u]], dma=True)

            ld_in(0)
            k = 0
            for n, wb, Bwb, tick in pipelined(ws, dsrcs, HT):
                cols = slice(n * 512, (n + 1) * 512)
                for i in range(HT):
                    tick(i)
                    rows = slice(r0 + i * 128, r0 + (i + 1) * 128)
                    pb = k % 4
                    u = k % 3
                    a = k % 2
                    if k + 1 < 4 * HT:
                        ld_in(k + 1)
                    k += 1
                    mm_acc(psum[pb][:, :], Bps[pb], [(at[a][:, f, :], wb[:, f, :]) for f in range(FC)], [Bat[a], Bwb])
                    P.add("dve", lambda e, u=u, pb=pb: e.tensor_tensor(out=xo[u], in0=psum[pb][:, :], in1=xp[u],
                                                                        op=ALU.add), r=[Bps[pb], Bxp[u]], w=[Bxo[u]])
                    P.add("sp", lambda e, u=u, rows=rows, cols=cols: e.dma_start(out=dst_ap[rows, cols], in_=xo[u]),
                          r=[Bxo[u]], w=[Bdst], dma=True)
            P.barrier()

        eps_t = A.alloc([128, 1], F32)
        one_t = A.alloc([128, 1], F32)
        lnsc_t = A.alloc([128, 1], F32)
        P.add("pool", lambda e: e.memset(eps_t, EPS), w=[B_cst])
        P.add("pool", lambda e: e.memset(one_t, 1.0), w=[B_cst])
        P.add("pool", lambda e: e.memset(lnsc_t, math.log(128 ** -0.5)), w=[B_cst])
        hist = A.alloc([128, 2 * FC, 2], F32)
        Bhist = P.buf("hist")
        KEEP2 = A.off
        P.barrier()

        for l in range(n_layers):
            src_ap, Bsrc = (x_in, B_x_in) if l == 0 else (x_l1, B["x_l1"])
            for j in range(2):
                phase1(l, j, src_ap, Bsrc)
            if stop_after == "p1":
                break
            phase2(l)
            if stop_after == "p2":
                break
            phase3(l)
            if stop_after == "p3":
                break
            for j in range(2):
                phase4(l, j, src_ap, Bsrc)
            if stop_after == "p4":
                break
            last = (l == n_layers - 1)
            for j in range(2):
                phase5(l, j, y_out if last else x_l1, B["y"] if last else B["x_l1"])

        P.barrier()
        P.emit()
    return nc


def _consts():
    c = np.zeros((128, 3, 128), np.float32)
    c[:, 0, :] = np.eye(128, dtype=np.float32)
    c[:, 1, :] = np.triu(np.ones((128, 128), np.float32))
    c[:, 2, :] = 1.0
    return c


def prep_shared(inp):
    f = lambda a: np.ascontiguousarray(np.asarray(a, dtype=np.float32))
    sh = {}
    sh["w_in"] = f(inp["w_in"])
    sh["w_pa"] = f(inp["w_proj_a"])
    sh["w_pm"] = f(inp["w_proj_m"])
    sh["w_out"] = f(inp["w_out"])
    sh["w_up"] = f(inp["w_up"])
    sh["w_down"] = f(inp["w_down"])
    sh["g1"] = f(inp["norm1_g"])
    sh["g2"] = f(inp["norm2_g"])
    gb = np.concatenate([np.asarray(inp["fox_f_bias"]), np.asarray(inp["m_f_bias"]), np.asarray(inp["m_i_bias"])], axis=1)
    sh["gbias"] = f(np.broadcast_to(gb[:, None, :], (NL, 128, 16)))
    sh["qkg"] = f(np.stack([np.asarray(inp["q_norm_g"]), np.asarray(inp["k_norm_g"])], axis=2))
    sh["mng"] = f(np.broadcast_to(np.asarray(inp["m_norm_g"]).reshape(NL, 1, 1024), (NL, 128, 1024)))
    sh["cw"] = f(np.asarray(inp["conv_w"]).reshape(NL, 3, 2 * FC, 128).transpose(0, 3, 2, 1))
    sh["cb"] = f(np.asarray(inp["conv_b"]).reshape(NL, 2 * FC, 128).transpose(0, 2, 1))
    sh["consts"] = _consts()
    return sh


def kernel(**inputs):
    x = np.asarray(inputs["x"], dtype=np.float32)
    Bn, S, _ = x.shape
    sh = prep_shared(inputs)
    nc = build(S=S)
    in_maps = []
    for b in range(Bn):
        m = dict(sh)
        m["x"] = np.ascontiguousarray(x[b])
        in_maps.append(m)
    res = run_bass_kernel_spmd(nc, in_maps, core_ids=list(range(Bn)))
    return np.stack([np.asarray(r["y"]) for r in res.results], axis=0).astype(np.float32)
```

```python
import math
from contextlib import ExitStack

import numpy as np
import concourse.bass as bass
import concourse.mybir as mybir
from concourse.bass_utils import run_bass_kernel_spmd

F32 = mybir.dt.float32
BF16 = mybir.dt.bfloat16
U8 = mybir.dt.uint8
AF = mybir.ActivationFunctionType
ALU = mybir.AluOpType
AX = mybir.AxisListType

D = 2048
DIN = 10256
DFF = 5632
NL = 2
NH = 8
MH = 4
EPS = 1e-6
KC = D // 128
FC = DFF // 128
ENG = ("pe", "act", "dve", "pool", "sp")


class Buf:
    def __init__(self, name):
        self.name = name
        self.writers = []
        self.readers = []
        self.sem = None
        self.nd = 0


class Op:
    pass


class Prog:
    def __init__(self, nc, strict=("act", "dve", "pool")):
        self.nc = nc
        self.ops = {e: [] for e in ENG}
        self.bufs = []
        self.strict = strict

    def buf(self, name):
        for b in self.bufs:
            if b.name == name:
                return b
        b = Buf(name)
        self.bufs.append(b)
        return b

    def add(self, eng, fn, r=(), w=(), dma=False):
        op = Op()
        op.eng, op.fn, op.dma, op.signal = eng, fn, dma, False
        op.deps_eng, op.deps_sem = {}, {}

        def dep(d):
            if d.dma:
                k = id(d.dst)
                v = op.deps_sem.get(k)
                if v is None or v[1] < d.count:
                    op.deps_sem[k] = (d.dst, d.count)
            else:
                if d.eng == eng and eng not in self.strict:
                    return
                if op.deps_eng.get(d.eng, -1) < d.seq:
                    op.deps_eng[d.eng] = d.seq

        for b in r:
            for d in b.writers:
                dep(d)
        for b in w:
            for d in b.readers:
                dep(d)
        for b in r:
            b.readers.append(op)
        for b in w:
            if b.readers:
                b.writers = [op]
                b.readers = []
            else:
                b.writers.append(op)
        op.seq = len(self.ops[eng])
        self.ops[eng].append(op)
        if dma:
            op.dst = w[0]
            op.dst.nd += 1
            op.count = 16 * op.dst.nd
        return op

    def barrier(self):
        lasts = {}
        for e in ENG:
            if e == "sp":
                continue
            for op in reversed(self.ops[e]):
                if op.fn is not None:
                    lasts[e] = op.seq
                    break
        sems = {id(b): (b, 16 * b.nd) for b in self.bufs if b.nd > 0}
        for e in ENG:
            op = Op()
            op.eng, op.fn, op.dma, op.signal = e, None, False, False
            op.deps_eng = {e2: s for e2, s in lasts.items() if e2 != e}
            op.deps_sem = dict(sems)
            op.seq = len(self.ops[e])
            self.ops[e].append(op)
        for b in self.bufs:
            b.writers = []
            b.readers = []

    def emit(self):
        nc = self.nc
        for e in ENG:
            for op in self.ops[e]:
                for pe_, s in op.deps_eng.items():
                    self.ops[pe_][s].signal = True
        for e in ENG:
            c = 0
            for op in self.ops[e]:
                if op.signal:
                    c += 1
                op.sig = c
        with ExitStack() as es:
            esem = {e: es.enter_context(nc.semaphore("S_" + e)) for e in ENG}
            for b in self.bufs:
                if b.nd > 0:
                    b.sem = es.enter_context(nc.semaphore("D_" + b.name))
            block = es.enter_context(nc.Block())

            def run(e, eng):
                waited = {}
                for op in self.ops[e]:
                    for pe_, s in op.deps_eng.items():
                        v = self.ops[pe_][s].sig
                        key = ("e", pe_)
                        if waited.get(key, 0) < v:
                            eng.wait_ge(esem[pe_], v)
                            waited[key] = v
                    for (dst, cnt) in op.deps_sem.values():
                        key = ("d", id(dst))
                        if waited.get(key, 0) < cnt:
                            eng.wait_ge(dst.sem, cnt)
                            waited[key] = cnt
                    if op.fn is not None:
                        inst = op.fn(eng)
                        if op.dma:
                            inst.then_inc(op.dst.sem, 16)
                        elif op.signal:
                            inst.then_inc(esem[e], 1)

            @block.tensor
            def _(eng):
                run("pe", eng)

            @block.scalar
            def _(eng):
                run("act", eng)

            @block.vector
            def _(eng):
                run("dve", eng)

            @block.gpsimd
            def _(eng):
                run("pool", eng)

            @block.sync
            def _(eng):
                run("sp", eng)


class Arena:
    def __init__(self, handle, size):
        self.h, self.size, self.off = handle, size, 0

    def reset(self, keep=0):
        self.off = keep

    def alloc(self, shape, dtype):
        esz = 2 if dtype == BF16 else 4
        n = 1
        for s in shape[1:]:
            n *= s
        nbytes = (n * esz + 63) // 64 * 64
        assert self.off + nbytes <= self.size, ("SBUF arena overflow", self.off, nbytes)
        v = self.h[0:shape[0], self.off:self.off + n * esz].bitcast(dtype)
        self.off += nbytes
        if len(shape) == 3:
            v = v.rearrange("p (a b) -> p a b", a=shape[1])
        elif len(shape) == 4:
            v = v.rearrange("p (a b c) -> p a b c", a=shape[1], b=shape[2])
        return v


def build(S=4096, n_layers=NL, debug=(), stop_after=None):
    nc = bass.Bass("TRN2", target_bir_lowering=False)
    P = Prog(nc)
    NT = S // 128
    T = S // 2
    HT = T // 128
    TG = T // 512

    def dram(name, shape, dt, kind="Internal"):
        if name in debug:
            kind = "ExternalOutput"
        return nc.dram_tensor(name, list(shape), dt, kind=kind).ap()

    x_in = dram("x", [S, D], F32, "ExternalInput")
    w_in = dram("w_in", [NL, D, DIN], F32, "ExternalInput")
    w_pa = dram("w_pa", [NL, 1024, D], F32, "ExternalInput")
    w_pm = dram("w_pm", [NL, 1024, D], F32, "ExternalInput")
    w_out = dram("w_out", [NL, D, D], F32, "ExternalInput")
    w_up = dram("w_up", [NL, D, 2 * DFF], F32, "ExternalInput")
    w_down = dram("w_down", [NL, DFF, D], F32, "ExternalInput")
    g1_d = dram("g1", [NL, D], F32, "ExternalInput")
    g2_d = dram("g2", [NL, D], F32, "ExternalInput")
    gbias_d = dram("gbias", [NL, 128, 16], F32, "ExternalInput")
    qkg_d = dram("qkg", [NL, 128, 2], F32, "ExternalInput")
    mng_d = dram("mng", [NL, 128, 1024], F32, "ExternalInput")
    cw_d = dram("cw", [NL, 128, 2 * FC, 3], F32, "ExternalInput")
    cb_d = dram("cb", [NL, 128, 2 * FC], F32, "ExternalInput")
    consts_d = dram("consts", [128, 3, 128], F32, "ExternalInput")
    y_out = dram("y", [S, D], F32, "ExternalOutput")

    qT_s = dram("qT_s", [NH, 128, S], BF16)
    kT_s = dram("kT_s", [NH, 128, S], BF16)
    vp_s = dram("vp_s", [NH, 128, NT, 129], BF16)
    tot_s = dram("tot_s", [NT, 12], F32)
    mqT_s = dram("mqT_s", [MH, 128, S], BF16)
    mkT_s = dram("mkT_s", [MH, 128, S], BF16)
    mk_s = dram("mk_s", [NT, 128, 512], BF16)
    mv_s = dram("mv_s", [NT, 128, MH, 257], BF16)
    mo_s = dram("mo_s", [NT, 128, 1024], BF16)
    gaT_s = dram("gaT_s", [KC, 128, S], BF16)
    gmT_s = dram("gmT_s", [KC, 128, S], BF16)
    aoT_s = dram("aoT_s", [NH, 128, S], BF16)
    moT_s = dram("moT_s", [NH, 128, S], BF16)
    x_mid = dram("x_mid", [S, D], F32)
    x_l1 = dram("x_l1", [S, D], F32)
    actT_s = dram("actT_s", [S // 128, 128, FC, 128], BF16)

    B_x_in = P.buf("x_in")
    B = {n: P.buf(n) for n in ("qT_s", "kT_s", "vp_s", "tot_s", "mqT_s", "mkT_s", "mk_s", "mv_s",
                               "mo_s", "gaT_s", "gmT_s", "aoT_s", "moT_s", "x_mid", "x_l1", "actT_s", "y")}
    B_w = P.buf("weights")

    with ExitStack() as es:
        ARENA_BYTES = 204 * 1024
        arena_h = es.enter_context(nc.sbuf_tensor("arena", [128, ARENA_BYTES], U8))
        A = Arena(arena_h, ARENA_BYTES)
        psum = [es.enter_context(nc.psum_tensor("ps%d" % i, [128, 512], F32)) for i in range(8)]
        Bps = [P.buf("ps%d" % i) for i in range(8)]

        def ps_bf(i):
            return psum[i][:, :].bitcast(BF16)

        cst = A.alloc([128, 3, 128], F32)
        ident_b = A.alloc([128, 128], BF16)
        ones_b = A.alloc([128, 128], BF16)
        tri_b = A.alloc([128, 128], BF16)
        B_cst = P.buf("cst")
        P.add("sp", lambda e: e.dma_start(out=cst, in_=consts_d), r=[B_w], w=[B_cst], dma=True)
        P.add("dve", lambda e: e.tensor_copy(out=ident_b, in_=cst[:, 0, :]), r=[B_cst], w=[B_cst])
        P.add("dve", lambda e: e.tensor_copy(out=tri_b, in_=cst[:, 1, :]), r=[B_cst], w=[B_cst])
        P.add("dve", lambda e: e.tensor_copy(out=ones_b, in_=cst[:, 2, :]), r=[B_cst], w=[B_cst])
        tri_f = cst[:, 1, :]
        ones_f = cst[:, 2, :]
        KEEP = A.off
        P.barrier()

        class WTile:
            def __init__(self, ws, srcs, ncols):
                self.ws, self.srcs, self.ncols = ws, srcs, ncols
                slot = ws.wi % len(ws.wb)
                ws.wi += 1
                self.wb, self.Bwb = ws.wb[slot], ws.Bwb[slot]
                self.pieces = [(k0, min(ws.nk, k0 + ws.kp)) for k0 in range(0, ws.nk, ws.kp)]
                self.stg_of = {}
                self.nd = 0
                self.ncst = 0

            def _dma(self):
                ws = self.ws
                k0, k1 = self.pieces[self.nd]
                si = ws.si % len(ws.stg)
                ws.si += 1
                stg, Bstg = ws.stg[si], ws.Bstg[si]
                self.stg_of[self.nd] = (stg, Bstg)
                self.nd += 1
                for (c0, c1, fn) in self.srcs:
                    P.add("sp", lambda e, stg=stg, k0=k0, k1=k1, c0=c0, c1=c1, fn=fn: e.dma_start(
                        out=stg[:, 0:k1 - k0, c0:c1], in_=fn(k0, k1)), r=[B_w], w=[Bstg], dma=True)

            def step(self):
                if self.ncst >= len(self.pieces):
                    return False
                while self.nd < min(len(self.pieces), self.ncst + 2):
                    self._dma()
                k0, k1 = self.pieces[self.ncst]
                stg, Bstg = self.stg_of.pop(self.ncst)
                self.ncst += 1
                wb, ncols = self.wb, self.ncols
                P.add("pool", lambda e, stg=stg, k0=k0, k1=k1: e.tensor_copy(
                    out=wb[:, k0:k1, 0:ncols], in_=stg[:, 0:k1 - k0, 0:ncols]), r=[Bstg], w=[self.Bwb])
                return True

            def finish(self):
                while self.step():
                    pass
                return self.wb, self.Bwb

        class WStream:
            def __init__(self, nk, ncols, nslots=2, kp=4, nstg=4, name="w"):
                self.nk, self.ncols, self.kp = nk, ncols, kp
                self.stg = [A.alloc([128, kp, ncols], F32) for _ in range(nstg)]
                self.Bstg = [P.buf("%s_stg%d" % (name, i)) for i in range(nstg)]
                self.wb = [A.alloc([128, nk, ncols], BF16) for _ in range(nslots)]
                self.Bwb = [P.buf("%s_wb%d" % (name, i)) for i in range(nslots)]
                self.si = 0
                self.wi = 0

            def tile(self, src_fn=None, srcs=None, ncols=None):
                ncols = ncols or self.ncols
                if srcs is None:
                    srcs = [(0, ncols, src_fn)]
                return WTile(self, srcs, ncols)

        def pipelined(ws, tiles_srcs, nunits, first=None):
            cur = first
            if cur is None:
                cur = ws.tile(**tiles_srcs[0])
                cur.finish()
            for n in range(len(tiles_srcs)):
                nxt = ws.tile(**tiles_srcs[n + 1]) if n + 1 < len(tiles_srcs) else None
                npieces = len(nxt.pieces) if nxt is not None else 0
                state = {"done": 0}

                def tick(u, nxt=nxt, npieces=npieces, state=state):
                    if nxt is None:
                        return
                    want = min(npieces, (u + 1) * npieces // nunits + 1)
                    while state["done"] < want and nxt.step():
                        state["done"] += 1
                yield n, cur.wb, cur.Bwb, tick
                if nxt is not None:
                    nxt.finish()
                cur = nxt

        def mm_acc(ps_ap, Bp, pairs, rbufs):
            n = len(pairs)
            for i, (l, r_) in enumerate(pairs):
                P.add("pe", lambda e, l=l, r_=r_, i=i: e.matmul(ps_ap, l, r_, start=(i == 0), stop=(i == n - 1)),
                      r=rbufs, w=[Bp])

        def phase_norm(src_ap, Bsrc, r0, hT, BhT, g_row):
            gbc = A.alloc([128, D], F32)
            Bgbc = P.buf("gbc")
            P.add("sp", lambda e: e.dma_start(out=gbc, in_=g_row.partition_broadcast(128)), r=[B_w], w=[Bgbc], dma=True)
            xt = [A.alloc([128, D], F32) for _ in range(2)]
            Bxt = [P.buf("xt%d" % i) for i in range(2)]
            hn = [A.alloc([128, D], BF16) for _ in range(2)]
            Bhn = [P.buf("hn%d" % i) for i in range(2)]
            junk = A.alloc([128, D], BF16)
            Bjunk = P.buf("junk")
            st = A.alloc([128, 8], F32)
            Bst = P.buf("st")
            for i in range(HT):
                s = i % 2
                rows = slice(r0 + i * 128, r0 + (i + 1) * 128)
                P.add("sp", lambda e, s=s, rows=rows: e.dma_start(out=xt[s], in_=src_ap[rows, :]),
                      r=[Bsrc], w=[Bxt[s]], dma=True)
                P.add("act", lambda e, s=s: e.activation(out=junk, in_=xt[s], func=AF.Square, accum_out=st[:, 0:1]),
                      r=[Bxt[s]], w=[Bjunk, Bst])
                P.add("act", lambda e: e.activation(out=st[:, 1:2], in_=st[:, 0:1], func=AF.Sqrt,
                                                    scale=1.0 / D, bias=eps_t[:, 0:1]), r=[Bst], w=[Bst])
                P.add("dve", lambda e: e.reciprocal(out=st[:, 2:3], in_=st[:, 1:2]), r=[Bst], w=[Bst])
                P.add("dve", lambda e, s=s: e.scalar_tensor_tensor(out=hn[s], in0=xt[s], scalar=st[:, 2:3], in1=gbc,
                                                                   op0=ALU.mult, op1=ALU.mult),
                      r=[Bst, Bxt[s], Bgbc], w=[Bhn[s]])
                for q in range(4):
                    pb = 4 + (q % 2)
                    pv = ps_bf(pb)
                    for c4 in range(4):
                        c = q * 4 + c4
                        P.add("pe", lambda e, s=s, c=c, c4=c4, pv=pv: e.transpose(
                            out=pv[:, c4 * 128:(c4 + 1) * 128], in_=hn[s][:, c * 128:(c + 1) * 128],
                            identity=ident_b), r=[Bhn[s], B_cst], w=[Bps[pb]])
                    ce = "act" if q % 2 == 0 else "dve"
                    dst = hT[:, q * 4:(q + 1) * 4, i * 128:(i + 1) * 128]
                    srcv = pv[:, 0:512].rearrange("p (a b) -> p a b", a=4)
                    if ce == "act":
                        P.add("act", lambda e, dst=dst, srcv=srcv: e.copy(out=dst, in_=srcv), r=[Bps[pb]], w=[BhT[i]])
                    else:
                        P.add("dve", lambda e, dst=dst, srcv=srcv: e.tensor_copy(out=dst, in_=srcv),
                              r=[Bps[pb]], w=[BhT[i]])
                yield

        eps_t = None

        def phase1(l, j, src_ap, Bsrc):
            A.reset(KEEP2)
            r0 = j * T
            hT = A.alloc([128, KC, T], BF16)
            BhT = [P.buf("hT%d" % i) for i in range(HT)]
            mark = A.off
            gb = A.alloc([128, 16], F32)
            qkg = A.alloc([128, 4], F32)
            Bpar = P.buf("par1")
            P.add("sp", lambda e: e.dma_start(out=gb, in_=gbias_d[l]), r=[B_w], w=[Bpar], dma=True)
            P.add("sp", lambda e: e.dma_start(out=qkg[:, 0:2], in_=qkg_d[l]), r=[B_w], w=[Bpar], dma=True)
            P.add("dve", lambda e: e.scalar_tensor_tensor(out=qkg[:, 2:3], in0=qkg[:, 0:1], scalar=float(128 ** -0.5),
                                                          in1=qkg[:, 1:2], op0=ALU.mult, op1=ALU.mult),
                  r=[Bpar], w=[Bpar])
            E_all = A.alloc([128, HT, 8], F32)
            QS_all = A.alloc([128, HT, 4], F32)
            KS_all = A.alloc([128, HT, 4], F32)
            Bgate = P.buf("gates")
            ws = WStream(KC, 512, name="win")
            wsm_stg = A.alloc([128, KC, 16], F32)
            wsm = A.alloc([128, KC, 16], BF16)
            Bwsm_stg, Bwsm = P.buf("wsm_stg"), P.buf("wsm")
            NST = 3
            st_bf = [A.alloc([128, 4, 132], BF16) for _ in range(NST)]
            Bst_bf = [P.buf("st_bf%d" % i) for i in range(NST)]
            st2_bf = [A.alloc([128, 512], BF16) for _ in range(NST)]
            Bst2_bf = [P.buf("st2_bf%d" % i) for i in range(NST)]
            mvst = [A.alloc([128, 2, 257], BF16) for _ in range(2)]
            Bmvst = [P.buf("mvst%d" % i) for i in range(2)]
            tmp_f = [A.alloc([128, 512], F32) for _ in range(2)]
            Btmp_f = [P.buf("tmp_f%d" % i) for i in range(2)]
            sm = A.alloc([128, 64], F32)
            Bsm = P.buf("sm")
            tsb = A.alloc([128, 16], F32)
            Btsb = P.buf("tsb")
            for s in range(2):
                P.add("pool", lambda e, s=s: e.memset(mvst[s][:, :, 256:257], 1.0), w=[Bmvst[s]])
            cnt = {"st": 0, "st2": 0, "mv": 0, "tf": 0, "ps": 0}

            def nxt(k, n):
                v = cnt[k] % n
                cnt[k] += 1
                return v

            groups = []
            groups += [("fv", 2048 + 512 * g, g) for g in range(2)]
            groups += [("mq", 3080, 0), ("mk", 3592, 0)]
            groups += [("mv", 4104 + 512 * g, g) for g in range(2)]
            groups += [("mo", 5128 + 512 * g, g) for g in range(2)]
            groups += [("fq", 512 * g, g) for g in range(2)]
            groups += [("fk", 1024 + 512 * g, g) for g in range(2)]
            groups += [("ga", 6160 + 512 * g, g) for g in range(4)]
            groups += [("gm", 8208 + 512 * g, g) for g in range(4)]

            def sm_src(c0, c1, lo, hi):
                return w_in[l, :, lo:hi].rearrange("(c p) n -> p c n", p=128)
            with nc.allow_non_contiguous_dma(reason="tiny gate columns"):
                for (dlo, lo, hi) in ((0, 3072, 3080), (8, 6156, 6160), (12, 6152, 6156)):
                    P.add("sp", lambda e, dlo=dlo, lo=lo, hi=hi: e.dma_start(
                        out=wsm_stg[:, :, dlo:dlo + hi - lo],
                        in_=w_in[l, :, lo:hi].rearrange("(c p) n -> p c n", p=128)),
                        r=[B_w], w=[Bwsm_stg], dma=True)
            for k in range(KC):
                P.add("dve", lambda e, k=k: e.tensor_copy(out=wsm[:, k, :], in_=wsm_stg[:, k, :]),
                      r=[Bwsm_stg], w=[Bwsm])
            gsrcs = [dict(src_fn=lambda k0, k1, c0=c0: w_in[l, k0 * 128:k1 * 128, c0:c0 + 512].rearrange(
                "(c p) n -> p c n", p=128)) for (kind, c0, g) in groups]
            first_tile = ws.tile(**gsrcs[0])
            first_tile.finish()
            norm_it = phase_norm(src_ap, Bsrc, r0, hT, BhT, g1_d[l:l + 1, :])
            zb = sm[:, 0:16]
            e1 = sm[:, 16:32]
            sp_ = sm[:, 32:48]
            t4 = sm[:, 48:52]

            def smA(i):
                tok = slice(i * 128, (i + 1) * 128)
                mm_acc(psum[7][:, 0:16], Bps[7], [(hT[:, c, tok], wsm[:, c, :]) for c in range(KC)], [BhT[i], Bwsm])
                P.add("dve", lambda e: e.tensor_tensor(out=zb, in0=psum[7][:, 0:16], in1=gb, op=ALU.add),
                      r=[Bps[7], Bpar], w=[Bsm])
                P.add("act", lambda e: e.activation(out=e1, in_=zb, func=AF.Exp, scale=-1.0), r=[Bsm], w=[Bsm])
                P.add("act", lambda e: e.activation(out=sp_, in_=e1, func=AF.Ln, bias=one_t[:, 0:1]), r=[Bsm], w=[Bsm])

            def smB(i):
                gi = j * HT + i
                P.add("pe", lambda e: e.matmul(psum[7][:, 32:44], tri_f, sp_[:, 0:12], start=True, stop=True),
                      r=[Bsm, B_cst], w=[Bps[7]])
                P.add("pe", lambda e: e.matmul(psum[7][:, 64:76], ones_f, sp_[:, 0:12], start=True, stop=True),
                      r=[Bsm, B_cst], w=[Bps[7]])
                cum = psum[7][:, 32:44]
                P.add("act", lambda e, i=i: e.activation(out=E_all[:, i, :], in_=cum[:, 0:8], func=AF.Exp),
                      r=[Bps[7]], w=[Bgate])
                P.add("act", lambda e, i=i: e.activation(out=QS_all[:, i, :], in_=cum[:, 8:12], func=AF.Exp,
                                                         scale=-1.0, bias=lnsc_t[:, 0:1]), r=[Bps[7]], w=[Bgate])
                P.add("dve", lambda e: e.tensor_tensor(out=t4, in0=cum[:, 8:12], in1=zb[:, 12:16], op=ALU.add),
                      r=[Bps[7], Bsm], w=[Bsm])
                P.add("act", lambda e, i=i: e.activation(out=KS_all[:, i, :], in_=t4, func=AF.Exp),
                      r=[Bsm], w=[Bgate])
                P.add("act", lambda e: e.copy(out=tsb[:, 0:12], in_=psum[7][:, 64:76]), r=[Bps[7]], w=[Btsb])
                P.add("sp", lambda e, gi=gi: e.dma_start(out=tot_s[gi:gi + 1, :], in_=tsb[0:1, 0:12]),
                      r=[Btsb], w=[B["tot_s"]], dma=True)

            for _ in norm_it:
                pass
            for i in range(HT):
                smA(i)
                smB(i)
            for gn, wb, Bwb, tick in pipelined(ws, gsrcs, 16, first=first_tile):
                kind, c0, g = groups[gn]
                if kind in ("fv", "mq", "mk", "mv", "mo"):
                  def tm_unit(i, wb=wb, Bwb=Bwb, kind=kind, g=g):
                    if True:
                        gi = j * HT + i
                        tok = slice(i * 128, (i + 1) * 128)
                        pb = nxt("ps", 4)
                        ps = psum[pb]
                        mm_acc(ps[:, :], Bps[pb], [(hT[:, c, tok], wb[:, c, :]) for c in range(KC)], [BhT[i], Bwb])
                        if kind == "fv":
                            s = nxt("st", NST)
                            h0 = g * 4
                            P.add("dve", lambda e, s=s, ps=ps, i=i, h0=h0: e.tensor_tensor(
                                out=st_bf[s][:, :, 0:128], in0=ps[:, :].rearrange("p (h d) -> p h d", h=4),
                                in1=E_all[:, i, h0:h0 + 4].unsqueeze(2).to_broadcast([128, 4, 128]), op=ALU.mult),
                                r=[Bps[pb], Bgate], w=[Bst_bf[s]])
                            P.add("dve", lambda e, s=s, i=i, h0=h0: e.tensor_copy(
                                out=st_bf[s][:, :, 128:129], in_=E_all[:, i, h0:h0 + 4].unsqueeze(2)),
                                r=[Bgate], w=[Bst_bf[s]])
                            P.add("sp", lambda e, s=s, h0=h0, gi=gi: e.dma_start(
                                out=vp_s[h0:h0 + 4, :, gi, :].rearrange("h p c -> p h c"),
                                in_=st_bf[s][:, :, 0:129]), r=[Bst_bf[s]], w=[B["vp_s"]], dma=True)
                        elif kind in ("mq", "mk"):
                            s = nxt("st2", NST)
                            SC = QS_all if kind == "mq" else KS_all
                            P.add("dve", lambda e, s=s, ps=ps, i=i, SC=SC: e.tensor_tensor(
                                out=st2_bf[s][:, :].rearrange("p (h d) -> p h d", h=4),
                                in0=ps[:, :].rearrange("p (h d) -> p h d", h=4),
                                in1=SC[:, i, 0:4].unsqueeze(2).to_broadcast([128, 4, 128]), op=ALU.mult),
                                r=[Bps[pb], Bgate], w=[Bst2_bf[s]])
                            if kind == "mk":
                                P.add("sp", lambda e, s=s, gi=gi: e.dma_start(out=mk_s[gi], in_=st2_bf[s]),
                                      r=[Bst2_bf[s]], w=[B["mk_s"]], dma=True)
                            tb = 4 + (i % 2)
                            pv = ps_bf(tb)
                            for hh in range(4):
                                P.add("pe", lambda e, s=s, hh=hh, pv=pv: e.transpose(
                                    out=pv[:, hh * 128:(hh + 1) * 128], in_=st2_bf[s][:, hh * 128:(hh + 1) * 128],
                                    identity=ident_b), r=[Bst2_bf[s], B_cst], w=[Bps[tb]])
                            s2 = nxt("st2", NST)
                            P.add("act", lambda e, s2=s2, pv=pv: e.copy(out=st2_bf[s2], in_=pv[:, 0:512]),
                                  r=[Bps[tb]], w=[Bst2_bf[s2]])
                            dstT = mqT_s if kind == "mq" else mkT_s
                            Bd = B["mqT_s"] if kind == "mq" else B["mkT_s"]
                            P.add("sp", lambda e, s2=s2, dstT=dstT, gi=gi: e.dma_start(
                                out=dstT[:, :, gi * 128:(gi + 1) * 128].rearrange("h p t -> p h t"),
                                in_=st2_bf[s2][:, :].rearrange("p (h t) -> p h t", h=4)),
                                r=[Bst2_bf[s2]], w=[Bd], dma=True)
                        elif kind == "mv":
                            s = nxt("mv", 2)
                            P.add("act", lambda e, s=s, ps=ps: e.copy(
                                out=mvst[s][:, :, 0:256], in_=ps[:, :].rearrange("p (h d) -> p h d", h=2)),
                                r=[Bps[pb]], w=[Bmvst[s]])
                            P.add("sp", lambda e, s=s, gi=gi, g=g: e.dma_start(
                                out=mv_s[gi, :, 2 * g:2 * g + 2, :], in_=mvst[s]),
                                r=[Bmvst[s]], w=[B["mv_s"]], dma=True)
                        else:
                            s = nxt("st2", NST)
                            P.add("act", lambda e, s=s, ps=ps: e.activation(out=st2_bf[s], in_=ps[:, :],
                                                                            func=AF.Sigmoid),
                                  r=[Bps[pb]], w=[Bst2_bf[s]])
                            P.add("sp", lambda e, s=s, gi=gi, g=g: e.dma_start(
                                out=mo_s[gi, :, 512 * g:512 * (g + 1)], in_=st2_bf[s]),
                                r=[Bst2_bf[s]], w=[B["mo_s"]], dma=True)
                  for i in range(HT):
                      tick(i)
                      tm_unit(i)
                else:
                    for q in range(4):
                        ch = g * 4 + q
                        for tg in range(TG):
                            tick(q * TG + tg)
                            tks = slice(tg * 512, (tg + 1) * 512)
                            gt0 = r0 + tg * 512
                            pb = nxt("ps", 4)
                            ps = psum[pb]
                            mm_acc(ps[:, :], Bps[pb], [(wb[:, c, q * 128:(q + 1) * 128], hT[:, c, tks])
                                                       for c in range(KC)], BhT[4 * tg:4 * tg + 4] + [Bwb])
                            if kind in ("ga", "gm"):
                                s = nxt("st2", NST)
                                P.add("act", lambda e, s=s, ps=ps: e.activation(out=st2_bf[s], in_=ps[:, :],
                                                                                func=AF.Sigmoid),
                                      r=[Bps[pb]], w=[Bst2_bf[s]])
                                dstT = gaT_s if kind == "ga" else gmT_s
                                Bd = B["gaT_s"] if kind == "ga" else B["gmT_s"]
                                P.add("sp", lambda e, s=s, dstT=dstT, ch=ch, gt0=gt0: e.dma_start(
                                    out=dstT[ch, :, gt0:gt0 + 512], in_=st2_bf[s]),
                                    r=[Bst2_bf[s]], w=[Bd], dma=True)
                            else:
                                s = nxt("st2", NST)
                                tf = nxt("tf", 2)
                                P.add("act", lambda e, s=s, ps=ps: e.activation(out=st2_bf[s], in_=ps[:, :],
                                                                                func=AF.Square),
                                      r=[Bps[pb]], w=[Bst2_bf[s]])
                                P.add("pe", lambda e, s=s: e.matmul(psum[6][:, :], ones_b, st2_bf[s],
                                                                    start=True, stop=True),
                                      r=[Bst2_bf[s], B_cst], w=[Bps[6]])
                                P.add("act", lambda e, tf=tf: e.activation(out=tmp_f[tf], in_=psum[6][:, :],
                                                                           func=AF.Sqrt, scale=1.0 / 128,
                                                                           bias=eps_t[:, 0:1]),
                                      r=[Bps[6]], w=[Btmp_f[tf]])
                                P.add("dve", lambda e, tf=tf: e.reciprocal(out=tmp_f[tf], in_=tmp_f[tf]),
                                      r=[Btmp_f[tf]], w=[Btmp_f[tf]])
                                s2 = nxt("st2", NST)
                                if kind == "fq":
                                    P.add("dve", lambda e, s2=s2, ps=ps, tf=tf: e.scalar_tensor_tensor(
                                        out=st2_bf[s2], in0=ps[:, :], scalar=qkg[:, 2:3], in1=tmp_f[tf],
                                        op0=ALU.mult, op1=ALU.mult), r=[Bps[pb], Btmp_f[tf], Bpar], w=[Bst2_bf[s2]])
                                else:
                                    P.add("dve", lambda e, s2=s2, ps=ps, tf=tf: e.tensor_tensor(
                                        out=st2_bf[s2], in0=ps[:, :], in1=tmp_f[tf], op=ALU.mult),
                                        r=[Bps[pb], Btmp_f[tf]], w=[Bst2_bf[s2]])
                                dstT = qT_s if kind == "fq" else kT_s
                                Bd = B["qT_s"] if kind == "fq" else B["kT_s"]
                                P.add("sp", lambda e, s2=s2, dstT=dstT, ch=ch, gt0=gt0: e.dma_start(
                                    out=dstT[ch, :, gt0:gt0 + 512], in_=st2_bf[s2]),
                                    r=[Bst2_bf[s2]], w=[Bd], dma=True)
            P.barrier()


        def phase2(l):
            A.reset(KEEP2)
            QG = S // 512
            totb = A.alloc([128, NT, 12], F32)
            fcn = A.alloc([128, NT, 8], F32)
            Btot = P.buf("totb")
            qkg = A.alloc([128, 4], F32)
            bnd = A.alloc([128, 4], F32)
            Bpar = P.buf("par2")
            P.add("sp", lambda e: e.dma_start(out=totb, in_=tot_s.partition_broadcast(128)),
                  r=[B["tot_s"]], w=[Btot], dma=True)
            P.add("sp", lambda e: e.dma_start(out=qkg[:, 0:2], in_=qkg_d[l]), r=[B_w], w=[Bpar], dma=True)
            P.add("dve", lambda e: e.scalar_tensor_tensor(out=qkg[:, 2:3], in0=qkg[:, 0:1], scalar=float(128 ** -0.5),
                                                          in1=qkg[:, 1:2], op0=ALU.mult, op1=ALU.mult),
                  r=[Bpar], w=[Bpar])
            P.add("pe", lambda e: e.transpose(out=psum[6][0:1, 0:128], in_=qkg[:, 2:3], identity=cst[:, 0, :]),
                  r=[Bpar, B_cst], w=[Bps[6]])
            P.add("dve", lambda e: e.tensor_reduce(out=bnd[0:1, 0:1], in_=psum[6][0:1, 0:128], axis=AX.X, op=ALU.max,
                                                   apply_absolute_value=True), r=[Bps[6]], w=[Bpar])
            P.add("dve", lambda e: e.tensor_scalar(out=bnd[0:1, 1:2], in0=bnd[0:1, 0:1], scalar1=128.0, scalar2=None,
                                                   op0=ALU.mult), r=[Bpar], w=[Bpar])
            P.add("pe", lambda e: e.matmul(psum[6][:, 128:129], ones_f[0:1, :], bnd[0:1, 1:2], start=True, stop=True),
                  r=[Bpar, B_cst], w=[Bps[6]])
            P.add("dve", lambda e: e.tensor_copy(out=bnd[:, 2:3], in_=psum[6][:, 128:129]), r=[Bps[6]], w=[Bpar])
            P.add("pool", lambda e: e.memset(fcn[:, 0, :], 0.0), w=[Btot])
            for i in range(1, NT):
                P.add("dve", lambda e, i=i: e.tensor_tensor(out=fcn[:, i, :], in0=fcn[:, i - 1, :],
                                                            in1=totb[:, i - 1, 0:8], op=ALU.add), r=[Btot], w=[Btot])
            NS = 2
            kT = [A.alloc([128, S], BF16) for _ in range(NS)]
            qT = [A.alloc([128, S], BF16) for _ in range(NS)]
            vp = [A.alloc([128, NT, 129], BF16) for _ in range(NS)]
            Dt = [A.alloc([128, NT, NT], F32) for _ in range(NS)]
            Dhi = [A.alloc([128, NT, NT], BF16) for _ in range(NS)]
            Dlo = [A.alloc([128, NT, NT], F32) for _ in range(NS)]
            DX = [A.alloc([128, NT, NT], BF16) for _ in range(NS)]
            BkT = [P.buf("a_kT%d" % i) for i in range(NS)]
            BqT = [P.buf("a_qT%d" % i) for i in range(NS)]
            Bvp = [P.buf("a_vp%d" % i) for i in range(NS)]
            BDt = [P.buf("a_D%d" % i) for i in range(NS)]
            NPT = 3
            pt = [A.alloc([128, 512], BF16) for _ in range(NPT)]
            Bpt = [P.buf("a_pt%d" % i) for i in range(NPT)]
            ao = [A.alloc([128, 512], BF16) for _ in range(2)]
            Bao = [P.buf("a_ao%d" % i) for i in range(2)]
            aoT = [A.alloc([128, 512], BF16) for _ in range(2)]
            BaoT = [P.buf("a_aoT%d" % i) for i in range(2)]
            rc = A.alloc([128, 8], F32)
            Brc = P.buf("a_rc")
            ptc = 0
            stc = 0
            for h in range(NH):
                s = h % NS
                P.add("sp", lambda e, s=s, h=h: e.dma_start(out=kT[s], in_=kT_s[h]), r=[B["kT_s"]], w=[BkT[s]], dma=True)
                P.add("sp", lambda e, s=s, h=h: e.dma_start(out=qT[s], in_=qT_s[h]), r=[B["qT_s"]], w=[BqT[s]], dma=True)
                P.add("sp", lambda e, s=s, h=h: e.dma_start(out=vp[s], in_=vp_s[h]), r=[B["vp_s"]], w=[Bvp[s]], dma=True)
                P.add("dve", lambda e, s=s, h=h: e.scalar_tensor_tensor(
                    out=Dt[s], in0=fcn[:, :, h].unsqueeze(1).to_broadcast([128, NT, NT]), scalar=bnd[:, 2:3],
                    in1=fcn[:, :, h].unsqueeze(2).to_broadcast([128, NT, NT]), op0=ALU.subtract, op1=ALU.subtract),
                    r=[Btot, Bpar], w=[BDt[s]])
                P.add("dve", lambda e, s=s: e.tensor_copy(out=Dhi[s], in_=Dt[s]), r=[BDt[s]], w=[BDt[s]])
                P.add("dve", lambda e, s=s: e.tensor_tensor(out=Dlo[s], in0=Dt[s], in1=Dhi[s], op=ALU.subtract),
                      r=[BDt[s]], w=[BDt[s]])
                P.add("dve", lambda e, s=s: e.tensor_scalar(out=Dlo[s], in0=Dlo[s], scalar1=cst[:, 0, 1:2],
                                                            scalar2=None, op0=ALU.mult), r=[BDt[s], B_cst], w=[BDt[s]])
                P.add("dve", lambda e, s=s: e.scalar_tensor_tensor(
                    out=DX[s], in0=Dhi[s], scalar=cst[:, 0, 0:1], in1=Dlo[s], op0=ALU.mult, op1=ALU.add),
                    r=[BDt[s], B_cst], w=[BDt[s]])
                steps = [(g, j) for g in range(QG) for j in range(4 * g + 4)]

                def emit_qk(st):
                    g, j = steps[st]
                    i0 = 4 * g
                    ilo = max(i0, j)
                    ncol = (i0 + 4 - ilo) * 128
                    qlo = ilo * 128
                    sb = 4 + (st % 2)
                    P.add("pe", lambda e, s=s, j=j, sb=sb, qlo=qlo, ncol=ncol: e.matmul(
                        psum[sb][:, 0:ncol], kT[s][:, j * 128:(j + 1) * 128], qT[s][:, qlo:qlo + ncol],
                        start=True, stop=False), r=[BkT[s], BqT[s]], w=[Bps[sb]])
                    nseg = i0 + 4 - ilo
                    P.add("pe", lambda e, s=s, j=j, sb=sb, ilo=ilo, nseg=nseg, ncol=ncol: e.matmul(
                        psum[sb][:, 0:ncol], ones_b,
                        DX[s][:, ilo:ilo + nseg, j:j + 1].to_broadcast([128, nseg, 128]),
                        start=False, stop=True), r=[BDt[s], B_cst], w=[Bps[sb]])

                emit_qk(0)
                for st, (g, j) in enumerate(steps):
                    if True:
                        i0 = 4 * g
                        ilo = max(i0, j)
                        sb = 4 + (st % 2)
                        p = ptc % NPT
                        ptc += 1
                        if st + 1 < len(steps):
                            emit_qk(st + 1)
                        ncol = (i0 + 4 - ilo) * 128
                        P.add("act", lambda e, p=p, sb=sb, ncol=ncol: e.activation(
                            out=pt[p][:, 0:ncol], in_=psum[sb][:, 0:ncol], func=AF.Exp),
                            r=[Bps[sb]], w=[Bpt[p]])
                        for i in range(ilo, i0 + 4):
                            c0 = (i - ilo) * 128
                            if i == j:
                                P.add("dve", lambda e, p=p, c0=c0: e.tensor_tensor(
                                    out=pt[p][:, c0:c0 + 128], in0=pt[p][:, c0:c0 + 128], in1=tri_b, op=ALU.mult),
                                    r=[Bpt[p], B_cst], w=[Bpt[p]])
                        for i in range(ilo, i0 + 4):
                            c0 = (i - ilo) * 128
                            ab = i - i0
                            P.add("pe", lambda e, s=s, p=p, c0=c0, ab=ab, i=i, j=j: e.matmul(
                                psum[ab][:, 0:129], pt[p][:, c0:c0 + 128], vp[s][:, j, :],
                                start=(j == 0), stop=(j == i)), r=[Bpt[p], Bvp[s]], w=[Bps[ab]])
                    if j != 4 * g + 3:
                        continue
                    a = stc % 2
                    stc += 1
                    for m in range(4):
                        P.add("dve", lambda e, m=m: e.reciprocal(out=rc[:, m:m + 1], in_=psum[m][:, 128:129]),
                              r=[Bps[m]], w=[Brc])
                        P.add("dve", lambda e, m=m, a=a: e.tensor_scalar(
                            out=ao[a][:, m * 128:(m + 1) * 128], in0=psum[m][:, 0:128], scalar1=rc[:, m:m + 1],
                            scalar2=None, op0=ALU.mult), r=[Bps[m], Brc], w=[Bao[a]])
                    pv = ps_bf(6)
                    for m in range(4):
                        P.add("pe", lambda e, m=m, a=a, pv=pv: e.transpose(
                            out=pv[:, m * 128:(m + 1) * 128], in_=ao[a][:, m * 128:(m + 1) * 128], identity=ident_b),
                            r=[Bao[a], B_cst], w=[Bps[6]])
                    P.add("dve", lambda e, a=a, pv=pv: e.tensor_copy(out=aoT[a], in_=pv[:, 0:512]),
                          r=[Bps[6]], w=[BaoT[a]])
                    P.add("sp", lambda e, a=a, h=h, g=g: e.dma_start(out=aoT_s[h, :, g * 512:(g + 1) * 512],
                                                                     in_=aoT[a]),
                          r=[BaoT[a]], w=[B["aoT_s"]], dma=True)
            P.barrier()

        def phase3(l):
            A.reset(KEEP2)
            totb = A.alloc([128, NT, 12], F32)
            eL = A.alloc([128, NT, 4], F32)
            Btot = P.buf("m_tot")
            gbc = A.alloc([128, 1024], F32)
            Bg = P.buf("m_g")
            P.add("sp", lambda e: e.dma_start(out=totb, in_=tot_s.partition_broadcast(128)),
                  r=[B["tot_s"]], w=[Btot], dma=True)
            P.add("sp", lambda e: e.dma_start(out=gbc, in_=mng_d[l]), r=[B_w], w=[Bg], dma=True)
            P.add("act", lambda e: e.activation(out=eL, in_=totb[:, :, 8:12], func=AF.Exp, scale=-1.0),
                  r=[Btot], w=[Btot])
            C32 = A.alloc([128, MH, 257], F32)
            Cb = A.alloc([128, MH, 257], BF16)
            BC32, BCb = P.buf("m_C32"), P.buf("m_Cb")
            NS = 2
            qT4 = [A.alloc([128, MH, 128], BF16) for _ in range(NS)]
            kT4 = [A.alloc([128, MH, 128], BF16) for _ in range(NS)]
            k4 = [A.alloc([128, 512], BF16) for _ in range(NS)]
            v4 = [A.alloc([128, MH, 257], BF16) for _ in range(NS)]
            o4 = [A.alloc([128, 1024], BF16) for _ in range(NS)]
            Bin = [P.buf("m_in%d" % i) for i in range(NS)]
            wt = [A.alloc([128, MH, 128], BF16) for _ in range(2)]
            Bwt = [P.buf("m_wt%d" % i) for i in range(2)]
            tmpU = [A.alloc([128, 257], F32) for _ in range(2)]
            BtmpU = [P.buf("m_tmpU%d" % i) for i in range(2)]
            sm = A.alloc([128, 16], F32)
            Bsm = P.buf("m_sm")
            junk = A.alloc([128, 256], BF16)
            Bjunk = P.buf("m_junk")
            mo = [A.alloc([128, 1024], BF16) for _ in range(2)]
            Bmo = [P.buf("m_mo%d" % i) for i in range(2)]
            moT = [A.alloc([128, 8, 128], BF16) for _ in range(2)]
            BmoT = [P.buf("m_moT%d" % i) for i in range(2)]
            uc = 0
            BC32h = [P.buf("m_C32_%d" % h) for h in range(MH)]
            BCbh = [P.buf("m_Cb_%d" % h) for h in range(MH)]
            for h in range(MH):
                P.add("pool", lambda e, h=h: e.memset(C32[:, h, :], 0.0), w=[BC32h[h]])
                P.add("pool", lambda e, h=h: e.memset(Cb[:, h, :], 0.0), w=[BCbh[h]])
            dn = A.alloc([128, 32], F32)
            Bdn, Bss = P.buf("m_dn"), P.buf("m_ss")
            t2 = [A.alloc([128, 256], F32) for _ in range(4)]
            Bt2 = [P.buf("m_t2%d" % i) for i in range(4)]
            pending = []
            for c in range(NT):
                s = c % NS
                tok = slice(c * 128, (c + 1) * 128)
                P.add("sp", lambda e, s=s, tok=tok: e.dma_start(
                    out=qT4[s], in_=mqT_s[:, :, tok].rearrange("h p t -> p h t")), r=[B["mqT_s"]], w=[Bin[s]], dma=True)
                P.add("sp", lambda e, s=s, tok=tok: e.dma_start(
                    out=kT4[s], in_=mkT_s[:, :, tok].rearrange("h p t -> p h t")), r=[B["mkT_s"]], w=[Bin[s]], dma=True)
                P.add("sp", lambda e, s=s, c=c: e.dma_start(out=k4[s], in_=mk_s[c]), r=[B["mk_s"]], w=[Bin[s]], dma=True)
                P.add("sp", lambda e, s=s, c=c: e.dma_start(out=v4[s], in_=mv_s[c]), r=[B["mv_s"]], w=[Bin[s]], dma=True)
                P.add("sp", lambda e, s=s, c=c: e.dma_start(out=o4[s], in_=mo_s[c]), r=[B["mo_s"]], w=[Bin[s]], dma=True)
                w_ = c % 2
                for h in range(MH):
                    P.add("pe", lambda e, s=s, h=h: e.matmul(psum[4][:, h * 128:(h + 1) * 128], kT4[s][:, h, :],
                                                             qT4[s][:, h, :], start=True, stop=True),
                          r=[Bin[s]], w=[Bps[4]])
                P.add("dve", lambda e, w_=w_: e.tensor_tensor(
                    out=wt[w_], in0=psum[4][:, :].rearrange("p (h t) -> p h t", h=MH),
                    in1=tri_b.unsqueeze(1).to_broadcast([128, MH, 128]), op=ALU.mult),
                    r=[Bps[4], B_cst], w=[Bwt[w_]])
                for h in range(MH):
                    hb = h // 2
                    hc = (h % 2) * 256
                    P.add("pe", lambda e, s=s, h=h, w_=w_, hb=hb, hc=hc: e.matmul(
                        psum[hb][:, hc:hc + 256], wt[w_][:, h, :], v4[s][:, h, 0:256], start=True, stop=False),
                        r=[Bwt[w_], Bin[s]], w=[Bps[hb]])
                    P.add("pe", lambda e, s=s, h=h, hb=hb, hc=hc: e.matmul(
                        psum[hb][:, hc:hc + 256], qT4[s][:, h, :], Cb[:, h, 0:256], start=False, stop=True),
                        r=[Bin[s], BCbh[h]], w=[Bps[hb]])
                for h in range(MH):
                    P.add("pe", lambda e, s=s, h=h, w_=w_: e.matmul(
                        psum[2][:, h:h + 1], wt[w_][:, h, :], v4[s][:, h, 256:257], start=True, stop=False),
                        r=[Bwt[w_], Bin[s]], w=[Bps[2]])
                    P.add("pe", lambda e, s=s, h=h: e.matmul(
                        psum[2][:, h:h + 1], qT4[s][:, h, :], Cb[:, h, 256:257], start=False, stop=True),
                        r=[Bin[s], BCbh[h]], w=[Bps[2]])
                for h in range(MH):
                    ub = 5 + (uc % 2)
                    u = uc % 2
                    uc += 1
                    P.add("pe", lambda e, s=s, h=h, ub=ub: e.matmul(psum[ub][:, 0:257], k4[s][:, h * 128:(h + 1) * 128],
                                                                    v4[s][:, h, :], start=True, stop=True),
                          r=[Bin[s]], w=[Bps[ub]])
                    P.add("dve", lambda e, u=u, ub=ub, h=h: e.tensor_tensor(
                        out=tmpU[u], in0=psum[ub][:, 0:257], in1=C32[:, h, :], op=ALU.add),
                        r=[Bps[ub], BC32h[h]], w=[BtmpU[u]])
                    P.add("act", lambda e, u=u, c=c, h=h: e.activation(
                        out=Cb[:, h, :], in_=tmpU[u], func=AF.Copy, scale=eL[:, c, h:h + 1]),
                        r=[BtmpU[u], Btot], w=[BCbh[h]])
                    P.add("dve", lambda e, u=u, c=c, h=h: e.tensor_scalar(
                        out=C32[:, h, :], in0=tmpU[u], scalar1=eL[:, c, h:h + 1], scalar2=None, op0=ALU.mult),
                        r=[BtmpU[u], Btot], w=[BC32h[h]])
                for fn in pending:
                    fn()
                pending = []
                P.add("dve", lambda e: e.tensor_reduce(out=dn[:, 0:4], in_=psum[2][:, 0:4].unsqueeze(2), axis=AX.X,
                                                       op=ALU.max, apply_absolute_value=True), r=[Bps[2]], w=[Bdn])
                P.add("dve", lambda e: e.tensor_scalar(out=dn[:, 4:8], in0=dn[:, 0:4], scalar1=1.0, scalar2=None,
                                                       op0=ALU.max), r=[Bdn], w=[Bdn])
                P.add("dve", lambda e: e.reciprocal(out=dn[:, 8:12], in_=dn[:, 4:8]), r=[Bdn], w=[Bdn])
                for h in range(MH):
                    hb = h // 2
                    hc = (h % 2) * 256
                    P.add("act", lambda e, h=h, hb=hb, hc=hc: e.activation(
                        out=junk, in_=psum[hb][:, hc:hc + 256], func=AF.Square, scale=dn[:, 8 + h:9 + h],
                        accum_out=dn[:, 12 + h:13 + h]), r=[Bps[hb], Bdn], w=[Bjunk, Bss])
                P.add("act", lambda e: e.activation(out=dn[:, 16:20], in_=dn[:, 12:16], func=AF.Sqrt, scale=1.0 / 256,
                                                    bias=eps_t[:, 0:1]), r=[Bss], w=[Bss])
                P.add("dve", lambda e: e.reciprocal(out=dn[:, 20:24], in_=dn[:, 16:20]), r=[Bss], w=[Bss])
                P.add("dve", lambda e: e.tensor_tensor(out=dn[:, 24:28], in0=dn[:, 20:24], in1=dn[:, 8:12], op=ALU.mult),
                      r=[Bss, Bdn], w=[Bss])
                for h in range(MH):
                    hb = h // 2
                    hc = (h % 2) * 256
                    P.add("dve", lambda e, h=h, hb=hb, hc=hc: e.scalar_tensor_tensor(
                        out=t2[h], in0=psum[hb][:, hc:hc + 256], scalar=dn[:, 24 + h:25 + h],
                        in1=gbc[:, h * 256:(h + 1) * 256], op0=ALU.mult, op1=ALU.mult),
                        r=[Bps[hb], Bss, Bg], w=[Bt2[h]])
                    P.add("pool", lambda e, h=h, w_=w_, s=s: e.tensor_tensor(
                        out=mo[w_][:, h * 256:(h + 1) * 256], in0=t2[h], in1=o4[s][:, h * 256:(h + 1) * 256],
                        op=ALU.mult), r=[Bt2[h], Bin[s]], w=[Bmo[w_]])

                def do_transposes(w_=w_, tok=tok):
                    pv = ps_bf(7)
                    for q in range(8):
                        P.add("pe", lambda e, q=q, pv=pv: e.transpose(
                            out=pv[:, q * 128:(q + 1) * 128], in_=mo[w_][:, q * 128:(q + 1) * 128], identity=ident_b),
                            r=[Bmo[w_], B_cst], w=[Bps[7]])
                    P.add("act", lambda e, pv=pv: e.copy(out=moT[w_], in_=pv[:, 0:1024].rearrange(
                        "p (q t) -> p q t", q=8)), r=[Bps[7]], w=[BmoT[w_]])
                    P.add("sp", lambda e: e.dma_start(
                        out=moT_s[:, :, tok].rearrange("q p t -> p q t"), in_=moT[w_]),
                        r=[BmoT[w_]], w=[B["moT_s"]], dma=True)
                pending.append(do_transposes)
            for fn in pending:
                fn()
            P.barrier()


        def phase4(l, j, src_ap, Bsrc):
            A.reset(KEEP2)
            r0 = j * T
            mT = A.alloc([128, KC, T], BF16)
            BmT = P.buf("mT")
            mark = A.off
            TH = T // 2
            aoT = A.alloc([128, NH, TH], BF16)
            moT = A.alloc([128, NH, TH], BF16)
            Bao, Bmo = P.buf("p4_ao"), P.buf("p4_mo")
            ws = WStream(8, 512, nslots=4, name="p4w")
            gat = [A.alloc([128, TH], BF16) for _ in range(2)]
            gmt = [A.alloc([128, TH], BF16) for _ in range(2)]
            Bgat = [P.buf("p4_ga%d" % i) for i in range(2)]
            Bgmt = [P.buf("p4_gm%d" % i) for i in range(2)]
            t1 = [A.alloc([128, 512], F32) for _ in range(2)]
            t2 = [A.alloc([128, 512], F32) for _ in range(2)]
            Bt1 = [P.buf("p4_t1%d" % i) for i in range(2)]
            Bt2 = [P.buf("p4_t2%d" % i) for i in range(2)]
            seq = [(th, n) for th in range(2) for n in range(4)]

            def mk(idx):
                th, n = seq[idx]
                ta = ws.tile(src_fn=lambda k0, k1, n=n: w_pa[l, k0 * 128:k1 * 128, n * 512:(n + 1) * 512].rearrange(
                    "(c p) n -> p c n", p=128))
                tm = ws.tile(src_fn=lambda k0, k1, n=n: w_pm[l, k0 * 128:k1 * 128, n * 512:(n + 1) * 512].rearrange(
                    "(c p) n -> p c n", p=128))
                return ta, tm

            def load_gates(th, dc):
                gs = dc % 2
                c0 = r0 + th * TH
                P.add("sp", lambda e, gs=gs, dc=dc, c0=c0: e.dma_start(out=gat[gs], in_=gaT_s[dc, :, c0:c0 + TH]),
                      r=[B["gaT_s"]], w=[Bgat[gs]], dma=True)
                P.add("sp", lambda e, gs=gs, dc=dc, c0=c0: e.dma_start(out=gmt[gs], in_=gmT_s[dc, :, c0:c0 + TH]),
                      r=[B["gmT_s"]], w=[Bgmt[gs]], dma=True)

            cur = mk(0)
            cur[0].finish()
            cur[1].finish()
            load_gates(0, 0)
            k = 0
            for idx, (th, n) in enumerate(seq):
                if n == 0:
                    c0 = r0 + th * TH
                    P.add("sp", lambda e, c0=c0: e.dma_start(
                        out=aoT, in_=aoT_s[:, :, c0:c0 + TH].rearrange("h p t -> p h t")),
                        r=[B["aoT_s"]], w=[Bao], dma=True)
                    P.add("sp", lambda e, c0=c0: e.dma_start(
                        out=moT, in_=moT_s[:, :, c0:c0 + TH].rearrange("h p t -> p h t")),
                        r=[B["moT_s"]], w=[Bmo], dma=True)
                nxt_t = mk(idx + 1) if idx + 1 < len(seq) else None
                wa, Bwa = cur[0].wb, cur[0].Bwb
                wm, Bwm = cur[1].wb, cur[1].Bwb
                un = 0
                for q in range(4):
                    dc = n * 4 + q
                    gs = dc % 2
                    if q < 3:
                        load_gates(th, dc + 1)
                    elif idx + 1 < len(seq):
                        load_gates(seq[idx + 1][0], seq[idx + 1][1] * 4)
                    for tg in range(TH // 512):
                        if nxt_t is not None and un % 2 == 0:
                            nxt_t[(un // 2) % 2].step()
                        un += 1
                        tks = slice(tg * 512, (tg + 1) * 512)
                        otk = slice(th * TH + tg * 512, th * TH + (tg + 1) * 512)
                        pa = (k % 3)
                        pm = 3 + (k % 3)
                        u = k % 2
                        k += 1
                        mm_acc(psum[pa][:, :], Bps[pa], [(wa[:, c, q * 128:(q + 1) * 128], aoT[:, c, tks])
                                                         for c in range(8)], [Bwa, Bao])
                        mm_acc(psum[pm][:, :], Bps[pm], [(wm[:, c, q * 128:(q + 1) * 128], moT[:, c, tks])
                                                         for c in range(8)], [Bwm, Bmo])
                        P.add("dve", lambda e, pa=pa, u=u, gs=gs, tks=tks: e.tensor_tensor(
                            out=t1[u], in0=psum[pa][:, :], in1=gat[gs][:, tks], op=ALU.mult),
                            r=[Bps[pa], Bgat[gs]], w=[Bt1[u]])
                        P.add("dve", lambda e, pm=pm, u=u, gs=gs, tks=tks: e.tensor_tensor(
                            out=t2[u], in0=psum[pm][:, :], in1=gmt[gs][:, tks], op=ALU.mult),
                            r=[Bps[pm], Bgmt[gs]], w=[Bt2[u]])
                        P.add("dve", lambda e, u=u, dc=dc, otk=otk: e.tensor_tensor(
                            out=mT[:, dc, otk], in0=t1[u], in1=t2[u], op=ALU.add),
                            r=[Bt1[u], Bt2[u]], w=[BmT])
                if nxt_t is not None:
                    nxt_t[0].finish()
                    nxt_t[1].finish()
                cur = nxt_t
            P.barrier()
            A.reset(mark)
            ws = WStream(KC, 512, name="p4o")
            xp = [A.alloc([128, 512], F32) for _ in range(3)]
            Bxp = [P.buf("p4_xp%d" % i) for i in range(3)]
            xo = [A.alloc([128, 512], F32) for _ in range(3)]
            Bxo = [P.buf("p4_xo%d" % i) for i in range(3)]
            osrcs = [dict(src_fn=lambda k0, k1, n=n: w_out[l, k0 * 128:k1 * 128, n * 512:(n + 1) * 512].rearrange(
                "(c p) n -> p c n", p=128)) for n in range(4)]

            def ld_x(kk):
                n, i = divmod(kk, HT)
                u = kk % 3
                rows = slice(r0 + i * 128, r0 + (i + 1) * 128)
                cols = slice(n * 512, (n + 1) * 512)
                P.add("sp", lambda e, u=u, rows=rows, cols=cols: e.dma_start(out=xp[u], in_=src_ap[rows, cols]),
                      r=[Bsrc], w=[Bxp[u]], dma=True)

            ld_x(0)
            k = 0
            for n, wb, Bwb, tick in pipelined(ws, osrcs, HT):
                cols = slice(n * 512, (n + 1) * 512)
                for i in range(HT):
                    tick(i)
                    rows = slice(r0 + i * 128, r0 + (i + 1) * 128)
                    tok = slice(i * 128, (i + 1) * 128)
                    pb = k % 4
                    u = k % 3
                    if k + 1 < 4 * HT:
                        ld_x(k + 1)
                    k += 1
                    mm_acc(psum[pb][:, :], Bps[pb], [(mT[:, c, tok], wb[:, c, :]) for c in range(KC)], [BmT, Bwb])
                    P.add("dve", lambda e, u=u, pb=pb: e.tensor_tensor(out=xo[u], in0=psum[pb][:, :], in1=xp[u],
                                                                        op=ALU.add), r=[Bps[pb], Bxp[u]], w=[Bxo[u]])
                    P.add("sp", lambda e, u=u, rows=rows, cols=cols: e.dma_start(out=x_mid[rows, cols], in_=xo[u]),
                          r=[Bxo[u]], w=[B["x_mid"]], dma=True)
            P.barrier()

        def phase5(l, j, dst_ap, Bdst):
            A.reset(KEEP2)
            r0 = j * T
            h2T = A.alloc([128, KC, T], BF16)
            Bh2T = [P.buf("h2T%d" % i) for i in range(HT)]
            mark = A.off
            cw = A.alloc([128, 2 * FC, 3], F32)
            cb = A.alloc([128, 2 * FC], F32)
            Bcw = P.buf("p5_cw")
            P.add("sp", lambda e: e.dma_start(out=cw, in_=cw_d[l]), r=[B_w], w=[Bcw], dma=True)
            P.add("sp", lambda e: e.dma_start(out=cb, in_=cb_d[l]), r=[B_w], w=[Bcw], dma=True)
            if j == 0:
                P.add("pool", lambda e: e.memset(hist, 0.0), w=[Bhist])
            ws = WStream(KC, 256, name="p5u")
            av = [[A.alloc([128, 512], F32) for _ in range(3)] for _ in range(2)]
            Bav = [[P.buf("p5_a%d%d" % (x, i)) for i in range(3)] for x in range(2)]
            sg = [A.alloc([128, 512], F32) for _ in range(3)]
            Bsg = [P.buf("p5_sg%d" % i) for i in range(3)]
            ast = [A.alloc([128, 512], BF16) for _ in range(3)]
            Bast = [P.buf("p5_ast%d" % i) for i in range(3)]
            k = 0
            usrcs = [dict(srcs=[
                (0, 128, lambda k0, k1, fc=fc: w_up[l, k0 * 128:k1 * 128, fc * 128:(fc + 1) * 128].rearrange(
                    "(c p) n -> p c n", p=128)),
                (128, 256, lambda k0, k1, fc=fc: w_up[l, k0 * 128:k1 * 128, DFF + fc * 128:DFF + (fc + 1) * 128].rearrange(
                    "(c p) n -> p c n", p=128))]) for fc in range(FC)]
            first_tile = ws.tile(**usrcs[0])
            first_tile.finish()
            norm_it = phase_norm(x_mid, B["x_mid"], r0, h2T, Bh2T, g2_d[l:l + 1, :])
            for _ in norm_it:
                pass
            for fc, wb, Bwb, tick in pipelined(ws, usrcs, TG, first=first_tile):
                for tg in range(TG):
                    tick(tg)
                    tks = slice(tg * 512, (tg + 1) * 512)
                    u = k % 3
                    k3 = k
                    k += 1
                    first = (j == 0 and tg == 0)
                    for x in range(2):
                        pb = 3 * x + (k3 % 3)
                        fcx = x * FC + fc
                        a = av[x][u]
                        Ba = Bav[x][u]
                        mm_acc(psum[pb][:, :], Bps[pb], [(wb[:, c, x * 128:(x + 1) * 128], h2T[:, c, tks])
                                                         for c in range(KC)], [Bwb] + Bh2T[4 * tg:4 * tg + 4])
                        P.add("act", lambda e, a=a, pb=pb, fcx=fcx: e.activation(
                            out=a, in_=psum[pb][:, :], func=AF.Identity, scale=cw[:, fcx, 2:3],
                            bias=cb[:, fcx:fcx + 1]), r=[Bps[pb], Bcw], w=[Ba])
                        P.add("dve", lambda e, a=a, pb=pb, fcx=fcx: e.scalar_tensor_tensor(
                            out=a[:, 1:512], in0=psum[pb][:, 0:511], scalar=cw[:, fcx, 1:2], in1=a[:, 1:512],
                            op0=ALU.mult, op1=ALU.add), r=[Bps[pb], Bcw, Ba], w=[Ba])
                        P.add("dve", lambda e, a=a, pb=pb, fcx=fcx: e.scalar_tensor_tensor(
                            out=a[:, 2:512], in0=psum[pb][:, 0:510], scalar=cw[:, fcx, 0:1], in1=a[:, 2:512],
                            op0=ALU.mult, op1=ALU.add), r=[Bps[pb], Bcw, Ba], w=[Ba])
                        if not first:
                            P.add("dve", lambda e, a=a, fcx=fcx: e.scalar_tensor_tensor(
                                out=a[:, 0:1], in0=hist[:, fcx, 1:2], scalar=cw[:, fcx, 1:2], in1=a[:, 0:1],
                                op0=ALU.mult, op1=ALU.add), r=[Bhist, Bcw, Ba], w=[Ba])
                            P.add("dve", lambda e, a=a, fcx=fcx: e.scalar_tensor_tensor(
                                out=a[:, 0:2], in0=hist[:, fcx, 0:2], scalar=cw[:, fcx, 0:1], in1=a[:, 0:2],
                                op0=ALU.mult, op1=ALU.add), r=[Bhist, Bcw, Ba], w=[Ba])
                        P.add("dve", lambda e, pb=pb, fcx=fcx: e.tensor_copy(out=hist[:, fcx, 0:2],
                                                                              in_=psum[pb][:, 510:512]),
                              r=[Bps[pb], Bhist], w=[Bhist])
                    P.add("act", lambda e, u=u: e.activation(out=sg[u], in_=av[0][u], func=AF.Silu),
                          r=[Bav[0][u]], w=[Bsg[u]])
                    o = k % 3
                    P.add("pool", lambda e, u=u, o=o: e.tensor_tensor(out=ast[o], in0=sg[u], in1=av[1][u], op=ALU.mult),
                          r=[Bsg[u], Bav[1][u]], w=[Bast[o]])
                    ti0 = (r0 + tg * 512) // 128
                    P.add("sp", lambda e, o=o, fc=fc, ti0=ti0: e.dma_start(
                        out=actT_s[ti0:ti0 + 4, :, fc, :].rearrange("i p t -> p i t"),
                        in_=ast[o][:, :].rearrange("p (i t) -> p i t", i=4)),
                        r=[Bast[o]], w=[B["actT_s"]], dma=True)
            P.barrier()
            A.reset(KEEP2)
            ws = WStream(FC, 512, name="p5d")
            at = [A.alloc([128, FC, 128], BF16) for _ in range(2)]
            Bat = [P.buf("p5_at%d" % i) for i in range(2)]
            xp = [A.alloc([128, 512], F32) for _ in range(3)]
            Bxp = [P.buf("p5_xp%d" % i) for i in range(3)]
            xo = [A.alloc([128, 512], F32) for _ in range(3)]
            Bxo = [P.buf("p5_xo%d" % i) for i in range(3)]
            dsrcs = [dict(src_fn=lambda k0, k1, n=n: w_down[l, k0 * 128:k1 * 128, n * 512:(n + 1) * 512].rearrange(
                "(c p) n -> p c n", p=128)) for n in range(4)]

            def ld_in(kk):
                n, i = divmod(kk, HT)
                u = kk % 3
                a = kk % 2
                rows = slice(r0 + i * 128, r0 + (i + 1) * 128)
                cols = slice(n * 512, (n + 1) * 512)
                ti = (r0 + i * 128) // 128
                P.add("sp", lambda e, a=a, ti=ti: e.dma_start(out=at[a], in_=actT_s[ti]),
                      r=[B["actT_s"]], w=[Bat[a]], dma=True)
                P.add("sp", lambda e, u=u, rows=rows, cols=cols: e.dma_start(out=xp[u], in_=x_mid[rows, cols]),
                      r=[B["x_mid"]], w=[Bxp[u]], dma=True)

            ld_in(0)
            k = 0
            for n, wb, Bwb, tick in pipelined(ws, dsrcs, HT):
                cols = slice(n * 512, (n + 1) * 512)
                for i in range(HT):
                    tick(i)
                    rows = slice(r0 + i * 128, r0 + (i + 1) * 128)
                    pb = k % 4
                    u = k % 3
                    a = k % 2
                    if k + 1 < 4 * HT:
                        ld_in(k + 1)
                    k += 1
                    mm_acc(psum[pb][:, :], Bps[pb], [(at[a][:, f, :], wb[:, f, :]) for f in range(FC)], [Bat[a], Bwb])
                    P.add("dve", lambda e, u=u, pb=pb: e.tensor_tensor(out=xo[u], in0=psum[pb][:, :], in1=xp[u],
                                                                        op=ALU.add), r=[Bps[pb], Bxp[u]], w=[Bxo[u]])
                    P.add("sp", lambda e, u=u, rows=rows, cols=cols: e.dma_start(out=dst_ap[rows, cols], in_=xo[u]),
                          r=[Bxo[u]], w=[Bdst], dma=True)
            P.barrier()

        eps_t = A.alloc([128, 1], F32)
        one_t = A.alloc([128, 1], F32)
        lnsc_t = A.alloc([128, 1], F32)
        P.add("pool", lambda e: e.memset(eps_t, EPS), w=[B_cst])
        P.add("pool", lambda e: e.memset(one_t, 1.0), w=[B_cst])
        P.add("pool", lambda e: e.memset(lnsc_t, math.log(128 ** -0.5)), w=[B_cst])
        hist = A.alloc([128, 2 * FC, 2], F32)
        Bhist = P.buf("hist")
        KEEP2 = A.off
        P.barrier()

        for l in range(n_layers):
            src_ap, Bsrc = (x_in, B_x_in) if l == 0 else (x_l1, B["x_l1"])
            for j in range(2):
                phase1(l, j, src_ap, Bsrc)
            if stop_after == "p1":
                break
            phase2(l)
            if stop_after == "p2":
                break
            phase3(l)
            if stop_after == "p3":
                break
            for j in range(2):
                phase4(l, j, src_ap, Bsrc)
            if stop_after == "p4":
                break
            last = (l == n_layers - 1)
            for j in range(2):
                phase5(l, j, y_out if last else x_l1, B["y"] if last else B["x_l1"])

        P.barrier()
        P.emit()
    return nc


def _consts():
    c = np.zeros((128, 3, 128), np.float32)
    c[:, 0, :] = np.eye(128, dtype=np.float32)
    c[:, 1, :] = np.triu(np.ones((128, 128), np.float32))
    c[:, 2, :] = 1.0
    return c


def prep_shared(inp):
    f = lambda a: np.ascontiguousarray(np.asarray(a, dtype=np.float32))
    sh = {}
    sh["w_in"] = f(inp["w_in"])
    sh["w_pa"] = f(inp["w_proj_a"])
    sh["w_pm"] = f(inp["w_proj_m"])
    sh["w_out"] = f(inp["w_out"])
    sh["w_up"] = f(inp["w_up"])
    sh["w_down"] = f(inp["w_down"])
    sh["g1"] = f(inp["norm1_g"])
    sh["g2"] = f(inp["norm2_g"])
    gb = np.concatenate([np.asarray(inp["fox_f_bias"]), np.asarray(inp["m_f_bias"]), np.asarray(inp["m_i_bias"])], axis=1)
    sh["gbias"] = f(np.broadcast_to(gb[:, None, :], (NL, 128, 16)))
    sh["qkg"] = f(np.stack([np.asarray(inp["q_norm_g"]), np.asarray(inp["k_norm_g"])], axis=2))
    sh["mng"] = f(np.broadcast_to(np.asarray(inp["m_norm_g"]).reshape(NL, 1, 1024), (NL, 128, 1024)))
    sh["cw"] = f(np.asarray(inp["conv_w"]).reshape(NL, 3, 2 * FC, 128).transpose(0, 3, 2, 1))
    sh["cb"] = f(np.asarray(inp["conv_b"]).reshape(NL, 2 * FC, 128).transpose(0, 2, 1))
    sh["consts"] = _consts()
    return sh


def kernel(**inputs):
    x = np.asarray(inputs["x"], dtype=np.float32)
    Bn, S, _ = x.shape
    sh = prep_shared(inputs)
    nc = build(S=S)
    in_maps = []
    for b in range(Bn):
        m = dict(sh)
        m["x"] = np.ascontiguousarray(x[b])
        in_maps.append(m)
    res = run_bass_kernel_spmd(nc, in_maps, core_ids=list(range(Bn)))
    return np.stack([np.asarray(r["y"]) for r in res.results], axis=0).astype(np.float32)
```

```python
import math
from contextlib import ExitStack

import numpy as np
import concourse.bass as bass
import concourse.mybir as mybir
from concourse.bass_utils import run_bass_kernel_spmd

F32 = mybir.dt.float32
BF16 = mybir.dt.bfloat16
U8 = mybir.dt.uint8
AF = mybir.ActivationFunctionType
ALU = mybir.AluOpType
AX = mybir.AxisListType

D = 2048
DIN = 10256
DFF = 5632
NL = 2
NH = 8
MH = 4
EPS = 1e-6
KC = D // 128
FC = DFF // 128
ENG = ("pe", "act", "dve", "pool", "sp")


class Buf:
    def __init__(self, name):
        self.name = name
        self.writers = []
        self.readers = []
        self.sem = None
        self.nd = 0


class Op:
    pass


class Prog:
    def __init__(self, nc, strict=("act", "dve", "pool")):
        self.nc = nc
        self.ops = {e: [] for e in ENG}
        self.bufs = []
        self.strict = strict

    def buf(self, name):
        for b in self.bufs:
            if b.name == name:
                return b
        b = Buf(name)
        self.bufs.append(b)
        return b

    def add(self, eng, fn, r=(), w=(), dma=False):
        op = Op()
        op.eng, op.fn, op.dma, op.signal = eng, fn, dma, False
        op.deps_eng, op.deps_sem = {}, {}

        def dep(d):
            if d.dma:
                k = id(d.dst)
                v = op.deps_sem.get(k)
                if v is None or v[1] < d.count:
                    op.deps_sem[k] = (d.dst, d.count)
            else:
                if d.eng == eng and eng not in self.strict:
                    return
                if op.deps_eng.get(d.eng, -1) < d.seq:
                    op.deps_eng[d.eng] = d.seq

        for b in r:
            for d in b.writers:
                dep(d)
        for b in w:
            for d in b.readers:
                dep(d)
        for b in r:
            b.readers.append(op)
        for b in w:
            if b.readers:
                b.writers = [op]
                b.readers = []
            else:
                b.writers.append(op)
        op.seq = len(self.ops[eng])
        self.ops[eng].append(op)
        if dma:
            op.dst = w[0]
            op.dst.nd += 1
            op.count = 16 * op.dst.nd
        return op

    def barrier(self):
        lasts = {}
        for e in ENG:
            if e == "sp":
                continue
            for op in reversed(self.ops[e]):
                if op.fn is not None:
                    lasts[e] = op.seq
                    break
        sems = {id(b): (b, 16 * b.nd) for b in self.bufs if b.nd > 0}
        for e in ENG:
            op = Op()
            op.eng, op.fn, op.dma, op.signal = e, None, False, False
            op.deps_eng = {e2: s for e2, s in lasts.items() if e2 != e}
            op.deps_sem = dict(sems)
            op.seq = len(self.ops[e])
            self.ops[e].append(op)
        for b in self.bufs:
            b.writers = []
            b.readers = []

    def emit(self):
        nc = self.nc
        for e in ENG:
            for op in self.ops[e]:
                for pe_, s in op.deps_eng.items():
                    self.ops[pe_][s].signal = True
        for e in ENG:
            c = 0
            for op in self.ops[e]:
                if op.signal:
                    c += 1
                op.sig = c
        with ExitStack() as es:
            esem = {e: es.enter_context(nc.semaphore("S_" + e)) for e in ENG}
            for b in self.bufs:
                if b.nd > 0:
                    b.sem = es.enter_context(nc.semaphore("D_" + b.name))
            block = es.enter_context(nc.Block())

            def run(e, eng):
                waited = {}
                for op in self.ops[e]:
                    for pe_, s in op.deps_eng.items():
                        v = self.ops[pe_][s].sig
                        key = ("e", pe_)
                        if waited.get(key, 0) < v:
                            eng.wait_ge(esem[pe_], v)
                            waited[key] = v
                    for (dst, cnt) in op.deps_sem.values():
                        key = ("d", id(dst))
                        if waited.get(key, 0) < cnt:
                            eng.wait_ge(dst.sem, cnt)
                            waited[key] = cnt
                    if op.fn is not None:
                        inst = op.fn(eng)
                        if op.dma:
                            inst.then_inc(op.dst.sem, 16)
                        elif op.signal:
                            inst.then_inc(esem[e], 1)

            @block.tensor
            def _(eng):
                run("pe", eng)

            @block.scalar
            def _(eng):
                run("act", eng)

            @block.vector
            def _(eng):
                run("dve", eng)

            @block.gpsimd
            def _(eng):
                run("pool", eng)

            @block.sync
            def _(eng):
                run("sp", eng)


class Arena:
    def __init__(self, handle, size):
        self.h, self.size, self.off = handle, size, 0

    def reset(self, keep=0):
        self.off = keep

    def alloc(self, shape, dtype):
        esz = 2 if dtype == BF16 else 4
        n = 1
        for s in shape[1:]:
            n *= s
        nbytes = (n * esz + 63) // 64 * 64
        assert self.off + nbytes <= self.size, ("SBUF arena overflow", self.off, nbytes)
        v = self.h[0:shape[0], self.off:self.off + n * esz].bitcast(dtype)
        self.off += nbytes
        if len(shape) == 3:
            v = v.rearrange("p (a b) -> p a b", a=shape[1])
        elif len(shape) == 4:
            v = v.rearrange("p (a b c) -> p a b c", a=shape[1], b=shape[2])
        return v


def build(S=4096, n_layers=NL, debug=(), stop_after=None):
    nc = bass.Bass("TRN2", target_bir_lowering=False)
    P = Prog(nc)
    NT = S // 128
    T = S // 2
    HT = T // 128
    TG = T // 512

    def dram(name, shape, dt, kind="Internal"):
        if name in debug:
            kind = "ExternalOutput"
        return nc.dram_tensor(name, list(shape), dt, kind=kind).ap()

    x_in = dram("x", [S, D], F32, "ExternalInput")
    w_in = dram("w_in", [NL, D, DIN], F32, "ExternalInput")
    w_pa = dram("w_pa", [NL, 1024, D], F32, "ExternalInput")
    w_pm = dram("w_pm", [NL, 1024, D], F32, "ExternalInput")
    w_out = dram("w_out", [NL, D, D], F32, "ExternalInput")
    w_up = dram("w_up", [NL, D, 2 * DFF], F32, "ExternalInput")
    w_down = dram("w_down", [NL, DFF, D], F32, "ExternalInput")
    g1_d = dram("g1", [NL, D], F32, "ExternalInput")
    g2_d = dram("g2", [NL, D], F32, "ExternalInput")
    gbias_d = dram("gbias", [NL, 128, 16], F32, "ExternalInput")
    qkg_d = dram("qkg", [NL, 128, 2], F32, "ExternalInput")
    mng_d = dram("mng", [NL, 128, 1024], F32, "ExternalInput")
    cw_d = dram("cw", [NL, 128, 2 * FC, 3], F32, "ExternalInput")
    cb_d = dram("cb", [NL, 128, 2 * FC], F32, "ExternalInput")
    consts_d = dram("consts", [128, 3, 128], F32, "ExternalInput")
    flag_d = dram("flag", [128, 1], F32, "ExternalInput")
    y_out = dram("y", [S // 2, D], F32, "ExternalOutput")

    qT_s = dram("qT_s", [NH, 128, S], BF16)
    kT_s = dram("kT_s", [NH, 128, S], BF16)
    vp_s = dram("vp_s", [NH, 128, NT, 129], BF16)
    tot_s = dram("tot_s", [NT, 12], F32)
    mqT_s = dram("mqT_s", [MH, 128, S], BF16)
    mkT_s = dram("mkT_s", [MH, 128, S], BF16)
    mk_s = dram("mk_s", [NT, 128, 512], BF16)
    mv_s = dram("mv_s", [NT, 128, MH, 257], BF16)
    mo_s = dram("mo_s", [NT, 128, 1024], BF16)
    gaT_s = dram("gaT_s", [KC, 128, S], BF16)
    gmT_s = dram("gmT_s", [KC, 128, S], BF16)
    aoT_s = dram("aoT_s", [NH, 128, S], BF16)
    moT_s = dram("moT_s", [NH, 128, S], BF16)
    x_mid = dram("x_mid", [S, D], F32)
    x_l1 = dram("x_l1", [S, D], F32)
    actT_s = dram("actT_s", [S // 128, 128, FC, 128], BF16)

    B_x_in = P.buf("x_in")
    B = {n: P.buf(n) for n in ("qT_s", "kT_s", "vp_s", "tot_s", "mqT_s", "mkT_s", "mk_s", "mv_s",
                               "mo_s", "gaT_s", "gmT_s", "aoT_s", "moT_s", "x_mid", "x_l1", "actT_s", "y")}
    B_w = P.buf("weights")

    with ExitStack() as es:
        ARENA_BYTES = 204 * 1024
        arena_h = es.enter_context(nc.sbuf_tensor("arena", [128, ARENA_BYTES], U8))
        A = Arena(arena_h, ARENA_BYTES)
        psum = [es.enter_context(nc.psum_tensor("ps%d" % i, [128, 512], F32)) for i in range(8)]
        Bps = [P.buf("ps%d" % i) for i in range(8)]

        def ps_bf(i):
            return psum[i][:, :].bitcast(BF16)

        cst = A.alloc([128, 3, 128], F32)
        ident_b = A.alloc([128, 128], BF16)
        ones_b = A.alloc([128, 128], BF16)
        tri_b = A.alloc([128, 128], BF16)
        B_cst = P.buf("cst")
        P.add("sp", lambda e: e.dma_start(out=cst, in_=consts_d), r=[B_w], w=[B_cst], dma=True)
        P.add("dve", lambda e: e.tensor_copy(out=ident_b, in_=cst[:, 0, :]), r=[B_cst], w=[B_cst])
        P.add("dve", lambda e: e.tensor_copy(out=tri_b, in_=cst[:, 1, :]), r=[B_cst], w=[B_cst])
        P.add("dve", lambda e: e.tensor_copy(out=ones_b, in_=cst[:, 2, :]), r=[B_cst], w=[B_cst])
        tri_f = cst[:, 1, :]
        ones_f = cst[:, 2, :]
        KEEP = A.off
        P.barrier()

        class WTile:
            def __init__(self, ws, srcs, ncols):
                self.ws, self.srcs, self.ncols = ws, srcs, ncols
                slot = ws.wi % len(ws.wb)
                ws.wi += 1
                self.wb, self.Bwb = ws.wb[slot], ws.Bwb[slot]
                self.pieces = [(k0, min(ws.nk, k0 + ws.kp)) for k0 in range(0, ws.nk, ws.kp)]
                self.stg_of = {}
                self.nd = 0
                self.ncst = 0

            def _dma(self):
                ws = self.ws
                k0, k1 = self.pieces[self.nd]
                si = ws.si % len(ws.stg)
                ws.si += 1
                stg, Bstg = ws.stg[si], ws.Bstg[si]
                self.stg_of[self.nd] = (stg, Bstg)
                self.nd += 1
                for (c0, c1, fn) in self.srcs:
                    P.add("sp", lambda e, stg=stg, k0=k0, k1=k1, c0=c0, c1=c1, fn=fn: e.dma_start(
                        out=stg[:, 0:k1 - k0, c0:c1], in_=fn(k0, k1)), r=[B_w], w=[Bstg], dma=True)

            def step(self):
                if self.ncst >= len(self.pieces):
                    return False
                while self.nd < min(len(self.pieces), self.ncst + 2):
                    self._dma()
                k0, k1 = self.pieces[self.ncst]
                stg, Bstg = self.stg_of.pop(self.ncst)
                self.ncst += 1
                wb, ncols = self.wb, self.ncols
                P.add("pool", lambda e, stg=stg, k0=k0, k1=k1: e.tensor_copy(
                    out=wb[:, k0:k1, 0:ncols], in_=stg[:, 0:k1 - k0, 0:ncols]), r=[Bstg], w=[self.Bwb])
                return True

            def finish(self):
                while self.step():
                    pass
                return self.wb, self.Bwb

        class WStream:
            def __init__(self, nk, ncols, nslots=2, kp=4, nstg=4, name="w"):
                self.nk, self.ncols, self.kp = nk, ncols, kp
                self.stg = [A.alloc([128, kp, ncols], F32) for _ in range(nstg)]
                self.Bstg = [P.buf("%s_stg%d" % (name, i)) for i in range(nstg)]
                self.wb = [A.alloc([128, nk, ncols], BF16) for _ in range(nslots)]
                self.Bwb = [P.buf("%s_wb%d" % (name, i)) for i in range(nslots)]
                self.si = 0
                self.wi = 0

            def tile(self, src_fn=None, srcs=None, ncols=None):
                ncols = ncols or self.ncols
                if srcs is None:
                    srcs = [(0, ncols, src_fn)]
                return WTile(self, srcs, ncols)

        def pipelined(ws, tiles_srcs, nunits, first=None):
            cur = first
            if cur is None:
                cur = ws.tile(**tiles_srcs[0])
                cur.finish()
            for n in range(len(tiles_srcs)):
                nxt = ws.tile(**tiles_srcs[n + 1]) if n + 1 < len(tiles_srcs) else None
                npieces = len(nxt.pieces) if nxt is not None else 0
                state = {"done": 0}

                def tick(u, nxt=nxt, npieces=npieces, state=state):
                    if nxt is None:
                        return
                    want = min(npieces, (u + 1) * npieces // nunits + 1)
                    while state["done"] < want and nxt.step():
                        state["done"] += 1
                yield n, cur.wb, cur.Bwb, tick
                if nxt is not None:
                    nxt.finish()
                cur = nxt

        def mm_acc(ps_ap, Bp, pairs, rbufs):
            n = len(pairs)
            for i, (l, r_) in enumerate(pairs):
                P.add("pe", lambda e, l=l, r_=r_, i=i: e.matmul(ps_ap, l, r_, start=(i == 0), stop=(i == n - 1)),
                      r=rbufs, w=[Bp])

        def phase_norm(src_ap, Bsrc, r0, hT, BhT, g_row):
            gbc = A.alloc([128, D], F32)
            Bgbc = P.buf("gbc")
            P.add("sp", lambda e: e.dma_start(out=gbc, in_=g_row.partition_broadcast(128)), r=[B_w], w=[Bgbc], dma=True)
            xt = [A.alloc([128, D], F32) for _ in range(2)]
            Bxt = [P.buf("xt%d" % i) for i in range(2)]
            hn = [A.alloc([128, D], BF16) for _ in range(2)]
            Bhn = [P.buf("hn%d" % i) for i in range(2)]
            junk = A.alloc([128, D], BF16)
            Bjunk = P.buf("junk")
            st = A.alloc([128, 8], F32)
            Bst = P.buf("st")
            for i in range(HT):
                s = i % 2
                rows = slice(r0 + i * 128, r0 + (i + 1) * 128)
                P.add("sp", lambda e, s=s, rows=rows: e.dma_start(out=xt[s], in_=src_ap[rows, :]),
                      r=[Bsrc], w=[Bxt[s]], dma=True)
                P.add("act", lambda e, s=s: e.activation(out=junk, in_=xt[s], func=AF.Square, accum_out=st[:, 0:1]),
                      r=[Bxt[s]], w=[Bjunk, Bst])
                P.add("act", lambda e: e.activation(out=st[:, 1:2], in_=st[:, 0:1], func=AF.Sqrt,
                                                    scale=1.0 / D, bias=eps_t[:, 0:1]), r=[Bst], w=[Bst])
                P.add("dve", lambda e: e.reciprocal(out=st[:, 2:3], in_=st[:, 1:2]), r=[Bst], w=[Bst])
                P.add("dve", lambda e, s=s: e.scalar_tensor_tensor(out=hn[s], in0=xt[s], scalar=st[:, 2:3], in1=gbc,
                                                                   op0=ALU.mult, op1=ALU.mult),
                      r=[Bst, Bxt[s], Bgbc], w=[Bhn[s]])
                for q in range(4):
                    pb = 4 + (q % 2)
                    pv = ps_bf(pb)
                    for c4 in range(4):
                        c = q * 4 + c4
                        P.add("pe", lambda e, s=s, c=c, c4=c4, pv=pv: e.transpose(
                            out=pv[:, c4 * 128:(c4 + 1) * 128], in_=hn[s][:, c * 128:(c + 1) * 128],
                            identity=ident_b), r=[Bhn[s], B_cst], w=[Bps[pb]])
                    ce = "act" if q % 2 == 0 else "dve"
                    dst = hT[:, q * 4:(q + 1) * 4, i * 128:(i + 1) * 128]
                    srcv = pv[:, 0:512].rearrange("p (a b) -> p a b", a=4)
                    if ce == "act":
                        P.add("act", lambda e, dst=dst, srcv=srcv: e.copy(out=dst, in_=srcv), r=[Bps[pb]], w=[BhT[i]])
                    else:
                        P.add("dve", lambda e, dst=dst, srcv=srcv: e.tensor_copy(out=dst, in_=srcv),
                              r=[Bps[pb]], w=[BhT[i]])
                yield

        eps_t = None

        def phase1(l, j, src_ap, Bsrc):
            A.reset(KEEP2)
            r0 = j * T
            hT = A.alloc([128, KC, T], BF16)
            BhT = [P.buf("hT%d" % i) for i in range(HT)]
            mark = A.off
            gb = A.alloc([128, 16], F32)
            qkg = A.alloc([128, 4], F32)
            Bpar = P.buf("par1")
            P.add("sp", lambda e: e.dma_start(out=gb, in_=gbias_d[l]), r=[B_w], w=[Bpar], dma=True)
            P.add("sp", lambda e: e.dma_start(out=qkg[:, 0:2], in_=qkg_d[l]), r=[B_w], w=[Bpar], dma=True)
            P.add("dve", lambda e: e.scalar_tensor_tensor(out=qkg[:, 2:3], in0=qkg[:, 0:1], scalar=float(128 ** -0.5),
                                                          in1=qkg[:, 1:2], op0=ALU.mult, op1=ALU.mult),
                  r=[Bpar], w=[Bpar])
            E_all = A.alloc([128, HT, 8], F32)
            QS_all = A.alloc([128, HT, 4], F32)
            KS_all = A.alloc([128, HT, 4], F32)
            Bgate = P.buf("gates")
            ws = WStream(KC, 512, name="win")
            wsm_stg = A.alloc([128, KC, 16], F32)
            wsm = A.alloc([128, KC, 16], BF16)
            Bwsm_stg, Bwsm = P.buf("wsm_stg"), P.buf("wsm")
            NST = 3
            st_bf = [A.alloc([128, 4, 132], BF16) for _ in range(NST)]
            Bst_bf = [P.buf("st_bf%d" % i) for i in range(NST)]
            st2_bf = [A.alloc([128, 512], BF16) for _ in range(NST)]
            Bst2_bf = [P.buf("st2_bf%d" % i) for i in range(NST)]
            mvst = [A.alloc([128, 2, 257], BF16) for _ in range(2)]
            Bmvst = [P.buf("mvst%d" % i) for i in range(2)]
            tmp_f = [A.alloc([128, 512], F32) for _ in range(2)]
            Btmp_f = [P.buf("tmp_f%d" % i) for i in range(2)]
            sm = A.alloc([128, 64], F32)
            Bsm = P.buf("sm")
            tsb = A.alloc([128, 16], F32)
            Btsb = P.buf("tsb")
            for s in range(2):
                P.add("pool", lambda e, s=s: e.memset(mvst[s][:, :, 256:257], 1.0), w=[Bmvst[s]])
            cnt = {"st": 0, "st2": 0, "mv": 0, "tf": 0, "ps": 0}

            def nxt(k, n):
                v = cnt[k] % n
                cnt[k] += 1
                return v

            groups = []
            groups += [("fv", 2048 + 512 * g, g) for g in range(2)]
            groups += [("mq", 3080, 0), ("mk", 3592, 0)]
            groups += [("mv", 4104 + 512 * g, g) for g in range(2)]
            groups += [("mo", 5128 + 512 * g, g) for g in range(2)]
            groups += [("fq", 512 * g, g) for g in range(2)]
            groups += [("fk", 1024 + 512 * g, g) for g in range(2)]
            groups += [("ga", 6160 + 512 * g, g) for g in range(4)]
            groups += [("gm", 8208 + 512 * g, g) for g in range(4)]

            def sm_src(c0, c1, lo, hi):
                return w_in[l, :, lo:hi].rearrange("(c p) n -> p c n", p=128)
            with nc.allow_non_contiguous_dma(reason="tiny gate columns"):
                for (dlo, lo, hi) in ((0, 3072, 3080), (8, 6156, 6160), (12, 6152, 6156)):
                    P.add("sp", lambda e, dlo=dlo, lo=lo, hi=hi: e.dma_start(
                        out=wsm_stg[:, :, dlo:dlo + hi - lo],
                        in_=w_in[l, :, lo:hi].rearrange("(c p) n -> p c n", p=128)),
                        r=[B_w], w=[Bwsm_stg], dma=True)
            for k in range(KC):
                P.add("dve", lambda e, k=k: e.tensor_copy(out=wsm[:, k, :], in_=wsm_stg[:, k, :]),
                      r=[Bwsm_stg], w=[Bwsm])
            gsrcs = [dict(src_fn=lambda k0, k1, c0=c0: w_in[l, k0 * 128:k1 * 128, c0:c0 + 512].rearrange(
                "(c p) n -> p c n", p=128)) for (kind, c0, g) in groups]
            first_tile = ws.tile(**gsrcs[0])
            first_tile.finish()
            norm_it = phase_norm(src_ap, Bsrc, r0, hT, BhT, g1_d[l:l + 1, :])
            zb = sm[:, 0:16]
            e1 = sm[:, 16:32]
            sp_ = sm[:, 32:48]
            t4 = sm[:, 48:52]

            def smA(i):
                tok = slice(i * 128, (i + 1) * 128)
                mm_acc(psum[7][:, 0:16], Bps[7], [(hT[:, c, tok], wsm[:, c, :]) for c in range(KC)], [BhT[i], Bwsm])
                P.add("dve", lambda e: e.tensor_tensor(out=zb, in0=psum[7][:, 0:16], in1=gb, op=ALU.add),
                      r=[Bps[7], Bpar], w=[Bsm])
                P.add("act", lambda e: e.activation(out=e1, in_=zb, func=AF.Exp, scale=-1.0), r=[Bsm], w=[Bsm])
                P.add("act", lambda e: e.activation(out=sp_, in_=e1, func=AF.Ln, bias=one_t[:, 0:1]), r=[Bsm], w=[Bsm])

            def smB(i):
                gi = j * HT + i
                P.add("pe", lambda e: e.matmul(psum[7][:, 32:44], tri_f, sp_[:, 0:12], start=True, stop=True),
                      r=[Bsm, B_cst], w=[Bps[7]])
                P.add("pe", lambda e: e.matmul(psum[7][:, 64:76], ones_f, sp_[:, 0:12], start=True, stop=True),
                      r=[Bsm, B_cst], w=[Bps[7]])
                cum = psum[7][:, 32:44]
                P.add("act", lambda e, i=i: e.activation(out=E_all[:, i, :], in_=cum[:, 0:8], func=AF.Exp),
                      r=[Bps[7]], w=[Bgate])
                P.add("act", lambda e, i=i: e.activation(out=QS_all[:, i, :], in_=cum[:, 8:12], func=AF.Exp,
                                                         scale=-1.0, bias=lnsc_t[:, 0:1]), r=[Bps[7]], w=[Bgate])
                P.add("dve", lambda e: e.tensor_tensor(out=t4, in0=cum[:, 8:12], in1=zb[:, 12:16], op=ALU.add),
                      r=[Bps[7], Bsm], w=[Bsm])
                P.add("act", lambda e, i=i: e.activation(out=KS_all[:, i, :], in_=t4, func=AF.Exp),
                      r=[Bsm], w=[Bgate])
                P.add("act", lambda e: e.copy(out=tsb[:, 0:12], in_=psum[7][:, 64:76]), r=[Bps[7]], w=[Btsb])
                P.add("sp", lambda e, gi=gi: e.dma_start(out=tot_s[gi:gi + 1, :], in_=tsb[0:1, 0:12]),
                      r=[Btsb], w=[B["tot_s"]], dma=True)

            for _ in norm_it:
                pass
            for i in range(HT):
                smA(i)
                smB(i)
            for gn, wb, Bwb, tick in pipelined(ws, gsrcs, 16, first=first_tile):
                kind, c0, g = groups[gn]
                if kind in ("fv", "mq", "mk", "mv", "mo"):
                  def tm_unit(i, wb=wb, Bwb=Bwb, kind=kind, g=g):
                    if True:
                        gi = j * HT + i
                        tok = slice(i * 128, (i + 1) * 128)
                        pb = nxt("ps", 4)
                        ps = psum[pb]
                        mm_acc(ps[:, :], Bps[pb], [(hT[:, c, tok], wb[:, c, :]) for c in range(KC)], [BhT[i], Bwb])
                        if kind == "fv":
                            s = nxt("st", NST)
                            h0 = g * 4
                            P.add("dve", lambda e, s=s, ps=ps, i=i, h0=h0: e.tensor_tensor(
                                out=st_bf[s][:, :, 0:128], in0=ps[:, :].rearrange("p (h d) -> p h d", h=4),
                                in1=E_all[:, i, h0:h0 + 4].unsqueeze(2).to_broadcast([128, 4, 128]), op=ALU.mult),
                                r=[Bps[pb], Bgate], w=[Bst_bf[s]])
                            P.add("dve", lambda e, s=s, i=i, h0=h0: e.tensor_copy(
                                out=st_bf[s][:, :, 128:129], in_=E_all[:, i, h0:h0 + 4].unsqueeze(2)),
                                r=[Bgate], w=[Bst_bf[s]])
                            P.add("sp", lambda e, s=s, h0=h0, gi=gi: e.dma_start(
                                out=vp_s[h0:h0 + 4, :, gi, :].rearrange("h p c -> p h c"),
                                in_=st_bf[s][:, :, 0:129]), r=[Bst_bf[s]], w=[B["vp_s"]], dma=True)
                        elif kind in ("mq", "mk"):
                            s = nxt("st2", NST)
                            SC = QS_all if kind == "mq" else KS_all
                            P.add("dve", lambda e, s=s, ps=ps, i=i, SC=SC: e.tensor_tensor(
                                out=st2_bf[s][:, :].rearrange("p (h d) -> p h d", h=4),
                                in0=ps[:, :].rearrange("p (h d) -> p h d", h=4),
                                in1=SC[:, i, 0:4].unsqueeze(2).to_broadcast([128, 4, 128]), op=ALU.mult),
                                r=[Bps[pb], Bgate], w=[Bst2_bf[s]])
                            if kind == "mk":
                                P.add("sp", lambda e, s=s, gi=gi: e.dma_start(out=mk_s[gi], in_=st2_bf[s]),
                                      r=[Bst2_bf[s]], w=[B["mk_s"]], dma=True)
                            tb = 4 + (i % 2)
                            pv = ps_bf(tb)
                            for hh in range(4):
                                P.add("pe", lambda e, s=s, hh=hh, pv=pv: e.transpose(
                                    out=pv[:, hh * 128:(hh + 1) * 128], in_=st2_bf[s][:, hh * 128:(hh + 1) * 128],
                                    identity=ident_b), r=[Bst2_bf[s], B_cst], w=[Bps[tb]])
                            s2 = nxt("st2", NST)
                            P.add("act", lambda e, s2=s2, pv=pv: e.copy(out=st2_bf[s2], in_=pv[:, 0:512]),
                                  r=[Bps[tb]], w=[Bst2_bf[s2]])
                            dstT = mqT_s if kind == "mq" else mkT_s
                            Bd = B["mqT_s"] if kind == "mq" else B["mkT_s"]
                            P.add("sp", lambda e, s2=s2, dstT=dstT, gi=gi: e.dma_start(
                                out=dstT[:, :, gi * 128:(gi + 1) * 128].rearrange("h p t -> p h t"),
                                in_=st2_bf[s2][:, :].rearrange("p (h t) -> p h t", h=4)),
                                r=[Bst2_bf[s2]], w=[Bd], dma=True)
                        elif kind == "mv":
                            s = nxt("mv", 2)
                            P.add("act", lambda e, s=s, ps=ps: e.copy(
                                out=mvst[s][:, :, 0:256], in_=ps[:, :].rearrange("p (h d) -> p h d", h=2)),
                                r=[Bps[pb]], w=[Bmvst[s]])
                            P.add("sp", lambda e, s=s, gi=gi, g=g: e.dma_start(
                                out=mv_s[gi, :, 2 * g:2 * g + 2, :], in_=mvst[s]),
                                r=[Bmvst[s]], w=[B["mv_s"]], dma=True)
                        else:
                            s = nxt("st2", NST)
                            P.add("act", lambda e, s=s, ps=ps: e.activation(out=st2_bf[s], in_=ps[:, :],
                                                                            func=AF.Sigmoid),
                                  r=[Bps[pb]], w=[Bst2_bf[s]])
                            P.add("sp", lambda e, s=s, gi=gi, g=g: e.dma_start(
                                out=mo_s[gi, :, 512 * g:512 * (g + 1)], in_=st2_bf[s]),
                                r=[Bst2_bf[s]], w=[B["mo_s"]], dma=True)
                  for i in range(HT):
                      tick(i)
                      tm_unit(i)
                else:
                    for q in range(4):
                        ch = g * 4 + q
                        for tg in range(TG):
                            tick(q * TG + tg)
                            tks = slice(tg * 512, (tg + 1) * 512)
                            gt0 = r0 + tg * 512
                            pb = nxt("ps", 4)
                            ps = psum[pb]
                            mm_acc(ps[:, :], Bps[pb], [(wb[:, c, q * 128:(q + 1) * 128], hT[:, c, tks])
                                                       for c in range(KC)], BhT[4 * tg:4 * tg + 4] + [Bwb])
                            if kind in ("ga", "gm"):
                                s = nxt("st2", NST)
                                P.add("act", lambda e, s=s, ps=ps: e.activation(out=st2_bf[s], in_=ps[:, :],
                                                                                func=AF.Sigmoid),
                                      r=[Bps[pb]], w=[Bst2_bf[s]])
                                dstT = gaT_s if kind == "ga" else gmT_s
                                Bd = B["gaT_s"] if kind == "ga" else B["gmT_s"]
                                P.add("sp", lambda e, s=s, dstT=dstT, ch=ch, gt0=gt0: e.dma_start(
                                    out=dstT[ch, :, gt0:gt0 + 512], in_=st2_bf[s]),
                                    r=[Bst2_bf[s]], w=[Bd], dma=True)
                            else:
                                s = nxt("st2", NST)
                                tf = nxt("tf", 2)
                                P.add("act", lambda e, s=s, ps=ps: e.activation(out=st2_bf[s], in_=ps[:, :],
                                                                                func=AF.Square),
                                      r=[Bps[pb]], w=[Bst2_bf[s]])
                                P.add("pe", lambda e, s=s: e.matmul(psum[6][:, :], ones_b, st2_bf[s],
                                                                    start=True, stop=True),
                                      r=[Bst2_bf[s], B_cst], w=[Bps[6]])
                                P.add("act", lambda e, tf=tf: e.activation(out=tmp_f[tf], in_=psum[6][:, :],
                                                                           func=AF.Sqrt, scale=1.0 / 128,
                                                                           bias=eps_t[:, 0:1]),
                                      r=[Bps[6]], w=[Btmp_f[tf]])
                                P.add("dve", lambda e, tf=tf: e.reciprocal(out=tmp_f[tf], in_=tmp_f[tf]),
                                      r=[Btmp_f[tf]], w=[Btmp_f[tf]])
                                s2 = nxt("st2", NST)
                                if kind == "fq":
                                    P.add("dve", lambda e, s2=s2, ps=ps, tf=tf: e.scalar_tensor_tensor(
                                        out=st2_bf[s2], in0=ps[:, :], scalar=qkg[:, 2:3], in1=tmp_f[tf],
                                        op0=ALU.mult, op1=ALU.mult), r=[Bps[pb], Btmp_f[tf], Bpar], w=[Bst2_bf[s2]])
                                else:
                                    P.add("dve", lambda e, s2=s2, ps=ps, tf=tf: e.tensor_tensor(
                                        out=st2_bf[s2], in0=ps[:, :], in1=tmp_f[tf], op=ALU.mult),
                                        r=[Bps[pb], Btmp_f[tf]], w=[Bst2_bf[s2]])
                                dstT = qT_s if kind == "fq" else kT_s
                                Bd = B["qT_s"] if kind == "fq" else B["kT_s"]
                                P.add("sp", lambda e, s2=s2, dstT=dstT, ch=ch, gt0=gt0: e.dma_start(
                                    out=dstT[ch, :, gt0:gt0 + 512], in_=st2_bf[s2]),
                                    r=[Bst2_bf[s2]], w=[Bd], dma=True)
            P.barrier()


        def phase2(l):
            A.reset(KEEP2)
            QG = S // 512
            totb = A.alloc([128, NT, 12], F32)
            fcn = A.alloc([128, NT, 8], F32)
            Btot = P.buf("totb")
            qkg = A.alloc([128, 4], F32)
            bnd = A.alloc([128, 4], F32)
            Bpar = P.buf("par2")
            P.add("sp", lambda e: e.dma_start(out=totb, in_=tot_s.partition_broadcast(128)),
                  r=[B["tot_s"]], w=[Btot], dma=True)
            P.add("sp", lambda e: e.dma_start(out=qkg[:, 0:2], in_=qkg_d[l]), r=[B_w], w=[Bpar], dma=True)
            P.add("dve", lambda e: e.scalar_tensor_tensor(out=qkg[:, 2:3], in0=qkg[:, 0:1], scalar=float(128 ** -0.5),
                                                          in1=qkg[:, 1:2], op0=ALU.mult, op1=ALU.mult),
                  r=[Bpar], w=[Bpar])
            P.add("pe", lambda e: e.transpose(out=psum[6][0:1, 0:128], in_=qkg[:, 2:3], identity=cst[:, 0, :]),
                  r=[Bpar, B_cst], w=[Bps[6]])
            P.add("dve", lambda e: e.tensor_reduce(out=bnd[0:1, 0:1], in_=psum[6][0:1, 0:128], axis=AX.X, op=ALU.max,
                                                   apply_absolute_value=True), r=[Bps[6]], w=[Bpar])
            P.add("dve", lambda e: e.tensor_scalar(out=bnd[0:1, 1:2], in0=bnd[0:1, 0:1], scalar1=128.0, scalar2=None,
                                                   op0=ALU.mult), r=[Bpar], w=[Bpar])
            P.add("pe", lambda e: e.matmul(psum[6][:, 128:129], ones_f[0:1, :], bnd[0:1, 1:2], start=True, stop=True),
                  r=[Bpar, B_cst], w=[Bps[6]])
            P.add("dve", lambda e: e.tensor_copy(out=bnd[:, 2:3], in_=psum[6][:, 128:129]), r=[Bps[6]], w=[Bpar])
            P.add("pool", lambda e: e.memset(fcn[:, 0, :], 0.0), w=[Btot])
            for i in range(1, NT):
                P.add("dve", lambda e, i=i: e.tensor_tensor(out=fcn[:, i, :], in0=fcn[:, i - 1, :],
                                                            in1=totb[:, i - 1, 0:8], op=ALU.add), r=[Btot], w=[Btot])
            NS = 2
            kT = [A.alloc([128, S], BF16) for _ in range(NS)]
            qT = [A.alloc([128, S], BF16) for _ in range(NS)]
            vp = [A.alloc([128, NT, 129], BF16) for _ in range(NS)]
            vpm = [A.alloc([128, NT // 2, 129], BF16) for _ in range(NS)]
            Bvpm = [P.buf("a_vpm%d" % i) for i in range(NS)]
            Dt = [A.alloc([128, NT, NT], F32) for _ in range(NS)]
            Dhi = [A.alloc([128, NT, NT], BF16) for _ in range(NS)]
            Dlo = [A.alloc([128, NT, NT], F32) for _ in range(NS)]
            DX = [A.alloc([128, NT, NT], BF16) for _ in range(NS)]
            BkT = [P.buf("a_kT%d" % i) for i in range(NS)]
            BqT = [P.buf("a_qT%d" % i) for i in range(NS)]
            Bvp = [P.buf("a_vp%d" % i) for i in range(NS)]
            BDt = [P.buf("a_D%d" % i) for i in range(NS)]
            NPT = 3
            pt = [A.alloc([128, 512], BF16) for _ in range(NPT)]
            Bpt = [P.buf("a_pt%d" % i) for i in range(NPT)]
            ao = [A.alloc([128, 512], BF16) for _ in range(2)]
            Bao = [P.buf("a_ao%d" % i) for i in range(2)]
            aoT = [A.alloc([128, 512], BF16) for _ in range(2)]
            BaoT = [P.buf("a_aoT%d" % i) for i in range(2)]
            rc = A.alloc([128, 8], F32)
            Brc = P.buf("a_rc")
            ptc = 0
            stc = 0
            for h in range(NH):
                s = h % NS
                P.add("sp", lambda e, s=s, h=h: e.dma_start(out=kT[s], in_=kT_s[h]), r=[B["kT_s"]], w=[BkT[s]], dma=True)
                P.add("sp", lambda e, s=s, h=h: e.dma_start(out=qT[s], in_=qT_s[h]), r=[B["qT_s"]], w=[BqT[s]], dma=True)
                P.add("sp", lambda e, s=s, h=h: e.dma_start(out=vp[s], in_=vp_s[h]), r=[B["vp_s"]], w=[Bvp[s]], dma=True)
                P.add("dve", lambda e, s=s: e.tensor_scalar(out=vpm[s], in0=vp[s][:, 0:NT // 2, :], scalar1=flg[:, 0:1],
                                                            scalar2=None, op0=ALU.mult),
                      r=[Bvp[s], B_cst], w=[Bvpm[s]])
                P.add("dve", lambda e, s=s, h=h: e.scalar_tensor_tensor(
                    out=Dt[s], in0=fcn[:, :, h].unsqueeze(1).to_broadcast([128, NT, NT]), scalar=bnd[:, 2:3],
                    in1=fcn[:, :, h].unsqueeze(2).to_broadcast([128, NT, NT]), op0=ALU.subtract, op1=ALU.subtract),
                    r=[Btot, Bpar], w=[BDt[s]])
                P.add("dve", lambda e, s=s: e.tensor_copy(out=Dhi[s], in_=Dt[s]), r=[BDt[s]], w=[BDt[s]])
                P.add("dve", lambda e, s=s: e.tensor_tensor(out=Dlo[s], in0=Dt[s], in1=Dhi[s], op=ALU.subtract),
                      r=[BDt[s]], w=[BDt[s]])
                P.add("dve", lambda e, s=s: e.tensor_scalar(out=Dlo[s], in0=Dlo[s], scalar1=cst[:, 0, 1:2],
                                                            scalar2=None, op0=ALU.mult), r=[BDt[s], B_cst], w=[BDt[s]])
                P.add("dve", lambda e, s=s: e.scalar_tensor_tensor(
                    out=DX[s], in0=Dhi[s], scalar=cst[:, 0, 0:1], in1=Dlo[s], op0=ALU.mult, op1=ALU.add),
                    r=[BDt[s], B_cst], w=[BDt[s]])
                steps = [(g, j) for g in range(QG) for j in range(4 * g + 4)]

                def emit_qk(st):
                    g, j = steps[st]
                    i0 = 4 * g
                    ilo = max(i0, j)
                    ncol = (i0 + 4 - ilo) * 128
                    qlo = ilo * 128
                    sb = 4 + (st % 2)
                    P.add("pe", lambda e, s=s, j=j, sb=sb, qlo=qlo, ncol=ncol: e.matmul(
                        psum[sb][:, 0:ncol], kT[s][:, j * 128:(j + 1) * 128], qT[s][:, qlo:qlo + ncol],
                        start=True, stop=False), r=[BkT[s], BqT[s]], w=[Bps[sb]])
                    nseg = i0 + 4 - ilo
                    P.add("pe", lambda e, s=s, j=j, sb=sb, ilo=ilo, nseg=nseg, ncol=ncol: e.matmul(
                        psum[sb][:, 0:ncol], ones_b,
                        DX[s][:, ilo:ilo + nseg, j:j + 1].to_broadcast([128, nseg, 128]),
                        start=False, stop=True), r=[BDt[s], B_cst], w=[Bps[sb]])

                emit_qk(0)
                for st, (g, j) in enumerate(steps):
                    if True:
                        i0 = 4 * g
                        ilo = max(i0, j)
                        sb = 4 + (st % 2)
                        p = ptc % NPT
                        ptc += 1
                        if st + 1 < len(steps):
                            emit_qk(st + 1)
                        ncol = (i0 + 4 - ilo) * 128
                        P.add("act", lambda e, p=p, sb=sb, ncol=ncol: e.activation(
                            out=pt[p][:, 0:ncol], in_=psum[sb][:, 0:ncol], func=AF.Exp),
                            r=[Bps[sb]], w=[Bpt[p]])
                        for i in range(ilo, i0 + 4):
                            c0 = (i - ilo) * 128
                            if i == j:
                                P.add("dve", lambda e, p=p, c0=c0: e.tensor_tensor(
                                    out=pt[p][:, c0:c0 + 128], in0=pt[p][:, c0:c0 + 128], in1=tri_b, op=ALU.mult),
                                    r=[Bpt[p], B_cst], w=[Bpt[p]])
                        for i in range(ilo, i0 + 4):
                            c0 = (i - ilo) * 128
                            ab = i - i0
                            msk = (i >= NT // 2 and j < NT // 2)
                            vsrc = vpm[s] if msk else vp[s]
                            Bv = Bvpm[s] if msk else Bvp[s]
                            P.add("pe", lambda e, vsrc=vsrc, p=p, c0=c0, ab=ab, i=i, j=j: e.matmul(
                                psum[ab][:, 0:129], pt[p][:, c0:c0 + 128], vsrc[:, j, :],
                                start=(j == 0), stop=(j == i)), r=[Bpt[p], Bv], w=[Bps[ab]])
                    if j != 4 * g + 3:
                        continue
                    a = stc % 2
                    stc += 1
                    for m in range(4):
                        P.add("dve", lambda e, m=m: e.reciprocal(out=rc[:, m:m + 1], in_=psum[m][:, 128:129]),
                              r=[Bps[m]], w=[Brc])
                        P.add("dve", lambda e, m=m, a=a: e.tensor_scalar(
                            out=ao[a][:, m * 128:(m + 1) * 128], in0=psum[m][:, 0:128], scalar1=rc[:, m:m + 1],
                            scalar2=None, op0=ALU.mult), r=[Bps[m], Brc], w=[Bao[a]])
                    pv = ps_bf(6)
                    for m in range(4):
                        P.add("pe", lambda e, m=m, a=a, pv=pv: e.transpose(
                            out=pv[:, m * 128:(m + 1) * 128], in_=ao[a][:, m * 128:(m + 1) * 128], identity=ident_b),
                            r=[Bao[a], B_cst], w=[Bps[6]])
                    P.add("dve", lambda e, a=a, pv=pv: e.tensor_copy(out=aoT[a], in_=pv[:, 0:512]),
                          r=[Bps[6]], w=[BaoT[a]])
                    P.add("sp", lambda e, a=a, h=h, g=g: e.dma_start(out=aoT_s[h, :, g * 512:(g + 1) * 512],
                                                                     in_=aoT[a]),
                          r=[BaoT[a]], w=[B["aoT_s"]], dma=True)
            P.barrier()

        def phase3(l):
            A.reset(KEEP2)
            totb = A.alloc([128, NT, 12], F32)
            eL = A.alloc([128, NT, 4], F32)
            Btot = P.buf("m_tot")
            gbc = A.alloc([128, 1024], F32)
            Bg = P.buf("m_g")
            P.add("sp", lambda e: e.dma_start(out=totb, in_=tot_s.partition_broadcast(128)),
                  r=[B["tot_s"]], w=[Btot], dma=True)
            P.add("sp", lambda e: e.dma_start(out=gbc, in_=mng_d[l]), r=[B_w], w=[Bg], dma=True)
            P.add("act", lambda e: e.activation(out=eL, in_=totb[:, :, 8:12], func=AF.Exp, scale=-1.0),
                  r=[Btot], w=[Btot])
            C32 = A.alloc([128, MH, 257], F32)
            Cb = A.alloc([128, MH, 257], BF16)
            BC32, BCb = P.buf("m_C32"), P.buf("m_Cb")
            NS = 2
            qT4 = [A.alloc([128, MH, 128], BF16) for _ in range(NS)]
            kT4 = [A.alloc([128, MH, 128], BF16) for _ in range(NS)]
            k4 = [A.alloc([128, 512], BF16) for _ in range(NS)]
            v4 = [A.alloc([128, MH, 257], BF16) for _ in range(NS)]
            o4 = [A.alloc([128, 1024], BF16) for _ in range(NS)]
            Bin = [P.buf("m_in%d" % i) for i in range(NS)]
            wt = [A.alloc([128, MH, 128], BF16) for _ in range(2)]
            Bwt = [P.buf("m_wt%d" % i) for i in range(2)]
            tmpU = [A.alloc([128, 257], F32) for _ in range(2)]
            BtmpU = [P.buf("m_tmpU%d" % i) for i in range(2)]
            sm = A.alloc([128, 16], F32)
            Bsm = P.buf("m_sm")
            junk = A.alloc([128, 256], BF16)
            Bjunk = P.buf("m_junk")
            mo = [A.alloc([128, 1024], BF16) for _ in range(2)]
            Bmo = [P.buf("m_mo%d" % i) for i in range(2)]
            moT = [A.alloc([128, 8, 128], BF16) for _ in range(2)]
            BmoT = [P.buf("m_moT%d" % i) for i in range(2)]
            uc = 0
            BC32h = [P.buf("m_C32_%d" % h) for h in range(MH)]
            BCbh = [P.buf("m_Cb_%d" % h) for h in range(MH)]
            for h in range(MH):
                P.add("pool", lambda e, h=h: e.memset(C32[:, h, :], 0.0), w=[BC32h[h]])
                P.add("pool", lambda e, h=h: e.memset(Cb[:, h, :], 0.0), w=[BCbh[h]])
            dn = A.alloc([128, 32], F32)
            Bdn, Bss = P.buf("m_dn"), P.buf("m_ss")
            t2 = [A.alloc([128, 256], F32) for _ in range(4)]
            Bt2 = [P.buf("m_t2%d" % i) for i in range(4)]
            pending = []
            for c in range(NT):
                s = c % NS
                tok = slice(c * 128, (c + 1) * 128)
                if c == NT // 2:
                    for h in range(MH):
                        P.add("dve", lambda e, h=h: e.tensor_scalar(out=C32[:, h, :], in0=C32[:, h, :],
                                                                    scalar1=flg[:, 0:1], scalar2=None, op0=ALU.mult),
                              r=[BC32h[h], B_cst], w=[BC32h[h]])
                        P.add("act", lambda e, h=h: e.copy(out=Cb[:, h, :], in_=C32[:, h, :]),
                              r=[BC32h[h]], w=[BCbh[h]])
                P.add("sp", lambda e, s=s, tok=tok: e.dma_start(
                    out=qT4[s], in_=mqT_s[:, :, tok].rearrange("h p t -> p h t")), r=[B["mqT_s"]], w=[Bin[s]], dma=True)
                P.add("sp", lambda e, s=s, tok=tok: e.dma_start(
                    out=kT4[s], in_=mkT_s[:, :, tok].rearrange("h p t -> p h t")), r=[B["mkT_s"]], w=[Bin[s]], dma=True)
                P.add("sp", lambda e, s=s, c=c: e.dma_start(out=k4[s], in_=mk_s[c]), r=[B["mk_s"]], w=[Bin[s]], dma=True)
                P.add("sp", lambda e, s=s, c=c: e.dma_start(out=v4[s], in_=mv_s[c]), r=[B["mv_s"]], w=[Bin[s]], dma=True)
                P.add("sp", lambda e, s=s, c=c: e.dma_start(out=o4[s], in_=mo_s[c]), r=[B["mo_s"]], w=[Bin[s]], dma=True)
                w_ = c % 2
                for h in range(MH):
                    P.add("pe", lambda e, s=s, h=h: e.matmul(psum[4][:, h * 128:(h + 1) * 128], kT4[s][:, h, :],
                                                             qT4[s][:, h, :], start=True, stop=True),
                          r=[Bin[s]], w=[Bps[4]])
                P.add("dve", lambda e, w_=w_: e.tensor_tensor(
                    out=wt[w_], in0=psum[4][:, :].rearrange("p (h t) -> p h t", h=MH),
                    in1=tri_b.unsqueeze(1).to_broadcast([128, MH, 128]), op=ALU.mult),
                    r=[Bps[4], B_cst], w=[Bwt[w_]])
                for h in range(MH):
                    hb = h // 2
                    hc = (h % 2) * 256
                    P.add("pe", lambda e, s=s, h=h, w_=w_, hb=hb, hc=hc: e.matmul(
                        psum[hb][:, hc:hc + 256], wt[w_][:, h, :], v4[s][:, h, 0:256], start=True, stop=False),
                        r=[Bwt[w_], Bin[s]], w=[Bps[hb]])
                    P.add("pe", lambda e, s=s, h=h, hb=hb, hc=hc: e.matmul(
                        psum[hb][:, hc:hc + 256], qT4[s][:, h, :], Cb[:, h, 0:256], start=False, stop=True),
                        r=[Bin[s], BCbh[h]], w=[Bps[hb]])
                for h in range(MH):
                    P.add("pe", lambda e, s=s, h=h, w_=w_: e.matmul(
                        psum[2][:, h:h + 1], wt[w_][:, h, :], v4[s][:, h, 256:257], start=True, stop=False),
                        r=[Bwt[w_], Bin[s]], w=[Bps[2]])
                    P.add("pe", lambda e, s=s, h=h: e.matmul(
                        psum[2][:, h:h + 1], qT4[s][:, h, :], Cb[:, h, 256:257], start=False, stop=True),
                        r=[Bin[s], BCbh[h]], w=[Bps[2]])
                for h in range(MH):
                    ub = 5 + (uc % 2)
                    u = uc % 2
                    uc += 1
                    P.add("pe", lambda e, s=s, h=h, ub=ub: e.matmul(psum[ub][:, 0:257], k4[s][:, h * 128:(h + 1) * 128],
                                                                    v4[s][:, h, :], start=True, stop=True),
                          r=[Bin[s]], w=[Bps[ub]])
                    P.add("dve", lambda e, u=u, ub=ub, h=h: e.tensor_tensor(
                        out=tmpU[u], in0=psum[ub][:, 0:257], in1=C32[:, h, :], op=ALU.add),
                        r=[Bps[ub], BC32h[h]], w=[BtmpU[u]])
                    P.add("act", lambda e, u=u, c=c, h=h: e.activation(
                        out=Cb[:, h, :], in_=tmpU[u], func=AF.Copy, scale=eL[:, c, h:h + 1]),
                        r=[BtmpU[u], Btot], w=[BCbh[h]])
                    P.add("dve", lambda e, u=u, c=c, h=h: e.tensor_scalar(
                        out=C32[:, h, :], in0=tmpU[u], scalar1=eL[:, c, h:h + 1], scalar2=None, op0=ALU.mult),
                        r=[BtmpU[u], Btot], w=[BC32h[h]])
                for fn in pending:
                    fn()
                pending = []
                P.add("dve", lambda e: e.tensor_reduce(out=dn[:, 0:4], in_=psum[2][:, 0:4].unsqueeze(2), axis=AX.X,
                                                       op=ALU.max, apply_absolute_value=True), r=[Bps[2]], w=[Bdn])
                P.add("dve", lambda e: e.tensor_scalar(out=dn[:, 4:8], in0=dn[:, 0:4], scalar1=1.0, scalar2=None,
                                                       op0=ALU.max), r=[Bdn], w=[Bdn])
                P.add("dve", lambda e: e.reciprocal(out=dn[:, 8:12], in_=dn[:, 4:8]), r=[Bdn], w=[Bdn])
                for h in range(MH):
                    hb = h // 2
                    hc = (h % 2) * 256
                    P.add("act", lambda e, h=h, hb=hb, hc=hc: e.activation(
                        out=junk, in_=psum[hb][:, hc:hc + 256], func=AF.Square, scale=dn[:, 8 + h:9 + h],
                        accum_out=dn[:, 12 + h:13 + h]), r=[Bps[hb], Bdn], w=[Bjunk, Bss])
                P.add("act", lambda e: e.activation(out=dn[:, 16:20], in_=dn[:, 12:16], func=AF.Sqrt, scale=1.0 / 256,
                                                    bias=eps_t[:, 0:1]), r=[Bss], w=[Bss])
                P.add("dve", lambda e: e.reciprocal(out=dn[:, 20:24], in_=dn[:, 16:20]), r=[Bss], w=[Bss])
                P.add("dve", lambda e: e.tensor_tensor(out=dn[:, 24:28], in0=dn[:, 20:24], in1=dn[:, 8:12], op=ALU.mult),
                      r=[Bss, Bdn], w=[Bss])
                for h in range(MH):
                    hb = h // 2
                    hc = (h % 2) * 256
                    P.add("dve", lambda e, h=h, hb=hb, hc=hc: e.scalar_tensor_tensor(
                        out=t2[h], in0=psum[hb][:, hc:hc + 256], scalar=dn[:, 24 + h:25 + h],
                        in1=gbc[:, h * 256:(h + 1) * 256], op0=ALU.mult, op1=ALU.mult),
                        r=[Bps[hb], Bss, Bg], w=[Bt2[h]])
                    P.add("pool", lambda e, h=h, w_=w_, s=s: e.tensor_tensor(
                        out=mo[w_][:, h * 256:(h + 1) * 256], in0=t2[h], in1=o4[s][:, h * 256:(h + 1) * 256],
                        op=ALU.mult), r=[Bt2[h], Bin[s]], w=[Bmo[w_]])

                def do_transposes(w_=w_, tok=tok):
                    pv = ps_bf(7)
                    for q in range(8):
                        P.add("pe", lambda e, q=q, pv=pv: e.transpose(
                            out=pv[:, q * 128:(q + 1) * 128], in_=mo[w_][:, q * 128:(q + 1) * 128], identity=ident_b),
                            r=[Bmo[w_], B_cst], w=[Bps[7]])
                    P.add("act", lambda e, pv=pv: e.copy(out=moT[w_], in_=pv[:, 0:1024].rearrange(
                        "p (q t) -> p q t", q=8)), r=[Bps[7]], w=[BmoT[w_]])
                    P.add("sp", lambda e: e.dma_start(
                        out=moT_s[:, :, tok].rearrange("q p t -> p q t"), in_=moT[w_]),
                        r=[BmoT[w_]], w=[B["moT_s"]], dma=True)
                pending.append(do_transposes)
            for fn in pending:
                fn()
            P.barrier()


        def phase4(l, j, src_ap, Bsrc):
            A.reset(KEEP2)
            r0 = j * T
            mT = A.alloc([128, KC, T], BF16)
            BmT = P.buf("mT")
            mark = A.off
            TH = T // 2
            aoT = A.alloc([128, NH, TH], BF16)
            moT = A.alloc([128, NH, TH], BF16)
            Bao, Bmo = P.buf("p4_ao"), P.buf("p4_mo")
            ws = WStream(8, 512, nslots=4, name="p4w")
            gat = [A.alloc([128, TH], BF16) for _ in range(2)]
            gmt = [A.alloc([128, TH], BF16) for _ in range(2)]
            Bgat = [P.buf("p4_ga%d" % i) for i in range(2)]
            Bgmt = [P.buf("p4_gm%d" % i) for i in range(2)]
            t1 = [A.alloc([128, 512], F32) for _ in range(2)]
            t2 = [A.alloc([128, 512], F32) for _ in range(2)]
            Bt1 = [P.buf("p4_t1%d" % i) for i in range(2)]
            Bt2 = [P.buf("p4_t2%d" % i) for i in range(2)]
            seq = [(th, n) for th in range(2) for n in range(4)]

            def mk(idx):
                th, n = seq[idx]
                ta = ws.tile(src_fn=lambda k0, k1, n=n: w_pa[l, k0 * 128:k1 * 128, n * 512:(n + 1) * 512].rearrange(
                    "(c p) n -> p c n", p=128))
                tm = ws.tile(src_fn=lambda k0, k1, n=n: w_pm[l, k0 * 128:k1 * 128, n * 512:(n + 1) * 512].rearrange(
                    "(c p) n -> p c n", p=128))
                return ta, tm

            def load_gates(th, dc):
                gs = dc % 2
                c0 = r0 + th * TH
                P.add("sp", lambda e, gs=gs, dc=dc, c0=c0: e.dma_start(out=gat[gs], in_=gaT_s[dc, :, c0:c0 + TH]),
                      r=[B["gaT_s"]], w=[Bgat[gs]], dma=True)
                P.add("sp", lambda e, gs=gs, dc=dc, c0=c0: e.dma_start(out=gmt[gs], in_=gmT_s[dc, :, c0:c0 + TH]),
                      r=[B["gmT_s"]], w=[Bgmt[gs]], dma=True)

            cur = mk(0)
            cur[0].finish()
            cur[1].finish()
            load_gates(0, 0)
            k = 0
            for idx, (th, n) in enumerate(seq):
                if n == 0:
                    c0 = r0 + th * TH
                    P.add("sp", lambda e, c0=c0: e.dma_start(
                        out=aoT, in_=aoT_s[:, :, c0:c0 + TH].rearrange("h p t -> p h t")),
                        r=[B["aoT_s"]], w=[Bao], dma=True)
                    P.add("sp", lambda e, c0=c0: e.dma_start(
                        out=moT, in_=moT_s[:, :, c0:c0 + TH].rearrange("h p t -> p h t")),
                        r=[B["moT_s"]], w=[Bmo], dma=True)
                nxt_t = mk(idx + 1) if idx + 1 < len(seq) else None
                wa, Bwa = cur[0].wb, cur[0].Bwb
                wm, Bwm = cur[1].wb, cur[1].Bwb
                un = 0
                for q in range(4):
                    dc = n * 4 + q
                    gs = dc % 2
                    if q < 3:
                        load_gates(th, dc + 1)
                    elif idx + 1 < len(seq):
                        load_gates(seq[idx + 1][0], seq[idx + 1][1] * 4)
                    for tg in range(TH // 512):
                        if nxt_t is not None and un % 2 == 0:
                            nxt_t[(un // 2) % 2].step()
                        un += 1
                        tks = slice(tg * 512, (tg + 1) * 512)
                        otk = slice(th * TH + tg * 512, th * TH + (tg + 1) * 512)
                        pa = (k % 3)
                        pm = 3 + (k % 3)
                        u = k % 2
                        k += 1
                        mm_acc(psum[pa][:, :], Bps[pa], [(wa[:, c, q * 128:(q + 1) * 128], aoT[:, c, tks])
                                                         for c in range(8)], [Bwa, Bao])
                        mm_acc(psum[pm][:, :], Bps[pm], [(wm[:, c, q * 128:(q + 1) * 128], moT[:, c, tks])
                                                         for c in range(8)], [Bwm, Bmo])
                        P.add("dve", lambda e, pa=pa, u=u, gs=gs, tks=tks: e.tensor_tensor(
                            out=t1[u], in0=psum[pa][:, :], in1=gat[gs][:, tks], op=ALU.mult),
                            r=[Bps[pa], Bgat[gs]], w=[Bt1[u]])
                        P.add("dve", lambda e, pm=pm, u=u, gs=gs, tks=tks: e.tensor_tensor(
                            out=t2[u], in0=psum[pm][:, :], in1=gmt[gs][:, tks], op=ALU.mult),
                            r=[Bps[pm], Bgmt[gs]], w=[Bt2[u]])
                        P.add("dve", lambda e, u=u, dc=dc, otk=otk: e.tensor_tensor(
                            out=mT[:, dc, otk], in0=t1[u], in1=t2[u], op=ALU.add),
                            r=[Bt1[u], Bt2[u]], w=[BmT])
                if nxt_t is not None:
                    nxt_t[0].finish()
                    nxt_t[1].finish()
                cur = nxt_t
            P.barrier()
            A.reset(mark)
            ws = WStream(KC, 512, name="p4o")
            xp = [A.alloc([128, 512], F32) for _ in range(3)]
            Bxp = [P.buf("p4_xp%d" % i) for i in range(3)]
            xo = [A.alloc([128, 512], F32) for _ in range(3)]
            Bxo = [P.buf("p4_xo%d" % i) for i in range(3)]
            osrcs = [dict(src_fn=lambda k0, k1, n=n: w_out[l, k0 * 128:k1 * 128, n * 512:(n + 1) * 512].rearrange(
                "(c p) n -> p c n", p=128)) for n in range(4)]

            def ld_x(kk):
                n, i = divmod(kk, HT)
                u = kk % 3
                rows = slice(r0 + i * 128, r0 + (i + 1) * 128)
                cols = slice(n * 512, (n + 1) * 512)
                P.add("sp", lambda e, u=u, rows=rows, cols=cols: e.dma_start(out=xp[u], in_=src_ap[rows, cols]),
                      r=[Bsrc], w=[Bxp[u]], dma=True)

            ld_x(0)
            k = 0
            for n, wb, Bwb, tick in pipelined(ws, osrcs, HT):
                cols = slice(n * 512, (n + 1) * 512)
                for i in range(HT):
                    tick(i)
                    rows = slice(r0 + i * 128, r0 + (i + 1) * 128)
                    tok = slice(i * 128, (i + 1) * 128)
                    pb = k % 4
                    u = k % 3
                    if k + 1 < 4 * HT:
                        ld_x(k + 1)
                    k += 1
                    mm_acc(psum[pb][:, :], Bps[pb], [(mT[:, c, tok], wb[:, c, :]) for c in range(KC)], [BmT, Bwb])
                    P.add("dve", lambda e, u=u, pb=pb: e.tensor_tensor(out=xo[u], in0=psum[pb][:, :], in1=xp[u],
                                                                        op=ALU.add), r=[Bps[pb], Bxp[u]], w=[Bxo[u]])
                    P.add("sp", lambda e, u=u, rows=rows, cols=cols: e.dma_start(out=x_mid[rows, cols], in_=xo[u]),
                          r=[Bxo[u]], w=[B["x_mid"]], dma=True)
            P.barrier()

        def phase5(l, j, dst_ap, Bdst, dst_off=0, halo=False):
            A.reset(KEEP2)
            r0 = j * T
            h2T = A.alloc([128, KC, T], BF16)
            Bh2T = [P.buf("h2T%d" % i) for i in range(HT)]
            mark = A.off
            cw = A.alloc([128, 2 * FC, 3], F32)
            cb = A.alloc([128, 2 * FC], F32)
            Bcw = P.buf("p5_cw")
            P.add("sp", lambda e: e.dma_start(out=cw, in_=cw_d[l]), r=[B_w], w=[Bcw], dma=True)
            P.add("sp", lambda e: e.dma_start(out=cb, in_=cb_d[l]), r=[B_w], w=[Bcw], dma=True)
            if j == 0:
                P.add("pool", lambda e: e.memset(hist, 0.0), w=[Bhist])
            elif not halo:
                P.add("dve", lambda e: e.tensor_scalar(out=hist, in0=hist, scalar1=flg[:, 0:1], scalar2=None,
                                                       op0=ALU.mult), r=[Bhist, B_cst], w=[Bhist])
            if halo:
                xh = A.alloc([2, D], F32)
                hh = A.alloc([2, D], BF16)
                hjunk = A.alloc([2, D], BF16)
                g2b = A.alloc([2, D], F32)
                hst = A.alloc([2, 4], F32)
                h2h = A.alloc([128, KC, 2], BF16)
                Bhalo = P.buf("p5_halo")
                P.add("sp", lambda e: e.dma_start(out=xh, in_=x_mid[r0 - 2:r0, :]), r=[B["x_mid"]], w=[Bhalo], dma=True)
                P.add("sp", lambda e: e.dma_start(out=g2b, in_=g2_d[l:l + 1, :].partition_broadcast(2)),
                      r=[B_w], w=[Bhalo], dma=True)
                P.add("act", lambda e: e.activation(out=hjunk, in_=xh, func=AF.Square, accum_out=hst[:, 0:1]),
                      r=[Bhalo], w=[Bhalo])
                P.add("act", lambda e: e.activation(out=hst[:, 1:2], in_=hst[:, 0:1], func=AF.Sqrt, scale=1.0 / D,
                                                    bias=eps_t[0:2, 0:1]), r=[Bhalo], w=[Bhalo])
                P.add("dve", lambda e: e.reciprocal(out=hst[:, 2:3], in_=hst[:, 1:2]), r=[Bhalo], w=[Bhalo])
                P.add("dve", lambda e: e.scalar_tensor_tensor(out=hh, in0=xh, scalar=hst[:, 2:3], in1=g2b,
                                                              op0=ALU.mult, op1=ALU.mult), r=[Bhalo], w=[Bhalo])
                pvh = ps_bf(6)
                for c in range(KC):
                    P.add("pe", lambda e, c=c: e.transpose(out=pvh[:, 2 * c:2 * c + 2],
                                                           in_=hh[0:2, c * 128:(c + 1) * 128],
                                                           identity=ident_b[0:2, 0:2]), r=[Bhalo, B_cst], w=[Bps[6]])
                P.add("dve", lambda e: e.tensor_copy(out=h2h, in_=pvh[:, 0:2 * KC].rearrange("p (c t) -> p c t", t=2)),
                      r=[Bps[6]], w=[Bhalo])
            ws = WStream(KC, 256, name="p5u")
            av = [[A.alloc([128, 512], F32) for _ in range(3)] for _ in range(2)]
            Bav = [[P.buf("p5_a%d%d" % (x, i)) for i in range(3)] for x in range(2)]
            sg = [A.alloc([128, 512], F32) for _ in range(3)]
            Bsg = [P.buf("p5_sg%d" % i) for i in range(3)]
            ast = [A.alloc([128, 512], BF16) for _ in range(3)]
            Bast = [P.buf("p5_ast%d" % i) for i in range(3)]
            k = 0
            usrcs = [dict(srcs=[
                (0, 128, lambda k0, k1, fc=fc: w_up[l, k0 * 128:k1 * 128, fc * 128:(fc + 1) * 128].rearrange(
                    "(c p) n -> p c n", p=128)),
                (128, 256, lambda k0, k1, fc=fc: w_up[l, k0 * 128:k1 * 128, DFF + fc * 128:DFF + (fc + 1) * 128].rearrange(
                    "(c p) n -> p c n", p=128))]) for fc in range(FC)]
            first_tile = ws.tile(**usrcs[0])
            first_tile.finish()
            norm_it = phase_norm(x_mid, B["x_mid"], r0, h2T, Bh2T, g2_d[l:l + 1, :])
            for _ in norm_it:
                pass
            for fc, wb, Bwb, tick in pipelined(ws, usrcs, TG, first=first_tile):
                if halo:
                    for x in range(2):
                        hb = 6 + x
                        fcx = x * FC + fc
                        mm_acc(psum[hb][:, 0:2], Bps[hb], [(wb[:, c, x * 128:(x + 1) * 128], h2h[:, c, :])
                                                           for c in range(KC)], [Bwb, Bhalo])
                        P.add("dve", lambda e, hb=hb, fcx=fcx: e.tensor_scalar(
                            out=hist[:, fcx, 0:2], in0=psum[hb][:, 0:2], scalar1=flg[:, 0:1], scalar2=None,
                            op0=ALU.mult), r=[Bps[hb], B_cst, Bhist], w=[Bhist])
                for tg in range(TG):
                    tick(tg)
                    tks = slice(tg * 512, (tg + 1) * 512)
                    u = k % 3
                    k3 = k
                    k += 1
                    first = (j == 0 and tg == 0)
                    for x in range(2):
                        pb = 3 * x + (k3 % 3)
                        fcx = x * FC + fc
                        a = av[x][u]
                        Ba = Bav[x][u]
                        mm_acc(psum[pb][:, :], Bps[pb], [(wb[:, c, x * 128:(x + 1) * 128], h2T[:, c, tks])
                                                         for c in range(KC)], [Bwb] + Bh2T[4 * tg:4 * tg + 4])
                        P.add("act", lambda e, a=a, pb=pb, fcx=fcx: e.activation(
                            out=a, in_=psum[pb][:, :], func=AF.Identity, scale=cw[:, fcx, 2:3],
                            bias=cb[:, fcx:fcx + 1]), r=[Bps[pb], Bcw], w=[Ba])
                        P.add("dve", lambda e, a=a, pb=pb, fcx=fcx: e.scalar_tensor_tensor(
                            out=a[:, 1:512], in0=psum[pb][:, 0:511], scalar=cw[:, fcx, 1:2], in1=a[:, 1:512],
                            op0=ALU.mult, op1=ALU.add), r=[Bps[pb], Bcw, Ba], w=[Ba])
                        P.add("dve", lambda e, a=a, pb=pb, fcx=fcx: e.scalar_tensor_tensor(
                            out=a[:, 2:512], in0=psum[pb][:, 0:510], scalar=cw[:, fcx, 0:1], in1=a[:, 2:512],
                            op0=ALU.mult, op1=ALU.add), r=[Bps[pb], Bcw, Ba], w=[Ba])
                        if not first:
                            P.add("dve", lambda e, a=a, fcx=fcx: e.scalar_tensor_tensor(
                                out=a[:, 0:1], in0=hist[:, fcx, 1:2], scalar=cw[:, fcx, 1:2], in1=a[:, 0:1],
                                op0=ALU.mult, op1=ALU.add), r=[Bhist, Bcw, Ba], w=[Ba])
                            P.add("dve", lambda e, a=a, fcx=fcx: e.scalar_tensor_tensor(
                                out=a[:, 0:2], in0=hist[:, fcx, 0:2], scalar=cw[:, fcx, 0:1], in1=a[:, 0:2],
                                op0=ALU.mult, op1=ALU.add), r=[Bhist, Bcw, Ba], w=[Ba])
                        P.add("dve", lambda e, pb=pb, fcx=fcx: e.tensor_copy(out=hist[:, fcx, 0:2],
                                                                              in_=psum[pb][:, 510:512]),
                              r=[Bps[pb], Bhist], w=[Bhist])
                    P.add("act", lambda e, u=u: e.activation(out=sg[u], in_=av[0][u], func=AF.Silu),
                          r=[Bav[0][u]], w=[Bsg[u]])
                    o = k % 3
                    P.add("pool", lambda e, u=u, o=o: e.tensor_tensor(out=ast[o], in0=sg[u], in1=av[1][u], op=ALU.mult),
                          r=[Bsg[u], Bav[1][u]], w=[Bast[o]])
                    ti0 = (r0 + tg * 512) // 128
                    P.add("sp", lambda e, o=o, fc=fc, ti0=ti0: e.dma_start(
                        out=actT_s[ti0:ti0 + 4, :, fc, :].rearrange("i p t -> p i t"),
                        in_=ast[o][:, :].rearrange("p (i t) -> p i t", i=4)),
                        r=[Bast[o]], w=[B["actT_s"]], dma=True)
            P.barrier()
            A.reset(KEEP2)
            ws = WStream(FC, 512, name="p5d")
            at = [A.alloc([128, FC, 128], BF16) for _ in range(2)]
            Bat = [P.buf("p5_at%d" % i) for i in range(2)]
            xp = [A.alloc([128, 512], F32) for _ in range(3)]
            Bxp = [P.buf("p5_xp%d" % i) for i in range(3)]
            xo = [A.alloc([128, 512], F32) for _ in range(3)]
            Bxo = [P.buf("p5_xo%d" % i) for i in range(3)]
            dsrcs = [dict(src_fn=lambda k0, k1, n=n: w_down[l, k0 * 128:k1 * 128, n * 512:(n + 1) * 512].rearrange(
                "(c p) n -> p c n", p=128)) for n in range(4)]

            def ld_in(kk):
                n, i = divmod(kk, HT)
                u = kk % 3
                a = kk % 2
                rows = slice(r0 + i * 128, r0 + (i + 1) * 128)
                cols = slice(n * 512, (n + 1) * 512)
                ti = (r0 + i * 128) // 128
                P.add("sp", lambda e, a=a, ti=ti: e.dma_start(out=at[a], in_=actT_s[ti]),
                      r=[B["actT_s"]], w=[Bat[a]], dma=True)
                P.add("sp", lambda e, u=u, rows=rows, cols=cols: e.dma_start(out=xp[u], in_=x_mid[rows, cols]),
                      r=[B["x_mid"]], w=[Bxp[u]], dma=True)

            ld_in(0)
            k = 0
            for n, wb, Bwb, tick in pipelined(ws, dsrcs, HT):
                cols = slice(n * 512, (n + 1) * 512)
                for i in range(HT):
                    tick(i)
                    rows = slice(r0 + i * 128, r0 + (i + 1) * 128)
                    pb = k % 4
                    u = k % 3
                    a = k % 2
                    if k + 1 < 4 * HT:
                        ld_in(k + 1)
                    k += 1
                    mm_acc(psum[pb][:, :], Bps[pb], [(at[a][:, f, :], wb[:, f, :]) for f in range(FC)], [Bat[a], Bwb])
                    P.add("dve", lambda e, u=u, pb=pb: e.tensor_tensor(out=xo[u], in0=psum[pb][:, :], in1=xp[u],
                                                                        op=ALU.add), r=[Bps[pb], Bxp[u]], w=[Bxo[u]])
                    drows = slice(rows.start - dst_off, rows.stop - dst_off)
                    P.add("sp", lambda e, u=u, drows=drows, cols=cols: e.dma_start(out=dst_ap[drows, cols], in_=xo[u]),
                          r=[Bxo[u]], w=[Bdst], dma=True)
            P.barrier()

        eps_t = A.alloc([128, 1], F32)
        one_t = A.alloc([128, 1], F32)
        lnsc_t = A.alloc([128, 1], F32)
        P.add("pool", lambda e: e.memset(eps_t, EPS), w=[B_cst])
        P.add("pool", lambda e: e.memset(one_t, 1.0), w=[B_cst])
        P.add("pool", lambda e: e.memset(lnsc_t, math.log(128 ** -0.5)), w=[B_cst])
        hist = A.alloc([128, 2 * FC, 2], F32)
        Bhist = P.buf("hist")
        flg = A.alloc([128, 1], F32)
        P.add("sp", lambda e: e.dma_start(out=flg, in_=flag_d), r=[B_w], w=[B_cst], dma=True)
        KEEP2 = A.off
        P.barrier()

        for l in range(n_layers):
            src_ap, Bsrc = (x_in, B_x_in) if l == 0 else (x_l1, B["x_l1"])
            for j in range(2):
                phase1(l, j, src_ap, Bsrc)
            if stop_after == "p1":
                break
            phase2(l)
            if stop_after == "p2":
                break
            phase3(l)
            if stop_after == "p3":
                break
            for j in range(2):
                phase4(l, j, src_ap, Bsrc)
            if stop_after == "p4":
                break
            last = (l == n_layers - 1)
            for j in range(2):
                if last:
                    if j == 1:
                        phase5(l, j, y_out, B["y"], dst_off=T, halo=True)
                else:
                    phase5(l, j, x_l1, B["x_l1"])

        P.barrier()
        P.emit()
    return nc


def _consts():
    c = np.zeros((128, 3, 128), np.float32)
    c[:, 0, :] = np.eye(128, dtype=np.float32)
    c[:, 1, :] = np.triu(np.ones((128, 128), np.float32))
    c[:, 2, :] = 1.0
    return c


def prep_shared(inp):
    f = lambda a: np.ascontiguousarray(np.asarray(a, dtype=np.float32))
    sh = {}
    sh["w_in"] = f(inp["w_in"])
    sh["w_pa"] = f(inp["w_proj_a"])
    sh["w_pm"] = f(inp["w_proj_m"])
    sh["w_out"] = f(inp["w_out"])
    sh["w_up"] = f(inp["w_up"])
    sh["w_down"] = f(inp["w_down"])
    sh["g1"] = f(inp["norm1_g"])
    sh["g2"] = f(inp["norm2_g"])
    gb = np.concatenate([np.asarray(inp["fox_f_bias"]), np.asarray(inp["m_f_bias"]), np.asarray(inp["m_i_bias"])], axis=1)
    sh["gbias"] = f(np.broadcast_to(gb[:, None, :], (NL, 128, 16)))
    sh["qkg"] = f(np.stack([np.asarray(inp["q_norm_g"]), np.asarray(inp["k_norm_g"])], axis=2))
    sh["mng"] = f(np.broadcast_to(np.asarray(inp["m_norm_g"]).reshape(NL, 1, 1024), (NL, 128, 1024)))
    sh["cw"] = f(np.asarray(inp["conv_w"]).reshape(NL, 3, 2 * FC, 128).transpose(0, 3, 2, 1))
    sh["cb"] = f(np.asarray(inp["conv_b"]).reshape(NL, 2 * FC, 128).transpose(0, 2, 1))
    sh["consts"] = _consts()
    return sh


def kernel(**inputs):
    x = np.asarray(inputs["x"], dtype=np.float32)
    Bn, S, _ = x.shape
    T = S // 2
    sh = prep_shared(inputs)
    nc = build(S=S)
    in_maps = []
    for b in range(Bn):
        for r in range(2):
            m = dict(sh)
            if r == 1:
                m["x"] = np.ascontiguousarray(x[b])
            else:
                m["x"] = np.ascontiguousarray(np.concatenate([np.zeros((T, D), np.float32), x[b, :T]], axis=0))
            m["flag"] = np.full((128, 1), float(r), np.float32)
            in_maps.append(m)
    res = run_bass_kernel_spmd(nc, in_maps, core_ids=list(range(2 * Bn)))
    out = np.empty((Bn, S, D), np.float32)
    for b in range(Bn):
        out[b, :T] = np.asarray(res.results[2 * b]["y"])
        out[b, T:] = np.asarray(res.results[2 * b + 1]["y"])
    return out
```

```python
import math
from contextlib import ExitStack

import numpy as np
import concourse.bass as bass
import concourse.mybir as mybir
from concourse.bass_utils import run_bass_kernel_spmd

F32 = mybir.dt.float32
BF16 = mybir.dt.bfloat16
U8 = mybir.dt.uint8
AF = mybir.ActivationFunctionType
ALU = mybir.AluOpType
AX = mybir.AxisListType

D = 2048
DIN = 10256
DFF = 5632
NL = 2
NH = 8
MH = 4
EPS = 1e-6
KC = D // 128
FC = DFF // 128
ENG = ("pe", "act", "dve", "pool", "sp")


class Buf:
    def __init__(self, name):
        self.name = name
        self.writers = []
        self.readers = []
        self.sem = None
        self.nd = 0


class Op:
    pass


class Prog:
    def __init__(self, nc, strict=("act", "dve", "pool")):
        self.nc = nc
        self.ops = {e: [] for e in ENG}
        self.bufs = []
        self.strict = strict

    def buf(self, name):
        for b in self.bufs:
            if b.name == name:
                return b
        b = Buf(name)
        self.bufs.append(b)
        return b

    def add(self, eng, fn, r=(), w=(), dma=False):
        op = Op()
        op.eng, op.fn, op.dma, op.signal = eng, fn, dma, False
        op.deps_eng, op.deps_sem = {}, {}

        def dep(d):
            if d.dma:
                k = id(d.dst)
                v = op.deps_sem.get(k)
                if v is None or v[1] < d.count:
                    op.deps_sem[k] = (d.dst, d.count)
            else:
                if d.eng == eng and eng not in self.strict:
                    return
                if op.deps_eng.get(d.eng, -1) < d.seq:
                    op.deps_eng[d.eng] = d.seq

        for b in r:
            for d in b.writers:
                dep(d)
        for b in w:
            for d in b.readers:
                dep(d)
        for b in r:
            b.readers.append(op)
        for b in w:
            if b.readers:
                b.writers = [op]
                b.readers = []
            else:
                b.writers.append(op)
        op.seq = len(self.ops[eng])
        self.ops[eng].append(op)
        if dma:
            op.dst = w[0]
            op.dst.nd += 1
            op.count = 16 * op.dst.nd
        return op

    def barrier(self):
        lasts = {}
        for e in ENG:
            if e == "sp":
                continue
            for op in reversed(self.ops[e]):
                if op.fn is not None:
                    lasts[e] = op.seq
                    break
        sems = {id(b): (b, 16 * b.nd) for b in self.bufs if b.nd > 0}
        for e in ENG:
            op = Op()
            op.eng, op.fn, op.dma, op.signal = e, None, False, False
            op.deps_eng = {e2: s for e2, s in lasts.items() if e2 != e}
            op.deps_sem = dict(sems)
            op.seq = len(self.ops[e])
            self.ops[e].append(op)
        for b in self.bufs:
            b.writers = []
            b.readers = []

    def emit(self):
        nc = self.nc
        for e in ENG:
            for op in self.ops[e]:
                for pe_, s in op.deps_eng.items():
                    self.ops[pe_][s].signal = True
        for e in ENG:
            c = 0
            for op in self.ops[e]:
                if op.signal:
                    c += 1
                op.sig = c
        with ExitStack() as es:
            esem = {e: es.enter_context(nc.semaphore("S_" + e)) for e in ENG}
            for b in self.bufs:
                if b.nd > 0:
                    b.sem = es.enter_context(nc.semaphore("D_" + b.name))
            block = es.enter_context(nc.Block())

            def run(e, eng):
                waited = {}
                for op in self.ops[e]:
                    for pe_, s in op.deps_eng.items():
                        v = self.ops[pe_][s].sig
                        key = ("e", pe_)
                        if waited.get(key, 0) < v:
                            eng.wait_ge(esem[pe_], v)
                            waited[key] = v
                    for (dst, cnt) in op.deps_sem.values():
                        key = ("d", id(dst))
                        if waited.get(key, 0) < cnt:
                            eng.wait_ge(dst.sem, cnt)
                            waited[key] = cnt
                    if op.fn is not None:
                        inst = op.fn(eng)
                        if op.dma:
                            inst.then_inc(op.dst.sem, 16)
                        elif op.signal:
                            inst.then_inc(esem[e], 1)

            @block.tensor
            def _(eng):
                run("pe", eng)

            @block.scalar
            def _(eng):
                run("act", eng)

            @block.vector
            def _(eng):
                run("dve", eng)

            @block.gpsimd
            def _(eng):
                run("pool", eng)

            @block.sync
            def _(eng):
                run("sp", eng)


class Arena:
    def __init__(self, handle, size):
        self.h, self.size, self.off = handle, size, 0

    def reset(self, keep=0):
        self.off = keep

    def alloc(self, shape, dtype):
        esz = 2 if dtype == BF16 else 4
        n = 1
        for s in shape[1:]:
            n *= s
        nbytes = (n * esz + 63) // 64 * 64
        assert self.off + nbytes <= self.size, ("SBUF arena overflow", self.off, nbytes)
        v = self.h[0:shape[0], self.off:self.off + n * esz].bitcast(dtype)
        self.off += nbytes
        if len(shape) == 3:
            v = v.rearrange("p (a b) -> p a b", a=shape[1])
        elif len(shape) == 4:
            v = v.rearrange("p (a b c) -> p a b c", a=shape[1], b=shape[2])
        return v


def build(S=4096, n_layers=NL, debug=(), stop_after=None):
    nc = bass.Bass("TRN2", target_bir_lowering=False)
    P = Prog(nc)
    NT = S // 128
    T = S // 2
    HT = T // 128
    TG = T // 512

    def dram(name, shape, dt, kind="Internal"):
        if name in debug:
            kind = "ExternalOutput"
        return nc.dram_tensor(name, list(shape), dt, kind=kind).ap()

    x_in = dram("x", [S, D], F32, "ExternalInput")
    w_in = dram("w_in", [NL, D, DIN], F32, "ExternalInput")
    w_pa = dram("w_pa", [NL, 1024, D], F32, "ExternalInput")
    w_pm = dram("w_pm", [NL, 1024, D], F32, "ExternalInput")
    w_out = dram("w_out", [NL, D, D], F32, "ExternalInput")
    w_up = dram("w_up", [NL, D, 2 * DFF], F32, "ExternalInput")
    w_down = dram("w_down", [NL, DFF, D], F32, "ExternalInput")
    g1_d = dram("g1", [NL, D], F32, "ExternalInput")
    g2_d = dram("g2", [NL, D], F32, "ExternalInput")
    gbias_d = dram("gbias", [NL, 128, 16], F32, "ExternalInput")
    qkg_d = dram("qkg", [NL, 128, 2], F32, "ExternalInput")
    mng_d = dram("mng", [NL, 128, 1024], F32, "ExternalInput")
    cw_d = dram("cw", [NL, 128, 2 * FC, 3], F32, "ExternalInput")
    cb_d = dram("cb", [NL, 128, 2 * FC], F32, "ExternalInput")
    consts_d = dram("consts", [128, 3, 128], F32, "ExternalInput")
    flag_d = dram("flag", [128, 1], F32, "ExternalInput")
    y_out = dram("y", [S // 2, D], F32, "ExternalOutput")

    qT_s = dram("qT_s", [NH, 128, S], BF16)
    kT_s = dram("kT_s", [NH, 128, S], BF16)
    vp_s = dram("vp_s", [NH, 128, NT, 129], BF16)
    tot_s = dram("tot_s", [NT, 12], F32)
    mqT_s = dram("mqT_s", [MH, 128, S], BF16)
    mkT_s = dram("mkT_s", [MH, 128, S], BF16)
    mk_s = dram("mk_s", [NT, 128, 512], BF16)
    mv_s = dram("mv_s", [NT, 128, MH, 257], BF16)
    mo_s = dram("mo_s", [NT, 128, 1024], BF16)
    gaT_s = dram("gaT_s", [KC, 128, S], BF16)
    gmT_s = dram("gmT_s", [KC, 128, S], BF16)
    aoT_s = dram("aoT_s", [NH, 128, S], BF16)
    moT_s = dram("moT_s", [NH, 128, S], BF16)
    x_mid = dram("x_mid", [S, D], F32)
    x_l1 = dram("x_l1", [S, D], F32)
    actT_s = dram("actT_s", [S // 128, 128, FC, 128], BF16)

    B_x_in = P.buf("x_in")
    B = {n: P.buf(n) for n in ("qT_s", "kT_s", "vp_s", "tot_s", "mqT_s", "mkT_s", "mk_s", "mv_s",
                               "mo_s", "gaT_s", "gmT_s", "aoT_s", "moT_s", "x_mid", "x_l1", "actT_s", "y")}
    B_w = P.buf("weights")

    with ExitStack() as es:
        ARENA_BYTES = 204 * 1024
        arena_h = es.enter_context(nc.sbuf_tensor("arena", [128, ARENA_BYTES], U8))
        A = Arena(arena_h, ARENA_BYTES)
        psum = [es.enter_context(nc.psum_tensor("ps%d" % i, [128, 512], F32)) for i in range(8)]
        Bps = [P.buf("ps%d" % i) for i in range(8)]

        def ps_bf(i):
            return psum[i][:, :].bitcast(BF16)

        cst = A.alloc([128, 3, 128], F32)
        ident_b = A.alloc([128, 128], BF16)
        ones_b = A.alloc([128, 128], BF16)
        tri_b = A.alloc([128, 128], BF16)
        B_cst = P.buf("cst")
        P.add("sp", lambda e: e.dma_start(out=cst, in_=consts_d), r=[B_w], w=[B_cst], dma=True)
        P.add("dve", lambda e: e.tensor_copy(out=ident_b, in_=cst[:, 0, :]), r=[B_cst], w=[B_cst])
        P.add("dve", lambda e: e.tensor_copy(out=tri_b, in_=cst[:, 1, :]), r=[B_cst], w=[B_cst])
        P.add("dve", lambda e: e.tensor_copy(out=ones_b, in_=cst[:, 2, :]), r=[B_cst], w=[B_cst])
        tri_f = cst[:, 1, :]
        ones_f = cst[:, 2, :]
        KEEP = A.off
        P.barrier()

        class WTile:
            def __init__(self, ws, srcs, ncols):
                self.ws, self.srcs, self.ncols = ws, srcs, ncols
                slot = ws.wi % len(ws.wb)
                ws.wi += 1
                self.wb, self.Bwb = ws.wb[slot], ws.Bwb[slot]
                self.pieces = [(k0, min(ws.nk, k0 + ws.kp)) for k0 in range(0, ws.nk, ws.kp)]
                self.stg_of = {}
                self.nd = 0
                self.ncst = 0

            def _dma(self):
                ws = self.ws
                k0, k1 = self.pieces[self.nd]
                si = ws.si % len(ws.stg)
                ws.si += 1
                stg, Bstg = ws.stg[si], ws.Bstg[si]
                self.stg_of[self.nd] = (stg, Bstg)
                self.nd += 1
                for (c0, c1, fn) in self.srcs:
                    P.add("sp", lambda e, stg=stg, k0=k0, k1=k1, c0=c0, c1=c1, fn=fn: e.dma_start(
                        out=stg[:, 0:k1 - k0, c0:c1], in_=fn(k0, k1)), r=[B_w], w=[Bstg], dma=True)

            def step(self):
                if self.ncst >= len(self.pieces):
                    return False
                while self.nd < min(len(self.pieces), self.ncst + 2):
                    self._dma()
                k0, k1 = self.pieces[self.ncst]
                stg, Bstg = self.stg_of.pop(self.ncst)
                self.ncst += 1
                wb, ncols = self.wb, self.ncols
                P.add("pool", lambda e, stg=stg, k0=k0, k1=k1: e.tensor_copy(
                    out=wb[:, k0:k1, 0:ncols], in_=stg[:, 0:k1 - k0, 0:ncols]), r=[Bstg], w=[self.Bwb])
                return True

            def finish(self):
                while self.step():
                    pass
                return self.wb, self.Bwb

        class WStream:
            def __init__(self, nk, ncols, nslots=2, kp=4, nstg=4, name="w"):
                self.nk, self.ncols, self.kp = nk, ncols, kp
                self.stg = [A.alloc([128, kp, ncols], F32) for _ in range(nstg)]
                self.Bstg = [P.buf("%s_stg%d" % (name, i)) for i in range(nstg)]
                self.wb = [A.alloc([128, nk, ncols], BF16) for _ in range(nslots)]
                self.Bwb = [P.buf("%s_wb%d" % (name, i)) for i in range(nslots)]
                self.si = 0
                self.wi = 0

            def tile(self, src_fn=None, srcs=None, ncols=None):
                ncols = ncols or self.ncols
                if srcs is None:
                    srcs = [(0, ncols, src_fn)]
                return WTile(self, srcs, ncols)

        def pipelined(ws, tiles_srcs, nunits, first=None):
            cur = first
            if cur is None:
                cur = ws.tile(**tiles_srcs[0])
                cur.finish()
            for n in range(len(tiles_srcs)):
                nxt = ws.tile(**tiles_srcs[n + 1]) if n + 1 < len(tiles_srcs) else None
                npieces = len(nxt.pieces) if nxt is not None else 0
                state = {"done": 0}

                def tick(u, nxt=nxt, npieces=npieces, state=state):
                    if nxt is None:
                        return
                    want = min(npieces, (u + 1) * npieces // nunits + 1)
                    while state["done"] < want and nxt.step():
                        state["done"] += 1
                yield n, cur.wb, cur.Bwb, tick
                if nxt is not None:
                    nxt.finish()
                cur = nxt

        def mm_acc(ps_ap, Bp, pairs, rbufs):
            n = len(pairs)
            for i, (l, r_) in enumerate(pairs):
                P.add("pe", lambda e, l=l, r_=r_, i=i: e.matmul(ps_ap, l, r_, start=(i == 0), stop=(i == n - 1)),
                      r=rbufs, w=[Bp])

        def phase_norm(src_ap, Bsrc, r0, hT, BhT, g_row):
            gbc = A.alloc([128, D], F32)
            Bgbc = P.buf("gbc")
            P.add("sp", lambda e: e.dma_start(out=gbc, in_=g_row.partition_broadcast(128)), r=[B_w], w=[Bgbc], dma=True)
            xt = [A.alloc([128, D], F32) for _ in range(2)]
            Bxt = [P.buf("xt%d" % i) for i in range(2)]
            hn = [A.alloc([128, D], BF16) for _ in range(2)]
            Bhn = [P.buf("hn%d" % i) for i in range(2)]
            junk = A.alloc([128, D], BF16)
            Bjunk = P.buf("junk")
            st = A.alloc([128, 8], F32)
            Bst = P.buf("st")
            for i in range(HT):
                s = i % 2
                rows = slice(r0 + i * 128, r0 + (i + 1) * 128)
                P.add("sp", lambda e, s=s, rows=rows: e.dma_start(out=xt[s], in_=src_ap[rows, :]),
                      r=[Bsrc], w=[Bxt[s]], dma=True)
                P.add("act", lambda e, s=s: e.activation(out=junk, in_=xt[s], func=AF.Square, accum_out=st[:, 0:1]),
                      r=[Bxt[s]], w=[Bjunk, Bst])
                P.add("act", lambda e: e.activation(out=st[:, 1:2], in_=st[:, 0:1], func=AF.Sqrt,
                                                    scale=1.0 / D, bias=eps_t[:, 0:1]), r=[Bst], w=[Bst])
                P.add("dve", lambda e: e.reciprocal(out=st[:, 2:3], in_=st[:, 1:2]), r=[Bst], w=[Bst])
                P.add("dve", lambda e, s=s: e.scalar_tensor_tensor(out=hn[s], in0=xt[s], scalar=st[:, 2:3], in1=gbc,
                                                                   op0=ALU.mult, op1=ALU.mult),
                      r=[Bst, Bxt[s], Bgbc], w=[Bhn[s]])
                for q in range(4):
                    pb = 4 + (q % 2)
                    pv = ps_bf(pb)
                    for c4 in range(4):
                        c = q * 4 + c4
                        P.add("pe", lambda e, s=s, c=c, c4=c4, pv=pv: e.transpose(
                            out=pv[:, c4 * 128:(c4 + 1) * 128], in_=hn[s][:, c * 128:(c + 1) * 128],
                            identity=ident_b), r=[Bhn[s], B_cst], w=[Bps[pb]])
                    ce = "act" if q % 2 == 0 else "dve"
                    dst = hT[:, q * 4:(q + 1) * 4, i * 128:(i + 1) * 128]
                    srcv = pv[:, 0:512].rearrange("p (a b) -> p a b", a=4)
                    if ce == "act":
                        P.add("act", lambda e, dst=dst, srcv=srcv: e.copy(out=dst, in_=srcv), r=[Bps[pb]], w=[BhT[i]])
                    else:
                        P.add("dve", lambda e, dst=dst, srcv=srcv: e.tensor_copy(out=dst, in_=srcv),
                              r=[Bps[pb]], w=[BhT[i]])
                yield

        eps_t = None

        def phase1(l, j, src_ap, Bsrc, trim=False):
            A.reset(KEEP2)
            r0 = j * T
            hT = A.alloc([128, KC, T], BF16)
            BhT = [P.buf("hT%d" % i) for i in range(HT)]
            mark = A.off
            gb = A.alloc([128, 16], F32)
            qkg = A.alloc([128, 4], F32)
            Bpar = P.buf("par1")
            P.add("sp", lambda e: e.dma_start(out=gb, in_=gbias_d[l]), r=[B_w], w=[Bpar], dma=True)
            P.add("sp", lambda e: e.dma_start(out=qkg[:, 0:2], in_=qkg_d[l]), r=[B_w], w=[Bpar], dma=True)
            P.add("dve", lambda e: e.scalar_tensor_tensor(out=qkg[:, 2:3], in0=qkg[:, 0:1], scalar=float(128 ** -0.5),
                                                          in1=qkg[:, 1:2], op0=ALU.mult, op1=ALU.mult),
                  r=[Bpar], w=[Bpar])
            E_all = A.alloc([128, HT, 8], F32)
            QS_all = A.alloc([128, HT, 4], F32)
            KS_all = A.alloc([128, HT, 4], F32)
            Bgate = P.buf("gates")
            ws = WStream(KC, 512, name="win")
            wsm_stg = A.alloc([128, KC, 16], F32)
            wsm = A.alloc([128, KC, 16], BF16)
            Bwsm_stg, Bwsm = P.buf("wsm_stg"), P.buf("wsm")
            NST = 3
            st_bf = [A.alloc([128, 4, 132], BF16) for _ in range(NST)]
            Bst_bf = [P.buf("st_bf%d" % i) for i in range(NST)]
            st2_bf = [A.alloc([128, 512], BF16) for _ in range(NST)]
            Bst2_bf = [P.buf("st2_bf%d" % i) for i in range(NST)]
            mvst = [A.alloc([128, 2, 257], BF16) for _ in range(2)]
            Bmvst = [P.buf("mvst%d" % i) for i in range(2)]
            tmp_f = [A.alloc([128, 512], F32) for _ in range(2)]
            Btmp_f = [P.buf("tmp_f%d" % i) for i in range(2)]
            sm = A.alloc([128, 64], F32)
            Bsm = P.buf("sm")
            tsb = A.alloc([128, 16], F32)
            Btsb = P.buf("tsb")
            for s in range(2):
                P.add("pool", lambda e, s=s: e.memset(mvst[s][:, :, 256:257], 1.0), w=[Bmvst[s]])
            cnt = {"st": 0, "st2": 0, "mv": 0, "tf": 0, "ps": 0}

            def nxt(k, n):
                v = cnt[k] % n
                cnt[k] += 1
                return v

            groups = []
            groups += [("fv", 2048 + 512 * g, g) for g in range(2)]
            groups += [("mq", 3080, 0), ("mk", 3592, 0)]
            groups += [("mv", 4104 + 512 * g, g) for g in range(2)]
            groups += [("mo", 5128 + 512 * g, g) for g in range(2)]
            groups += [("fq", 512 * g, g) for g in range(2)]
            groups += [("fk", 1024 + 512 * g, g) for g in range(2)]
            groups += [("ga", 6160 + 512 * g, g) for g in range(4)]
            groups += [("gm", 8208 + 512 * g, g) for g in range(4)]

            def sm_src(c0, c1, lo, hi):
                return w_in[l, :, lo:hi].rearrange("(c p) n -> p c n", p=128)
            with nc.allow_non_contiguous_dma(reason="tiny gate columns"):
                for (dlo, lo, hi) in ((0, 3072, 3080), (8, 6156, 6160), (12, 6152, 6156)):
                    P.add("sp", lambda e, dlo=dlo, lo=lo, hi=hi: e.dma_start(
                        out=wsm_stg[:, :, dlo:dlo + hi - lo],
                        in_=w_in[l, :, lo:hi].rearrange("(c p) n -> p c n", p=128)),
                        r=[B_w], w=[Bwsm_stg], dma=True)
            for k in range(KC):
                P.add("dve", lambda e, k=k: e.tensor_copy(out=wsm[:, k, :], in_=wsm_stg[:, k, :]),
                      r=[Bwsm_stg], w=[Bwsm])
            gsrcs = [dict(src_fn=lambda k0, k1, c0=c0: w_in[l, k0 * 128:k1 * 128, c0:c0 + 512].rearrange(
                "(c p) n -> p c n", p=128)) for (kind, c0, g) in groups]
            first_tile = ws.tile(**gsrcs[0])
            first_tile.finish()
            norm_it = phase_norm(src_ap, Bsrc, r0, hT, BhT, g1_d[l:l + 1, :])
            zb = sm[:, 0:16]
            e1 = sm[:, 16:32]
            sp_ = sm[:, 32:48]
            t4 = sm[:, 48:52]

            def smA(i):
                tok = slice(i * 128, (i + 1) * 128)
                mm_acc(psum[7][:, 0:16], Bps[7], [(hT[:, c, tok], wsm[:, c, :]) for c in range(KC)], [BhT[i], Bwsm])
                P.add("dve", lambda e: e.tensor_tensor(out=zb, in0=psum[7][:, 0:16], in1=gb, op=ALU.add),
                      r=[Bps[7], Bpar], w=[Bsm])
                P.add("act", lambda e: e.activation(out=e1, in_=zb, func=AF.Exp, scale=-1.0), r=[Bsm], w=[Bsm])
                P.add("act", lambda e: e.activation(out=sp_, in_=e1, func=AF.Ln, bias=one_t[:, 0:1]), r=[Bsm], w=[Bsm])

            def smB(i):
                gi = j * HT + i
                P.add("pe", lambda e: e.matmul(psum[7][:, 32:44], tri_f, sp_[:, 0:12], start=True, stop=True),
                      r=[Bsm, B_cst], w=[Bps[7]])
                P.add("pe", lambda e: e.matmul(psum[7][:, 64:76], ones_f, sp_[:, 0:12], start=True, stop=True),
                      r=[Bsm, B_cst], w=[Bps[7]])
                cum = psum[7][:, 32:44]
                P.add("act", lambda e, i=i: e.activation(out=E_all[:, i, :], in_=cum[:, 0:8], func=AF.Exp),
                      r=[Bps[7]], w=[Bgate])
                P.add("act", lambda e, i=i: e.activation(out=QS_all[:, i, :], in_=cum[:, 8:12], func=AF.Exp,
                                                         scale=-1.0, bias=lnsc_t[:, 0:1]), r=[Bps[7]], w=[Bgate])
                P.add("dve", lambda e: e.tensor_tensor(out=t4, in0=cum[:, 8:12], in1=zb[:, 12:16], op=ALU.add),
                      r=[Bps[7], Bsm], w=[Bsm])
                P.add("act", lambda e, i=i: e.activation(out=KS_all[:, i, :], in_=t4, func=AF.Exp),
                      r=[Bsm], w=[Bgate])
                P.add("act", lambda e: e.copy(out=tsb[:, 0:12], in_=psum[7][:, 64:76]), r=[Bps[7]], w=[Btsb])
                P.add("sp", lambda e, gi=gi: e.dma_start(out=tot_s[gi:gi + 1, :], in_=tsb[0:1, 0:12]),
                      r=[Btsb], w=[B["tot_s"]], dma=True)

            for _ in norm_it:
                pass
            for i in range(HT):
                smA(i)
                smB(i)
            for gn, wb, Bwb, tick in pipelined(ws, gsrcs, 16, first=first_tile):
                kind, c0, g = groups[gn]
                if kind in ("fv", "mq", "mk", "mv", "mo"):
                  def tm_unit(i, wb=wb, Bwb=Bwb, kind=kind, g=g):
                    if True:
                        gi = j * HT + i
                        tok = slice(i * 128, (i + 1) * 128)
                        pb = nxt("ps", 4)
                        ps = psum[pb]
                        mm_acc(ps[:, :], Bps[pb], [(hT[:, c, tok], wb[:, c, :]) for c in range(KC)], [BhT[i], Bwb])
                        if kind == "fv":
                            s = nxt("st", NST)
                            h0 = g * 4
                            P.add("dve", lambda e, s=s, ps=ps, i=i, h0=h0: e.tensor_tensor(
                                out=st_bf[s][:, :, 0:128], in0=ps[:, :].rearrange("p (h d) -> p h d", h=4),
                                in1=E_all[:, i, h0:h0 + 4].unsqueeze(2).to_broadcast([128, 4, 128]), op=ALU.mult),
                                r=[Bps[pb], Bgate], w=[Bst_bf[s]])
                            P.add("dve", lambda e, s=s, i=i, h0=h0: e.tensor_copy(
                                out=st_bf[s][:, :, 128:129], in_=E_all[:, i, h0:h0 + 4].unsqueeze(2)),
                                r=[Bgate], w=[Bst_bf[s]])
                            P.add("sp", lambda e, s=s, h0=h0, gi=gi: e.dma_start(
                                out=vp_s[h0:h0 + 4, :, gi, :].rearrange("h p c -> p h c"),
                                in_=st_bf[s][:, :, 0:129]), r=[Bst_bf[s]], w=[B["vp_s"]], dma=True)
                        elif kind in ("mq", "mk"):
                            s = nxt("st2", NST)
                            SC = QS_all if kind == "mq" else KS_all
                            P.add("dve", lambda e, s=s, ps=ps, i=i, SC=SC: e.tensor_tensor(
                                out=st2_bf[s][:, :].rearrange("p (h d) -> p h d", h=4),
                                in0=ps[:, :].rearrange("p (h d) -> p h d", h=4),
                                in1=SC[:, i, 0:4].unsqueeze(2).to_broadcast([128, 4, 128]), op=ALU.mult),
                                r=[Bps[pb], Bgate], w=[Bst2_bf[s]])
                            if kind == "mk":
                                P.add("sp", lambda e, s=s, gi=gi: e.dma_start(out=mk_s[gi], in_=st2_bf[s]),
                                      r=[Bst2_bf[s]], w=[B["mk_s"]], dma=True)
                            tb = 4 + (i % 2)
                            pv = ps_bf(tb)
                            for hh in range(4):
                                P.add("pe", lambda e, s=s, hh=hh, pv=pv: e.transpose(
                                    out=pv[:, hh * 128:(hh + 1) * 128], in_=st2_bf[s][:, hh * 128:(hh + 1) * 128],
                                    identity=ident_b), r=[Bst2_bf[s], B_cst], w=[Bps[tb]])
                            s2 = nxt("st2", NST)
                            P.add("act", lambda e, s2=s2, pv=pv: e.copy(out=st2_bf[s2], in_=pv[:, 0:512]),
                                  r=[Bps[tb]], w=[Bst2_bf[s2]])
                            dstT = mqT_s if kind == "mq" else mkT_s
                            Bd = B["mqT_s"] if kind == "mq" else B["mkT_s"]
                            P.add("sp", lambda e, s2=s2, dstT=dstT, gi=gi: e.dma_start(
                                out=dstT[:, :, gi * 128:(gi + 1) * 128].rearrange("h p t -> p h t"),
                                in_=st2_bf[s2][:, :].rearrange("p (h t) -> p h t", h=4)),
                                r=[Bst2_bf[s2]], w=[Bd], dma=True)
                        elif kind == "mv":
                            s = nxt("mv", 2)
                            P.add("act", lambda e, s=s, ps=ps: e.copy(
                                out=mvst[s][:, :, 0:256], in_=ps[:, :].rearrange("p (h d) -> p h d", h=2)),
                                r=[Bps[pb]], w=[Bmvst[s]])
                            P.add("sp", lambda e, s=s, gi=gi, g=g: e.dma_start(
                                out=mv_s[gi, :, 2 * g:2 * g + 2, :], in_=mvst[s]),
                                r=[Bmvst[s]], w=[B["mv_s"]], dma=True)
                        else:
                            s = nxt("st2", NST)
                            P.add("act", lambda e, s=s, ps=ps: e.activation(out=st2_bf[s], in_=ps[:, :],
                                                                            func=AF.Sigmoid),
                                  r=[Bps[pb]], w=[Bst2_bf[s]])
                            P.add("sp", lambda e, s=s, gi=gi, g=g: e.dma_start(
                                out=mo_s[gi, :, 512 * g:512 * (g + 1)], in_=st2_bf[s]),
                                r=[Bst2_bf[s]], w=[B["mo_s"]], dma=True)
                  for i in ([HT - 1] if (trim and kind in ("mq", "mo")) else range(HT)):
                      tick(i)
                      tm_unit(i)
                else:
                    for q in range(4):
                        ch = g * 4 + q
                        for tg in ([TG - 1] if (trim and kind in ("fq", "ga", "gm")) else range(TG)):
                            tick(q * TG + tg)
                            tks = slice(tg * 512, (tg + 1) * 512)
                            gt0 = r0 + tg * 512
                            pb = nxt("ps", 4)
                            ps = psum[pb]
                            mm_acc(ps[:, :], Bps[pb], [(wb[:, c, q * 128:(q + 1) * 128], hT[:, c, tks])
                                                       for c in range(KC)], BhT[4 * tg:4 * tg + 4] + [Bwb])
                            if kind in ("ga", "gm"):
                                s = nxt("st2", NST)
                                P.add("act", lambda e, s=s, ps=ps: e.activation(out=st2_bf[s], in_=ps[:, :],
                                                                                func=AF.Sigmoid),
                                      r=[Bps[pb]], w=[Bst2_bf[s]])
                                dstT = gaT_s if kind == "ga" else gmT_s
                                Bd = B["gaT_s"] if kind == "ga" else B["gmT_s"]
                                P.add("sp", lambda e, s=s, dstT=dstT, ch=ch, gt0=gt0: e.dma_start(
                                    out=dstT[ch, :, gt0:gt0 + 512], in_=st2_bf[s]),
                                    r=[Bst2_bf[s]], w=[Bd], dma=True)
                            else:
                                s = nxt("st2", NST)
                                tf = nxt("tf", 2)
                                P.add("act", lambda e, s=s, ps=ps: e.activation(out=st2_bf[s], in_=ps[:, :],
                                                                                func=AF.Square),
                                      r=[Bps[pb]], w=[Bst2_bf[s]])
                                P.add("pe", lambda e, s=s: e.matmul(psum[6][:, :], ones_b, st2_bf[s],
                                                                    start=True, stop=True),
                                      r=[Bst2_bf[s], B_cst], w=[Bps[6]])
                                P.add("act", lambda e, tf=tf: e.activation(out=tmp_f[tf], in_=psum[6][:, :],
                                                                           func=AF.Sqrt, scale=1.0 / 128,
                                                                           bias=eps_t[:, 0:1]),
                                      r=[Bps[6]], w=[Btmp_f[tf]])
                                P.add("dve", lambda e, tf=tf: e.reciprocal(out=tmp_f[tf], in_=tmp_f[tf]),
                                      r=[Btmp_f[tf]], w=[Btmp_f[tf]])
                                s2 = nxt("st2", NST)
                                if kind == "fq":
                                    P.add("dve", lambda e, s2=s2, ps=ps, tf=tf: e.scalar_tensor_tensor(
                                        out=st2_bf[s2], in0=ps[:, :], scalar=qkg[:, 2:3], in1=tmp_f[tf],
                                        op0=ALU.mult, op1=ALU.mult), r=[Bps[pb], Btmp_f[tf], Bpar], w=[Bst2_bf[s2]])
                                else:
                                    P.add("dve", lambda e, s2=s2, ps=ps, tf=tf: e.tensor_tensor(
                                        out=st2_bf[s2], in0=ps[:, :], in1=tmp_f[tf], op=ALU.mult),
                                        r=[Bps[pb], Btmp_f[tf]], w=[Bst2_bf[s2]])
                                dstT = qT_s if kind == "fq" else kT_s
                                Bd = B["qT_s"] if kind == "fq" else B["kT_s"]
                                P.add("sp", lambda e, s2=s2, dstT=dstT, ch=ch, gt0=gt0: e.dma_start(
                                    out=dstT[ch, :, gt0:gt0 + 512], in_=st2_bf[s2]),
                                    r=[Bst2_bf[s2]], w=[Bd], dma=True)
            P.barrier()


        def phase2(l, trim=False):
            A.reset(KEEP2)
            QG = S // 512
            totb = A.alloc([128, NT, 12], F32)
            fcn = A.alloc([128, NT, 8], F32)
            Btot = P.buf("totb")
            qkg = A.alloc([128, 4], F32)
            bnd = A.alloc([128, 4], F32)
            Bpar = P.buf("par2")
            P.add("sp", lambda e: e.dma_start(out=totb, in_=tot_s.partition_broadcast(128)),
                  r=[B["tot_s"]], w=[Btot], dma=True)
            P.add("sp", lambda e: e.dma_start(out=qkg[:, 0:2], in_=qkg_d[l]), r=[B_w], w=[Bpar], dma=True)
            P.add("dve", lambda e: e.scalar_tensor_tensor(out=qkg[:, 2:3], in0=qkg[:, 0:1], scalar=float(128 ** -0.5),
                                                          in1=qkg[:, 1:2], op0=ALU.mult, op1=ALU.mult),
                  r=[Bpar], w=[Bpar])
            P.add("pe", lambda e: e.transpose(out=psum[6][0:1, 0:128], in_=qkg[:, 2:3], identity=cst[:, 0, :]),
                  r=[Bpar, B_cst], w=[Bps[6]])
            P.add("dve", lambda e: e.tensor_reduce(out=bnd[0:1, 0:1], in_=psum[6][0:1, 0:128], axis=AX.X, op=ALU.max,
                                                   apply_absolute_value=True), r=[Bps[6]], w=[Bpar])
            P.add("dve", lambda e: e.tensor_scalar(out=bnd[0:1, 1:2], in0=bnd[0:1, 0:1], scalar1=128.0, scalar2=None,
                                                   op0=ALU.mult), r=[Bpar], w=[Bpar])
            P.add("pe", lambda e: e.matmul(psum[6][:, 128:129], ones_f[0:1, :], bnd[0:1, 1:2], start=True, stop=True),
                  r=[Bpar, B_cst], w=[Bps[6]])
            P.add("dve", lambda e: e.tensor_copy(out=bnd[:, 2:3], in_=psum[6][:, 128:129]), r=[Bps[6]], w=[Bpar])
            P.add("pool", lambda e: e.memset(fcn[:, 0, :], 0.0), w=[Btot])
            for i in range(1, NT):
                P.add("dve", lambda e, i=i: e.tensor_tensor(out=fcn[:, i, :], in0=fcn[:, i - 1, :],
                                                            in1=totb[:, i - 1, 0:8], op=ALU.add), r=[Btot], w=[Btot])
            NS = 2
            kT = [A.alloc([128, S], BF16) for _ in range(NS)]
            qT = [A.alloc([128, S], BF16) for _ in range(NS)]
            vp = [A.alloc([128, NT, 129], BF16) for _ in range(NS)]
            vpm = [A.alloc([128, NT // 2, 129], BF16) for _ in range(NS)]
            Bvpm = [P.buf("a_vpm%d" % i) for i in range(NS)]
            Dt = [A.alloc([128, NT, NT], F32) for _ in range(NS)]
            Dhi = [A.alloc([128, NT, NT], BF16) for _ in range(NS)]
            Dlo = [A.alloc([128, NT, NT], F32) for _ in range(NS)]
            DX = [A.alloc([128, NT, NT], BF16) for _ in range(NS)]
            BkT = [P.buf("a_kT%d" % i) for i in range(NS)]
            BqT = [P.buf("a_qT%d" % i) for i in range(NS)]
            Bvp = [P.buf("a_vp%d" % i) for i in range(NS)]
            BDt = [P.buf("a_D%d" % i) for i in range(NS)]
            NPT = 3
            pt = [A.alloc([128, 512], BF16) for _ in range(NPT)]
            Bpt = [P.buf("a_pt%d" % i) for i in range(NPT)]
            ao = [A.alloc([128, 512], BF16) for _ in range(2)]
            Bao = [P.buf("a_ao%d" % i) for i in range(2)]
            aoT = [A.alloc([128, 512], BF16) for _ in range(2)]
            BaoT = [P.buf("a_aoT%d" % i) for i in range(2)]
            rc = A.alloc([128, 8], F32)
            Brc = P.buf("a_rc")
            ptc = 0
            stc = 0
            for h in range(NH):
                s = h % NS
                P.add("sp", lambda e, s=s, h=h: e.dma_start(out=kT[s], in_=kT_s[h]), r=[B["kT_s"]], w=[BkT[s]], dma=True)
                P.add("sp", lambda e, s=s, h=h: e.dma_start(out=qT[s], in_=qT_s[h]), r=[B["qT_s"]], w=[BqT[s]], dma=True)
                P.add("sp", lambda e, s=s, h=h: e.dma_start(out=vp[s], in_=vp_s[h]), r=[B["vp_s"]], w=[Bvp[s]], dma=True)
                P.add("dve", lambda e, s=s: e.tensor_scalar(out=vpm[s], in0=vp[s][:, 0:NT // 2, :], scalar1=flg[:, 0:1],
                                                            scalar2=None, op0=ALU.mult),
                      r=[Bvp[s], B_cst], w=[Bvpm[s]])
                P.add("dve", lambda e, s=s, h=h: e.scalar_tensor_tensor(
                    out=Dt[s], in0=fcn[:, :, h].unsqueeze(1).to_broadcast([128, NT, NT]), scalar=bnd[:, 2:3],
                    in1=fcn[:, :, h].unsqueeze(2).to_broadcast([128, NT, NT]), op0=ALU.subtract, op1=ALU.subtract),
                    r=[Btot, Bpar], w=[BDt[s]])
                P.add("dve", lambda e, s=s: e.tensor_copy(out=Dhi[s], in_=Dt[s]), r=[BDt[s]], w=[BDt[s]])
                P.add("dve", lambda e, s=s: e.tensor_tensor(out=Dlo[s], in0=Dt[s], in1=Dhi[s], op=ALU.subtract),
                      r=[BDt[s]], w=[BDt[s]])
                P.add("dve", lambda e, s=s: e.tensor_scalar(out=Dlo[s], in0=Dlo[s], scalar1=cst[:, 0, 1:2],
                                                            scalar2=None, op0=ALU.mult), r=[BDt[s], B_cst], w=[BDt[s]])
                P.add("dve", lambda e, s=s: e.scalar_tensor_tensor(
                    out=DX[s], in0=Dhi[s], scalar=cst[:, 0, 0:1], in1=Dlo[s], op0=ALU.mult, op1=ALU.add),
                    r=[BDt[s], B_cst], w=[BDt[s]])
                steps = [(g, j) for g in range(QG) for j in range(4 * g + 4) if not (trim and g < QG // 2 - 1)]

                def emit_qk(st):
                    g, j = steps[st]
                    i0 = 4 * g
                    ilo = max(i0, j)
                    ncol = (i0 + 4 - ilo) * 128
                    qlo = ilo * 128
                    sb = 4 + (st % 2)
                    P.add("pe", lambda e, s=s, j=j, sb=sb, qlo=qlo, ncol=ncol: e.matmul(
                        psum[sb][:, 0:ncol], kT[s][:, j * 128:(j + 1) * 128], qT[s][:, qlo:qlo + ncol],
                        start=True, stop=False), r=[BkT[s], BqT[s]], w=[Bps[sb]])
                    nseg = i0 + 4 - ilo
                    P.add("pe", lambda e, s=s, j=j, sb=sb, ilo=ilo, nseg=nseg, ncol=ncol: e.matmul(
                        psum[sb][:, 0:ncol], ones_b,
                        DX[s][:, ilo:ilo + nseg, j:j + 1].to_broadcast([128, nseg, 128]),
                        start=False, stop=True), r=[BDt[s], B_cst], w=[Bps[sb]])

                emit_qk(0)
                for st, (g, j) in enumerate(steps):
                    if True:
                        i0 = 4 * g
                        ilo = max(i0, j)
                        sb = 4 + (st % 2)
                        p = ptc % NPT
                        ptc += 1
                        if st + 1 < len(steps):
                            emit_qk(st + 1)
                        ncol = (i0 + 4 - ilo) * 128
                        P.add("act", lambda e, p=p, sb=sb, ncol=ncol: e.activation(
                            out=pt[p][:, 0:ncol], in_=psum[sb][:, 0:ncol], func=AF.Exp),
                            r=[Bps[sb]], w=[Bpt[p]])
                        for i in range(ilo, i0 + 4):
                            c0 = (i - ilo) * 128
                            if i == j:
                                P.add("dve", lambda e, p=p, c0=c0: e.tensor_tensor(
                                    out=pt[p][:, c0:c0 + 128], in0=pt[p][:, c0:c0 + 128], in1=tri_b, op=ALU.mult),
                                    r=[Bpt[p], B_cst], w=[Bpt[p]])
                        for i in range(ilo, i0 + 4):
                            c0 = (i - ilo) * 128
                            ab = i - i0
                            msk = (i >= NT // 2 and j < NT // 2)
                            vsrc = vpm[s] if msk else vp[s]
                            Bv = Bvpm[s] if msk else Bvp[s]
                            P.add("pe", lambda e, vsrc=vsrc, p=p, c0=c0, ab=ab, i=i, j=j: e.matmul(
                                psum[ab][:, 0:129], pt[p][:, c0:c0 + 128], vsrc[:, j, :],
                                start=(j == 0), stop=(j == i)), r=[Bpt[p], Bv], w=[Bps[ab]])
                    if j != 4 * g + 3:
                        continue
                    a = stc % 2
                    stc += 1
                    for m in range(4):
                        P.add("dve", lambda e, m=m: e.reciprocal(out=rc[:, m:m + 1], in_=psum[m][:, 128:129]),
                              r=[Bps[m]], w=[Brc])
                        P.add("dve", lambda e, m=m, a=a: e.tensor_scalar(
                            out=ao[a][:, m * 128:(m + 1) * 128], in0=psum[m][:, 0:128], scalar1=rc[:, m:m + 1],
                            scalar2=None, op0=ALU.mult), r=[Bps[m], Brc], w=[Bao[a]])
                    pv = ps_bf(6)
                    for m in range(4):
                        P.add("pe", lambda e, m=m, a=a, pv=pv: e.transpose(
                            out=pv[:, m * 128:(m + 1) * 128], in_=ao[a][:, m * 128:(m + 1) * 128], identity=ident_b),
                            r=[Bao[a], B_cst], w=[Bps[6]])
                    P.add("dve", lambda e, a=a, pv=pv: e.tensor_copy(out=aoT[a], in_=pv[:, 0:512]),
                          r=[Bps[6]], w=[BaoT[a]])
                    P.add("sp", lambda e, a=a, h=h, g=g: e.dma_start(out=aoT_s[h, :, g * 512:(g + 1) * 512],
                                                                     in_=aoT[a]),
                          r=[BaoT[a]], w=[B["aoT_s"]], dma=True)
            P.barrier()

        def phase3(l, trim=False):
            A.reset(KEEP2)
            totb = A.alloc([128, NT, 12], F32)
            eL = A.alloc([128, NT, 4], F32)
            Btot = P.buf("m_tot")
            gbc = A.alloc([128, 1024], F32)
            Bg = P.buf("m_g")
            P.add("sp", lambda e: e.dma_start(out=totb, in_=tot_s.partition_broadcast(128)),
                  r=[B["tot_s"]], w=[Btot], dma=True)
            P.add("sp", lambda e: e.dma_start(out=gbc, in_=mng_d[l]), r=[B_w], w=[Bg], dma=True)
            P.add("act", lambda e: e.activation(out=eL, in_=totb[:, :, 8:12], func=AF.Exp, scale=-1.0),
                  r=[Btot], w=[Btot])
            C32 = A.alloc([128, MH, 257], F32)
            Cb = A.alloc([128, MH, 257], BF16)
            BC32, BCb = P.buf("m_C32"), P.buf("m_Cb")
            NS = 2
            qT4 = [A.alloc([128, MH, 128], BF16) for _ in range(NS)]
            kT4 = [A.alloc([128, MH, 128], BF16) for _ in range(NS)]
            k4 = [A.alloc([128, 512], BF16) for _ in range(NS)]
            v4 = [A.alloc([128, MH, 257], BF16) for _ in range(NS)]
            o4 = [A.alloc([128, 1024], BF16) for _ in range(NS)]
            Bin = [P.buf("m_in%d" % i) for i in range(NS)]
            wt = [A.alloc([128, MH, 128], BF16) for _ in range(2)]
            Bwt = [P.buf("m_wt%d" % i) for i in range(2)]
            tmpU = [A.alloc([128, 257], F32) for _ in range(2)]
            BtmpU = [P.buf("m_tmpU%d" % i) for i in range(2)]
            sm = A.alloc([128, 16], F32)
            Bsm = P.buf("m_sm")
            junk = A.alloc([128, 256], BF16)
            Bjunk = P.buf("m_junk")
            mo = [A.alloc([128, 1024], BF16) for _ in range(2)]
            Bmo = [P.buf("m_mo%d" % i) for i in range(2)]
            moT = [A.alloc([128, 8, 128], BF16) for _ in range(2)]
            BmoT = [P.buf("m_moT%d" % i) for i in range(2)]
            uc = 0
            BC32h = [P.buf("m_C32_%d" % h) for h in range(MH)]
            BCbh = [P.buf("m_Cb_%d" % h) for h in range(MH)]
            for h in range(MH):
                P.add("pool", lambda e, h=h: e.memset(C32[:, h, :], 0.0), w=[BC32h[h]])
                P.add("pool", lambda e, h=h: e.memset(Cb[:, h, :], 0.0), w=[BCbh[h]])
            dn = A.alloc([128, 32], F32)
            Bdn, Bss = P.buf("m_dn"), P.buf("m_ss")
            t2 = [A.alloc([128, 256], F32) for _ in range(4)]
            Bt2 = [P.buf("m_t2%d" % i) for i in range(4)]
            pending = []
            for c in range(NT):
                s = c % NS
                tok = slice(c * 128, (c + 1) * 128)
                if c == NT // 2:
                    for h in range(MH):
                        P.add("dve", lambda e, h=h: e.tensor_scalar(out=C32[:, h, :], in0=C32[:, h, :],
                                                                    scalar1=flg[:, 0:1], scalar2=None, op0=ALU.mult),
                              r=[BC32h[h], B_cst], w=[BC32h[h]])
                        P.add("act", lambda e, h=h: e.copy(out=Cb[:, h, :], in_=C32[:, h, :]),
                              r=[BC32h[h]], w=[BCbh[h]])
                P.add("sp", lambda e, s=s, tok=tok: e.dma_start(
                    out=qT4[s], in_=mqT_s[:, :, tok].rearrange("h p t -> p h t")), r=[B["mqT_s"]], w=[Bin[s]], dma=True)
                P.add("sp", lambda e, s=s, tok=tok: e.dma_start(
                    out=kT4[s], in_=mkT_s[:, :, tok].rearrange("h p t -> p h t")), r=[B["mkT_s"]], w=[Bin[s]], dma=True)
                P.add("sp", lambda e, s=s, c=c: e.dma_start(out=k4[s], in_=mk_s[c]), r=[B["mk_s"]], w=[Bin[s]], dma=True)
                P.add("sp", lambda e, s=s, c=c: e.dma_start(out=v4[s], in_=mv_s[c]), r=[B["mv_s"]], w=[Bin[s]], dma=True)
                P.add("sp", lambda e, s=s, c=c: e.dma_start(out=o4[s], in_=mo_s[c]), r=[B["mo_s"]], w=[Bin[s]], dma=True)
                w_ = c % 2
                need_out = not (trim and c < NT // 2 - 1)
                if need_out:
                    for h in range(MH):
                        P.add("pe", lambda e, s=s, h=h: e.matmul(psum[4][:, h * 128:(h + 1) * 128], kT4[s][:, h, :],
                                                                 qT4[s][:, h, :], start=True, stop=True),
                              r=[Bin[s]], w=[Bps[4]])
                    P.add("dve", lambda e, w_=w_: e.tensor_tensor(
                        out=wt[w_], in0=psum[4][:, :].rearrange("p (h t) -> p h t", h=MH),
                        in1=tri_b.unsqueeze(1).to_broadcast([128, MH, 128]), op=ALU.mult),
                        r=[Bps[4], B_cst], w=[Bwt[w_]])
                    for h in range(MH):
                        hb = h // 2
                        hc = (h % 2) * 256
                        P.add("pe", lambda e, s=s, h=h, w_=w_, hb=hb, hc=hc: e.matmul(
                            psum[hb][:, hc:hc + 256], wt[w_][:, h, :], v4[s][:, h, 0:256], start=True, stop=False),
                            r=[Bwt[w_], Bin[s]], w=[Bps[hb]])
                        P.add("pe", lambda e, s=s, h=h, hb=hb, hc=hc: e.matmul(
                            psum[hb][:, hc:hc + 256], qT4[s][:, h, :], Cb[:, h, 0:256], start=False, stop=True),
                            r=[Bin[s], BCbh[h]], w=[Bps[hb]])
                    for h in range(MH):
                        P.add("pe", lambda e, s=s, h=h, w_=w_: e.matmul(
                            psum[2][:, h:h + 1], wt[w_][:, h, :], v4[s][:, h, 256:257], start=True, stop=False),
                            r=[Bwt[w_], Bin[s]], w=[Bps[2]])
                        P.add("pe", lambda e, s=s, h=h: e.matmul(
                            psum[2][:, h:h + 1], qT4[s][:, h, :], Cb[:, h, 256:257], start=False, stop=True),
                            r=[Bin[s], BCbh[h]], w=[Bps[2]])
                for h in range(MH):
                    ub = 5 + (uc % 2)
                    u = uc % 2
                    uc += 1
                    P.add("pe", lambda e, s=s, h=h, ub=ub: e.matmul(psum[ub][:, 0:257], k4[s][:, h * 128:(h + 1) * 128],
                                                                    v4[s][:, h, :], start=True, stop=True),
                          r=[Bin[s]], w=[Bps[ub]])
                    P.add("dve", lambda e, u=u, ub=ub, h=h: e.tensor_tensor(
                        out=tmpU[u], in0=psum[ub][:, 0:257], in1=C32[:, h, :], op=ALU.add),
                        r=[Bps[ub], BC32h[h]], w=[BtmpU[u]])
                    P.add("act", lambda e, u=u, c=c, h=h: e.activation(
                        out=Cb[:, h, :], in_=tmpU[u], func=AF.Copy, scale=eL[:, c, h:h + 1]),
                        r=[BtmpU[u], Btot], w=[BCbh[h]])
                    P.add("dve", lambda e, u=u, c=c, h=h: e.tensor_scalar(
                        out=C32[:, h, :], in0=tmpU[u], scalar1=eL[:, c, h:h + 1], scalar2=None, op0=ALU.mult),
                        r=[BtmpU[u], Btot], w=[BC32h[h]])
                for fn in pending:
                    fn()
                pending = []
                if need_out:
                    P.add("dve", lambda e: e.tensor_reduce(out=dn[:, 0:4], in_=psum[2][:, 0:4].unsqueeze(2), axis=AX.X,
                                                           op=ALU.max, apply_absolute_value=True), r=[Bps[2]], w=[Bdn])
                    P.add("dve", lambda e: e.tensor_scalar(out=dn[:, 4:8], in0=dn[:, 0:4], scalar1=1.0, scalar2=None,
                                                           op0=ALU.max), r=[Bdn], w=[Bdn])
                    P.add("dve", lambda e: e.reciprocal(out=dn[:, 8:12], in_=dn[:, 4:8]), r=[Bdn], w=[Bdn])
                    for h in range(MH):
                        hb = h // 2
                        hc = (h % 2) * 256
                        P.add("act", lambda e, h=h, hb=hb, hc=hc: e.activation(
                            out=junk, in_=psum[hb][:, hc:hc + 256], func=AF.Square, scale=dn[:, 8 + h:9 + h],
                            accum_out=dn[:, 12 + h:13 + h]), r=[Bps[hb], Bdn], w=[Bjunk, Bss])
                    P.add("act", lambda e: e.activation(out=dn[:, 16:20], in_=dn[:, 12:16], func=AF.Sqrt, scale=1.0 / 256,
                                                        bias=eps_t[:, 0:1]), r=[Bss], w=[Bss])
                    P.add("dve", lambda e: e.reciprocal(out=dn[:, 20:24], in_=dn[:, 16:20]), r=[Bss], w=[Bss])
                    P.add("dve", lambda e: e.tensor_tensor(out=dn[:, 24:28], in0=dn[:, 20:24], in1=dn[:, 8:12], op=ALU.mult),
                          r=[Bss, Bdn], w=[Bss])
                    for h in range(MH):
                        hb = h // 2
                        hc = (h % 2) * 256
                        P.add("dve", lambda e, h=h, hb=hb, hc=hc: e.scalar_tensor_tensor(
                            out=t2[h], in0=psum[hb][:, hc:hc + 256], scalar=dn[:, 24 + h:25 + h],
                            in1=gbc[:, h * 256:(h + 1) * 256], op0=ALU.mult, op1=ALU.mult),
                            r=[Bps[hb], Bss, Bg], w=[Bt2[h]])
                        P.add("pool", lambda e, h=h, w_=w_, s=s: e.tensor_tensor(
                            out=mo[w_][:, h * 256:(h + 1) * 256], in0=t2[h], in1=o4[s][:, h * 256:(h + 1) * 256],
                            op=ALU.mult), r=[Bt2[h], Bin[s]], w=[Bmo[w_]])

                    def do_transposes(w_=w_, tok=tok):
                        pv = ps_bf(7)
                        for q in range(8):
                            P.add("pe", lambda e, q=q, pv=pv: e.transpose(
                                out=pv[:, q * 128:(q + 1) * 128], in_=mo[w_][:, q * 128:(q + 1) * 128], identity=ident_b),
                                r=[Bmo[w_], B_cst], w=[Bps[7]])
                        P.add("act", lambda e, pv=pv: e.copy(out=moT[w_], in_=pv[:, 0:1024].rearrange(
                            "p (q t) -> p q t", q=8)), r=[Bps[7]], w=[BmoT[w_]])
                        P.add("sp", lambda e: e.dma_start(
                            out=moT_s[:, :, tok].rearrange("q p t -> p q t"), in_=moT[w_]),
                            r=[BmoT[w_]], w=[B["moT_s"]], dma=True)
                    pending.append(do_transposes)
            for fn in pending:
                fn()
            P.barrier()


        def phase4(l, j, src_ap, Bsrc, trim=False):
            A.reset(KEEP2)
            r0 = j * T
            mT = A.alloc([128, KC, T], BF16)
            BmT = P.buf("mT")
            mark = A.off
            TH = T // 2
            aoT = A.alloc([128, NH, TH], BF16)
            moT = A.alloc([128, NH, TH], BF16)
            Bao, Bmo = P.buf("p4_ao"), P.buf("p4_mo")
            ws = WStream(8, 512, nslots=4, name="p4w")
            gat = [A.alloc([128, TH], BF16) for _ in range(2)]
            gmt = [A.alloc([128, TH], BF16) for _ in range(2)]
            Bgat = [P.buf("p4_ga%d" % i) for i in range(2)]
            Bgmt = [P.buf("p4_gm%d" % i) for i in range(2)]
            t1 = [A.alloc([128, 512], F32) for _ in range(2)]
            t2 = [A.alloc([128, 512], F32) for _ in range(2)]
            Bt1 = [P.buf("p4_t1%d" % i) for i in range(2)]
            Bt2 = [P.buf("p4_t2%d" % i) for i in range(2)]
            seq = [(th, n) for th in ([1] if trim else range(2)) for n in range(4)]

            def mk(idx):
                th, n = seq[idx]
                ta = ws.tile(src_fn=lambda k0, k1, n=n: w_pa[l, k0 * 128:k1 * 128, n * 512:(n + 1) * 512].rearrange(
                    "(c p) n -> p c n", p=128))
                tm = ws.tile(src_fn=lambda k0, k1, n=n: w_pm[l, k0 * 128:k1 * 128, n * 512:(n + 1) * 512].rearrange(
                    "(c p) n -> p c n", p=128))
                return ta, tm

            def load_gates(th, dc):
                gs = dc % 2
                c0 = r0 + th * TH
                P.add("sp", lambda e, gs=gs, dc=dc, c0=c0: e.dma_start(out=gat[gs], in_=gaT_s[dc, :, c0:c0 + TH]),
                      r=[B["gaT_s"]], w=[Bgat[gs]], dma=True)
                P.add("sp", lambda e, gs=gs, dc=dc, c0=c0: e.dma_start(out=gmt[gs], in_=gmT_s[dc, :, c0:c0 + TH]),
                      r=[B["gmT_s"]], w=[Bgmt[gs]], dma=True)

            cur = mk(0)
            cur[0].finish()
            cur[1].finish()
            load_gates(seq[0][0], seq[0][1] * 4)
            k = 0
            for idx, (th, n) in enumerate(seq):
                if n == 0:
                    c0 = r0 + th * TH
                    P.add("sp", lambda e, c0=c0: e.dma_start(
                        out=aoT, in_=aoT_s[:, :, c0:c0 + TH].rearrange("h p t -> p h t")),
                        r=[B["aoT_s"]], w=[Bao], dma=True)
                    P.add("sp", lambda e, c0=c0: e.dma_start(
                        out=moT, in_=moT_s[:, :, c0:c0 + TH].rearrange("h p t -> p h t")),
                        r=[B["moT_s"]], w=[Bmo], dma=True)
                nxt_t = mk(idx + 1) if idx + 1 < len(seq) else None
                wa, Bwa = cur[0].wb, cur[0].Bwb
                wm, Bwm = cur[1].wb, cur[1].Bwb
                un = 0
                for q in range(4):
                    dc = n * 4 + q
                    gs = dc % 2
                    if q < 3:
                        load_gates(th, dc + 1)
                    elif idx + 1 < len(seq):
                        load_gates(seq[idx + 1][0], seq[idx + 1][1] * 4)
                    for tg in ([TH // 512 - 1] if trim else range(TH // 512)):
                        if nxt_t is not None and un % 2 == 0:
                            nxt_t[(un // 2) % 2].step()
                        un += 1
                        tks = slice(tg * 512, (tg + 1) * 512)
                        otk = slice(th * TH + tg * 512, th * TH + (tg + 1) * 512)
                        pa = (k % 3)
                        pm = 3 + (k % 3)
                        u = k % 2
                        k += 1
                        mm_acc(psum[pa][:, :], Bps[pa], [(wa[:, c, q * 128:(q + 1) * 128], aoT[:, c, tks])
                                                         for c in range(8)], [Bwa, Bao])
                        mm_acc(psum[pm][:, :], Bps[pm], [(wm[:, c, q * 128:(q + 1) * 128], moT[:, c, tks])
                                                         for c in range(8)], [Bwm, Bmo])
                        P.add("dve", lambda e, pa=pa, u=u, gs=gs, tks=tks: e.tensor_tensor(
                            out=t1[u], in0=psum[pa][:, :], in1=gat[gs][:, tks], op=ALU.mult),
                            r=[Bps[pa], Bgat[gs]], w=[Bt1[u]])
                        P.add("dve", lambda e, pm=pm, u=u, gs=gs, tks=tks: e.tensor_tensor(
                            out=t2[u], in0=psum[pm][:, :], in1=gmt[gs][:, tks], op=ALU.mult),
                            r=[Bps[pm], Bgmt[gs]], w=[Bt2[u]])
                        P.add("dve", lambda e, u=u, dc=dc, otk=otk: e.tensor_tensor(
                            out=mT[:, dc, otk], in0=t1[u], in1=t2[u], op=ALU.add),
                            r=[Bt1[u], Bt2[u]], w=[BmT])
                if nxt_t is not None:
                    nxt_t[0].finish()
                    nxt_t[1].finish()
                cur = nxt_t
            P.barrier()
            A.reset(mark)
            ws = WStream(KC, 512, name="p4o")
            xp = [A.alloc([128, 512], F32) for _ in range(3)]
            Bxp = [P.buf("p4_xp%d" % i) for i in range(3)]
            xo = [A.alloc([128, 512], F32) for _ in range(3)]
            Bxo = [P.buf("p4_xo%d" % i) for i in range(3)]
            osrcs = [dict(src_fn=lambda k0, k1, n=n: w_out[l, k0 * 128:k1 * 128, n * 512:(n + 1) * 512].rearrange(
                "(c p) n -> p c n", p=128)) for n in range(4)]

            tiles_i = [HT - 1] if trim else list(range(HT))
            units = [(n, i) for n in range(4) for i in tiles_i]

            def ld_x(kk):
                n, i = units[kk]
                u = kk % 3
                rows = slice(r0 + i * 128, r0 + (i + 1) * 128)
                cols = slice(n * 512, (n + 1) * 512)
                P.add("sp", lambda e, u=u, rows=rows, cols=cols: e.dma_start(out=xp[u], in_=src_ap[rows, cols]),
                      r=[Bsrc], w=[Bxp[u]], dma=True)

            ld_x(0)
            k = 0
            for n, wb, Bwb, tick in pipelined(ws, osrcs, len(tiles_i)):
                cols = slice(n * 512, (n + 1) * 512)
                for ui, i in enumerate(tiles_i):
                    tick(ui)
                    rows = slice(r0 + i * 128, r0 + (i + 1) * 128)
                    tok = slice(i * 128, (i + 1) * 128)
                    pb = k % 4
                    u = k % 3
                    if k + 1 < len(units):
                        ld_x(k + 1)
                    k += 1
                    mm_acc(psum[pb][:, :], Bps[pb], [(mT[:, c, tok], wb[:, c, :]) for c in range(KC)], [BmT, Bwb])
                    P.add("dve", lambda e, u=u, pb=pb: e.tensor_tensor(out=xo[u], in0=psum[pb][:, :], in1=xp[u],
                                                                        op=ALU.add), r=[Bps[pb], Bxp[u]], w=[Bxo[u]])
                    P.add("sp", lambda e, u=u, rows=rows, cols=cols: e.dma_start(out=x_mid[rows, cols], in_=xo[u]),
                          r=[Bxo[u]], w=[B["x_mid"]], dma=True)
            P.barrier()

        def phase5(l, j, dst_ap, Bdst, dst_off=0, halo=False):
            A.reset(KEEP2)
            r0 = j * T
            h2T = A.alloc([128, KC, T], BF16)
            Bh2T = [P.buf("h2T%d" % i) for i in range(HT)]
            mark = A.off
            cw = A.alloc([128, 2 * FC, 3], F32)
            cb = A.alloc([128, 2 * FC], F32)
            Bcw = P.buf("p5_cw")
            P.add("sp", lambda e: e.dma_start(out=cw, in_=cw_d[l]), r=[B_w], w=[Bcw], dma=True)
            P.add("sp", lambda e: e.dma_start(out=cb, in_=cb_d[l]), r=[B_w], w=[Bcw], dma=True)
            if j == 0:
                P.add("pool", lambda e: e.memset(hist, 0.0), w=[Bhist])
            elif not halo:
                P.add("dve", lambda e: e.tensor_scalar(out=hist, in0=hist, scalar1=flg[:, 0:1], scalar2=None,
                                                       op0=ALU.mult), r=[Bhist, B_cst], w=[Bhist])
            if halo:
                xh = A.alloc([2, D], F32)
                hh = A.alloc([2, D], BF16)
                hjunk = A.alloc([2, D], BF16)
                g2b = A.alloc([2, D], F32)
                hst = A.alloc([2, 4], F32)
                h2h = A.alloc([128, KC, 2], BF16)
                Bhalo = P.buf("p5_halo")
                P.add("sp", lambda e: e.dma_start(out=xh, in_=x_mid[r0 - 2:r0, :]), r=[B["x_mid"]], w=[Bhalo], dma=True)
                P.add("sp", lambda e: e.dma_start(out=g2b, in_=g2_d[l:l + 1, :].partition_broadcast(2)),
                      r=[B_w], w=[Bhalo], dma=True)
                P.add("act", lambda e: e.activation(out=hjunk, in_=xh, func=AF.Square, accum_out=hst[:, 0:1]),
                      r=[Bhalo], w=[Bhalo])
                P.add("act", lambda e: e.activation(out=hst[:, 1:2], in_=hst[:, 0:1], func=AF.Sqrt, scale=1.0 / D,
                                                    bias=eps_t[0:2, 0:1]), r=[Bhalo], w=[Bhalo])
                P.add("dve", lambda e: e.reciprocal(out=hst[:, 2:3], in_=hst[:, 1:2]), r=[Bhalo], w=[Bhalo])
                P.add("dve", lambda e: e.scalar_tensor_tensor(out=hh, in0=xh, scalar=hst[:, 2:3], in1=g2b,
                                                              op0=ALU.mult, op1=ALU.mult), r=[Bhalo], w=[Bhalo])
                pvh = ps_bf(6)
                for c in range(KC):
                    P.add("pe", lambda e, c=c: e.transpose(out=pvh[:, 2 * c:2 * c + 2],
                                                           in_=hh[0:2, c * 128:(c + 1) * 128],
                                                           identity=ident_b[0:2, 0:2]), r=[Bhalo, B_cst], w=[Bps[6]])
                P.add("dve", lambda e: e.tensor_copy(out=h2h, in_=pvh[:, 0:2 * KC].rearrange("p (c t) -> p c t", t=2)),
                      r=[Bps[6]], w=[Bhalo])
            ws = WStream(KC, 256, name="p5u")
            av = [[A.alloc([128, 512], F32) for _ in range(3)] for _ in range(2)]
            Bav = [[P.buf("p5_a%d%d" % (x, i)) for i in range(3)] for x in range(2)]
            sg = [A.alloc([128, 512], F32) for _ in range(3)]
            Bsg = [P.buf("p5_sg%d" % i) for i in range(3)]
            ast = [A.alloc([128, 512], BF16) for _ in range(3)]
            Bast = [P.buf("p5_ast%d" % i) for i in range(3)]
            k = 0
            usrcs = [dict(srcs=[
                (0, 128, lambda k0, k1, fc=fc: w_up[l, k0 * 128:k1 * 128, fc * 128:(fc + 1) * 128].rearrange(
                    "(c p) n -> p c n", p=128)),
                (128, 256, lambda k0, k1, fc=fc: w_up[l, k0 * 128:k1 * 128, DFF + fc * 128:DFF + (fc + 1) * 128].rearrange(
                    "(c p) n -> p c n", p=128))]) for fc in range(FC)]
            first_tile = ws.tile(**usrcs[0])
            first_tile.finish()
            norm_it = phase_norm(x_mid, B["x_mid"], r0, h2T, Bh2T, g2_d[l:l + 1, :])
            for _ in norm_it:
                pass
            for fc, wb, Bwb, tick in pipelined(ws, usrcs, TG, first=first_tile):
                if halo:
                    for x in range(2):
                        hb = 6 + x
                        fcx = x * FC + fc
                        mm_acc(psum[hb][:, 0:2], Bps[hb], [(wb[:, c, x * 128:(x + 1) * 128], h2h[:, c, :])
                                                           for c in range(KC)], [Bwb, Bhalo])
                        P.add("dve", lambda e, hb=hb, fcx=fcx: e.tensor_scalar(
                            out=hist[:, fcx, 0:2], in0=psum[hb][:, 0:2], scalar1=flg[:, 0:1], scalar2=None,
                            op0=ALU.mult), r=[Bps[hb], B_cst, Bhist], w=[Bhist])
                for tg in range(TG):
                    tick(tg)
                    tks = slice(tg * 512, (tg + 1) * 512)
                    u = k % 3
                    k3 = k
                    k += 1
                    first = (j == 0 and tg == 0)
                    for x in range(2):
                        pb = 3 * x + (k3 % 3)
                        fcx = x * FC + fc
                        a = av[x][u]
                        Ba = Bav[x][u]
                        mm_acc(psum[pb][:, :], Bps[pb], [(wb[:, c, x * 128:(x + 1) * 128], h2T[:, c, tks])
                                                         for c in range(KC)], [Bwb] + Bh2T[4 * tg:4 * tg + 4])
                        P.add("act", lambda e, a=a, pb=pb, fcx=fcx: e.activation(
                            out=a, in_=psum[pb][:, :], func=AF.Identity, scale=cw[:, fcx, 2:3],
                            bias=cb[:, fcx:fcx + 1]), r=[Bps[pb], Bcw], w=[Ba])
                        P.add("dve", lambda e, a=a, pb=pb, fcx=fcx: e.scalar_tensor_tensor(
                            out=a[:, 1:512], in0=psum[pb][:, 0:511], scalar=cw[:, fcx, 1:2], in1=a[:, 1:512],
                            op0=ALU.mult, op1=ALU.add), r=[Bps[pb], Bcw, Ba], w=[Ba])
                        P.add("dve", lambda e, a=a, pb=pb, fcx=fcx: e.scalar_tensor_tensor(
                            out=a[:, 2:512], in0=psum[pb][:, 0:510], scalar=cw[:, fcx, 0:1], in1=a[:, 2:512],
                            op0=ALU.mult, op1=ALU.add), r=[Bps[pb], Bcw, Ba], w=[Ba])
                        if not first:
                            P.add("dve", lambda e, a=a, fcx=fcx: e.scalar_tensor_tensor(
                                out=a[:, 0:1], in0=hist[:, fcx, 1:2], scalar=cw[:, fcx, 1:2], in1=a[:, 0:1],
                                op0=ALU.mult, op1=ALU.add), r=[Bhist, Bcw, Ba], w=[Ba])
                            P.add("dve", lambda e, a=a, fcx=fcx: e.scalar_tensor_tensor(
                                out=a[:, 0:2], in0=hist[:, fcx, 0:2], scalar=cw[:, fcx, 0:1], in1=a[:, 0:2],
                                op0=ALU.mult, op1=ALU.add), r=[Bhist, Bcw, Ba], w=[Ba])
                        P.add("dve", lambda e, pb=pb, fcx=fcx: e.tensor_copy(out=hist[:, fcx, 0:2],
                                                                              in_=psum[pb][:, 510:512]),
                              r=[Bps[pb], Bhist], w=[Bhist])
                    P.add("act", lambda e, u=u: e.activation(out=sg[u], in_=av[0][u], func=AF.Silu),
                          r=[Bav[0][u]], w=[Bsg[u]])
                    o = k % 3
                    P.add("pool", lambda e, u=u, o=o: e.tensor_tensor(out=ast[o], in0=sg[u], in1=av[1][u], op=ALU.mult),
                          r=[Bsg[u], Bav[1][u]], w=[Bast[o]])
                    ti0 = (r0 + tg * 512) // 128
                    P.add("sp", lambda e, o=o, fc=fc, ti0=ti0: e.dma_start(
                        out=actT_s[ti0:ti0 + 4, :, fc, :].rearrange("i p t -> p i t"),
                        in_=ast[o][:, :].rearrange("p (i t) -> p i t", i=4)),
                        r=[Bast[o]], w=[B["actT_s"]], dma=True)
            P.barrier()
            A.reset(KEEP2)
            ws = WStream(FC, 512, name="p5d")
            at = [A.alloc([128, FC, 128], BF16) for _ in range(2)]
            Bat = [P.buf("p5_at%d" % i) for i in range(2)]
            xp = [A.alloc([128, 512], F32) for _ in range(3)]
            Bxp = [P.buf("p5_xp%d" % i) for i in range(3)]
            xo = [A.alloc([128, 512], F32) for _ in range(3)]
            Bxo = [P.buf("p5_xo%d" % i) for i in range(3)]
            dsrcs = [dict(src_fn=lambda k0, k1, n=n: w_down[l, k0 * 128:k1 * 128, n * 512:(n + 1) * 512].rearrange(
                "(c p) n -> p c n", p=128)) for n in range(4)]

            def ld_in(kk):
                n, i = divmod(kk, HT)
                u = kk % 3
                a = kk % 2
                rows = slice(r0 + i * 128, r0 + (i + 1) * 128)
                cols = slice(n * 512, (n + 1) * 512)
                ti = (r0 + i * 128) // 128
                P.add("sp", lambda e, a=a, ti=ti: e.dma_start(out=at[a], in_=actT_s[ti]),
                      r=[B["actT_s"]], w=[Bat[a]], dma=True)
                P.add("sp", lambda e, u=u, rows=rows, cols=cols: e.dma_start(out=xp[u], in_=x_mid[rows, cols]),
                      r=[B["x_mid"]], w=[Bxp[u]], dma=True)

            ld_in(0)
            k = 0
            for n, wb, Bwb, tick in pipelined(ws, dsrcs, HT):
                cols = slice(n * 512, (n + 1) * 512)
                for i in range(HT):
                    tick(i)
                    rows = slice(r0 + i * 128, r0 + (i + 1) * 128)
                    pb = k % 4
                    u = k % 3
                    a = k % 2
                    if k + 1 < 4 * HT:
                        ld_in(k + 1)
                    k += 1
                    mm_acc(psum[pb][:, :], Bps[pb], [(at[a][:, f, :], wb[:, f, :]) for f in range(FC)], [Bat[a], Bwb])
                    P.add("dve", lambda e, u=u, pb=pb: e.tensor_tensor(out=xo[u], in0=psum[pb][:, :], in1=xp[u],
                                                                        op=ALU.add), r=[Bps[pb], Bxp[u]], w=[Bxo[u]])
                    drows = slice(rows.start - dst_off, rows.stop - dst_off)
                    P.add("sp", lambda e, u=u, drows=drows, cols=cols: e.dma_start(out=dst_ap[drows, cols], in_=xo[u]),
                          r=[Bxo[u]], w=[Bdst], dma=True)
            P.barrier()

        eps_t = A.alloc([128, 1], F32)
        one_t = A.alloc([128, 1], F32)
        lnsc_t = A.alloc([128, 1], F32)
        P.add("pool", lambda e: e.memset(eps_t, EPS), w=[B_cst])
        P.add("pool", lambda e: e.memset(one_t, 1.0), w=[B_cst])
        P.add("pool", lambda e: e.memset(lnsc_t, math.log(128 ** -0.5)), w=[B_cst])
        hist = A.alloc([128, 2 * FC, 2], F32)
        Bhist = P.buf("hist")
        flg = A.alloc([128, 1], F32)
        P.add("sp", lambda e: e.dma_start(out=flg, in_=flag_d), r=[B_w], w=[B_cst], dma=True)
        KEEP2 = A.off
        P.barrier()

        for l in range(n_layers):
            src_ap, Bsrc = (x_in, B_x_in) if l == 0 else (x_l1, B["x_l1"])
            last = (l == n_layers - 1)
            for j in range(2):
                phase1(l, j, src_ap, Bsrc, trim=(last and j == 0))
            if stop_after == "p1":
                break
            phase2(l, trim=last)
            if stop_after == "p2":
                break
            phase3(l, trim=last)
            if stop_after == "p3":
                break
            for j in range(2):
                phase4(l, j, src_ap, Bsrc, trim=(last and j == 0))
            if stop_after == "p4":
                break
            last = (l == n_layers - 1)
            for j in range(2):
                if last:
                    if j == 1:
                        phase5(l, j, y_out, B["y"], dst_off=T, halo=True)
                else:
                    phase5(l, j, x_l1, B["x_l1"])

        P.barrier()
        P.emit()
    return nc


def _consts():
    c = np.zeros((128, 3, 128), np.float32)
    c[:, 0, :] = np.eye(128, dtype=np.float32)
    c[:, 1, :] = np.triu(np.ones((128, 128), np.float32))
    c[:, 2, :] = 1.0
    return c


def prep_shared(inp):
    f = lambda a: np.ascontiguousarray(np.asarray(a, dtype=np.float32))
    sh = {}
    sh["w_in"] = f(inp["w_in"])
    sh["w_pa"] = f(inp["w_proj_a"])
    sh["w_pm"] = f(inp["w_proj_m"])
    sh["w_out"] = f(inp["w_out"])
    sh["w_up"] = f(inp["w_up"])
    sh["w_down"] = f(inp["w_down"])
    sh["g1"] = f(inp["norm1_g"])
    sh["g2"] = f(inp["norm2_g"])
    gb = np.concatenate([np.asarray(inp["fox_f_bias"]), np.asarray(inp["m_f_bias"]), np.asarray(inp["m_i_bias"])], axis=1)
    sh["gbias"] = f(np.broadcast_to(gb[:, None, :], (NL, 128, 16)))
    sh["qkg"] = f(np.stack([np.asarray(inp["q_norm_g"]), np.asarray(inp["k_norm_g"])], axis=2))
    sh["mng"] = f(np.broadcast_to(np.asarray(inp["m_norm_g"]).reshape(NL, 1, 1024), (NL, 128, 1024)))
    sh["cw"] = f(np.asarray(inp["conv_w"]).reshape(NL, 3, 2 * FC, 128).transpose(0, 3, 2, 1))
    sh["cb"] = f(np.asarray(inp["conv_b"]).reshape(NL, 2 * FC, 128).transpose(0, 2, 1))
    sh["consts"] = _consts()
    return sh


def kernel(**inputs):
    x = np.asarray(inputs["x"], dtype=np.float32)
    Bn, S, _ = x.shape
    T = S // 2
    sh = prep_shared(inputs)
    nc = build(S=S)
    in_maps = []
    for b in range(Bn):
        for r in range(2):
            m = dict(sh)
            if r == 1:
                m["x"] = np.ascontiguousarray(x[b])
            else:
                m["x"] = np.ascontiguousarray(np.concatenate([np.zeros((T, D), np.float32), x[b, :T]], axis=0))
            m["flag"] = np.full((128, 1), float(r), np.float32)
            in_maps.append(m)
    res = run_bass_kernel_spmd(nc, in_maps, core_ids=list(range(2 * Bn)))
    out = np.empty((Bn, S, D), np.float32)
    for b in range(Bn):
        out[b, :T] = np.asarray(res.results[2 * b]["y"])
        out[b, T:] = np.asarray(res.results[2 * b + 1]["y"])
    return out
```

```python
import math
from contextlib import ExitStack

import numpy as np
import concourse.bass as bass
import concourse.mybir as mybir
from concourse.bass_utils import run_bass_kernel_spmd

F32 = mybir.dt.float32
BF16 = mybir.dt.bfloat16
U8 = mybir.dt.uint8
AF = mybir.ActivationFunctionType
ALU = mybir.AluOpType
AX = mybir.AxisListType

D = 2048
DIN = 10256
DFF = 5632
NL = 2
NH = 8
MH = 4
EPS = 1e-6
KC = D // 128
FC = DFF // 128
ENG = ("pe", "act", "dve", "pool", "sp")


class Buf:
    def __init__(self, name):
        self.name = name
        self.writers = []
        self.readers = []
        self.sem = None
        self.nd = 0
        self.persistent = False


class Op:
    pass


class Prog:
    def __init__(self, nc, strict=("act", "dve", "pool")):
        self.nc = nc
        self.ops = {e: [] for e in ENG}
        self.bufs = []
        self.strict = strict

    def buf(self, name):
        for b in self.bufs:
            if b.name == name:
                return b
        b = Buf(name)
        self.bufs.append(b)
        return b

    def add(self, eng, fn, r=(), w=(), dma=False):
        op = Op()
        op.eng, op.fn, op.dma, op.signal = eng, fn, dma, False
        op.deps_eng, op.deps_sem = {}, {}

        def dep(d):
            if d.dma:
                k = id(d.dst)
                v = op.deps_sem.get(k)
                if v is None or v[1] < d.count:
                    op.deps_sem[k] = (d.dst, d.count)
            else:
                if d.eng == eng and eng not in self.strict:
                    return
                if op.deps_eng.get(d.eng, -1) < d.seq:
                    op.deps_eng[d.eng] = d.seq

        for b in r:
            for d in b.writers:
                dep(d)
        for b in w:
            for d in b.readers:
                dep(d)
        for b in r:
            b.readers.append(op)
        for b in w:
            if b.readers:
                b.writers = [op]
                b.readers = []
            else:
                b.writers.append(op)
        op.seq = len(self.ops[eng])
        self.ops[eng].append(op)
        if dma:
            op.dst = w[0]
            op.dst.nd += 1
            op.count = 16 * op.dst.nd
        return op

    def barrier(self, final=False):
        lasts = {}
        for e in ENG:
            if e == "sp":
                continue
            for op in reversed(self.ops[e]):
                if op.fn is not None:
                    lasts[e] = op.seq
                    break
        sems = {id(b): (b, 16 * b.nd) for b in self.bufs if b.nd > 0 and (final or not b.persistent)}
        for e in ENG:
            op = Op()
            op.eng, op.fn, op.dma, op.signal = e, None, False, False
            op.deps_eng = {e2: s for e2, s in lasts.items() if e2 != e}
            op.deps_sem = dict(sems)
            op.seq = len(self.ops[e])
            self.ops[e].append(op)
        for b in self.bufs:
            if b.persistent and not final:
                continue
            b.writers = []
            b.readers = []

    def emit(self):
        nc = self.nc
        for e in ENG:
            for op in self.ops[e]:
                for pe_, s in op.deps_eng.items():
                    self.ops[pe_][s].signal = True
        for e in ENG:
            c = 0
            for op in self.ops[e]:
                if op.signal:
                    c += 1
                op.sig = c
        with ExitStack() as es:
            esem = {e: es.enter_context(nc.semaphore("S_" + e)) for e in ENG}
            for b in self.bufs:
                if b.nd > 0:
                    b.sem = es.enter_context(nc.semaphore("D_" + b.name))
            block = es.enter_context(nc.Block())

            def run(e, eng):
                waited = {}
                for op in self.ops[e]:
                    for pe_, s in op.deps_eng.items():
                        v = self.ops[pe_][s].sig
                        key = ("e", pe_)
                        if waited.get(key, 0) < v:
                            eng.wait_ge(esem[pe_], v)
                            waited[key] = v
                    for (dst, cnt) in op.deps_sem.values():
                        key = ("d", id(dst))
                        if waited.get(key, 0) < cnt:
                            eng.wait_ge(dst.sem, cnt)
                            waited[key] = cnt
                    if op.fn is not None:
                        inst = op.fn(eng)
                        if op.dma:
                            inst.then_inc(op.dst.sem, 16)
                        elif op.signal:
                            inst.then_inc(esem[e], 1)

            @block.tensor
            def _(eng):
                run("pe", eng)

            @block.scalar
            def _(eng):
                run("act", eng)

            @block.vector
            def _(eng):
                run("dve", eng)

            @block.gpsimd
            def _(eng):
                run("pool", eng)

            @block.sync
            def _(eng):
                run("sp", eng)


class Arena:
    def __init__(self, handle, size):
        self.h, self.size, self.off = handle, size, 0

    def reset(self, keep=0):
        self.off = keep

    def alloc(self, shape, dtype):
        esz = 2 if dtype == BF16 else 4
        n = 1
        for s in shape[1:]:
            n *= s
        nbytes = (n * esz + 63) // 64 * 64
        assert self.off + nbytes <= self.size, ("SBUF arena overflow", self.off, nbytes)
        v = self.h[0:shape[0], self.off:self.off + n * esz].bitcast(dtype)
        self.off += nbytes
        if len(shape) == 3:
            v = v.rearrange("p (a b) -> p a b", a=shape[1])
        elif len(shape) == 4:
            v = v.rearrange("p (a b c) -> p a b c", a=shape[1], b=shape[2])
        return v


def build(S=4096, n_layers=NL, debug=(), stop_after=None):
    nc = bass.Bass("TRN2", target_bir_lowering=False)
    P = Prog(nc)
    NT = S // 128
    T = S // 2
    HT = T // 128
    TG = T // 512

    def dram(name, shape, dt, kind="Internal"):
        if name in debug:
            kind = "ExternalOutput"
        return nc.dram_tensor(name, list(shape), dt, kind=kind).ap()

    x_in = dram("x", [S, D], F32, "ExternalInput")
    w_in = dram("w_in", [NL, D, DIN], F32, "ExternalInput")
    w_pa = dram("w_pa", [NL, 1024, D], F32, "ExternalInput")
    w_pm = dram("w_pm", [NL, 1024, D], F32, "ExternalInput")
    w_out = dram("w_out", [NL, D, D], F32, "ExternalInput")
    w_up = dram("w_up", [NL, D, 2 * DFF], F32, "ExternalInput")
    w_down = dram("w_down", [NL, DFF, D], F32, "ExternalInput")
    g1_d = dram("g1", [NL, D], F32, "ExternalInput")
    g2_d = dram("g2", [NL, D], F32, "ExternalInput")
    gbias_d = dram("gbias", [NL, 128, 16], F32, "ExternalInput")
    qkg_d = dram("qkg", [NL, 128, 2], F32, "ExternalInput")
    mng_d = dram("mng", [NL, 128, 1024], F32, "ExternalInput")
    cw_d = dram("cw", [NL, 128, 2 * FC, 3], F32, "ExternalInput")
    cb_d = dram("cb", [NL, 128, 2 * FC], F32, "ExternalInput")
    consts_d = dram("consts", [128, 3, 128], F32, "ExternalInput")
    flag_d = dram("flag", [128, 1], F32, "ExternalInput")
    y_out = dram("y", [S // 2, D], F32, "ExternalOutput")

    qT_s = dram("qT_s", [NH, 128, S], BF16)
    kT_s = dram("kT_s", [NH, 128, S], BF16)
    vp_s = dram("vp_s", [NH, 128, NT, 129], BF16)
    tot_s = dram("tot_s", [NT, 12], F32)
    mqT_s = dram("mqT_s", [MH, 128, S], BF16)
    mkT_s = dram("mkT_s", [MH, 128, S], BF16)
    mk_s = dram("mk_s", [NT, 128, 512], BF16)
    mv_s = dram("mv_s", [NT, 128, MH, 257], BF16)
    mo_s = dram("mo_s", [NT, 128, 1024], BF16)
    gaT_s = dram("gaT_s", [KC, 128, S], BF16)
    gmT_s = dram("gmT_s", [KC, 128, S], BF16)
    aoT_s = dram("aoT_s", [NH, 128, S], BF16)
    moT_s = dram("moT_s", [NH, 128, S], BF16)
    x_mid = dram("x_mid", [S, D], F32)
    x_l1 = dram("x_l1", [S, D], F32)
    actT_s = dram("actT_s", [S // 128, 128, FC, 128], BF16)

    B_x_in = P.buf("x_in")
    B = {n: P.buf(n) for n in ("qT_s", "kT_s", "vp_s", "tot_s", "mqT_s", "mkT_s", "mk_s", "mv_s",
                               "mo_s", "gaT_s", "gmT_s", "aoT_s", "moT_s", "x_mid", "x_l1", "actT_s", "y")}
    B_w = P.buf("weights")

    with ExitStack() as es:
        ARENA_BYTES = 204 * 1024
        arena_h = es.enter_context(nc.sbuf_tensor("arena", [128, ARENA_BYTES], U8))
        A = Arena(arena_h, ARENA_BYTES)
        psum = [es.enter_context(nc.psum_tensor("ps%d" % i, [128, 512], F32)) for i in range(8)]
        Bps = [P.buf("ps%d" % i) for i in range(8)]

        def ps_bf(i):
            return psum[i][:, :].bitcast(BF16)

        cst = A.alloc([128, 3, 128], F32)
        ident_b = A.alloc([128, 128], BF16)
        ones_b = A.alloc([128, 128], BF16)
        tri_b = A.alloc([128, 128], BF16)
        B_cst = P.buf("cst")
        P.add("sp", lambda e: e.dma_start(out=cst, in_=consts_d), r=[B_w], w=[B_cst], dma=True)
        P.add("dve", lambda e: e.tensor_copy(out=ident_b, in_=cst[:, 0, :]), r=[B_cst], w=[B_cst])
        P.add("dve", lambda e: e.tensor_copy(out=tri_b, in_=cst[:, 1, :]), r=[B_cst], w=[B_cst])
        P.add("dve", lambda e: e.tensor_copy(out=ones_b, in_=cst[:, 2, :]), r=[B_cst], w=[B_cst])
        tri_f = cst[:, 1, :]
        ones_f = cst[:, 2, :]
        KEEP = A.off
        P.barrier()

        PST = []
        BPST = []

        class WTile:
            def __init__(self, ws, srcs, ncols, key=None):
                self.ws, self.srcs, self.ncols = ws, srcs, ncols
                slot = ws.wi % len(ws.wb)
                ws.wi += 1
                self.wb, self.Bwb = ws.wb[slot], ws.Bwb[slot]
                self.pieces = [(k0, min(ws.nk, k0 + ws.kp)) for k0 in range(0, ws.nk, ws.kp)]
                self.stg_of = {}
                self.nd = 0
                self.ncst = 0
                if key is not None and key in PREF:
                    for si in PREF.pop(key):
                        self.stg_of[self.nd] = (ws.stg[si], ws.Bstg[si])
                        self.nd += 1

            def _dma(self):
                ws = self.ws
                k0, k1 = self.pieces[self.nd]
                si = GSI[0] % len(ws.stg)
                GSI[0] += 1
                stg, Bstg = ws.stg[si], ws.Bstg[si]
                self.stg_of[self.nd] = (stg, Bstg)
                self.nd += 1
                for (c0, c1, fn) in self.srcs:
                    P.add("sp", lambda e, stg=stg, k0=k0, k1=k1, c0=c0, c1=c1, fn=fn: e.dma_start(
                        out=stg[:, 0:k1 - k0, c0:c1], in_=fn(k0, k1)), r=[B_w], w=[Bstg], dma=True)

            def step(self):
                if self.ncst >= len(self.pieces):
                    return False
                while self.nd < min(len(self.pieces), self.ncst + 2):
                    self._dma()
                k0, k1 = self.pieces[self.ncst]
                stg, Bstg = self.stg_of.pop(self.ncst)
                self.ncst += 1
                wb, ncols = self.wb, self.ncols
                P.add("pool", lambda e, stg=stg, k0=k0, k1=k1: e.tensor_copy(
                    out=wb[:, k0:k1, 0:ncols], in_=stg[:, 0:k1 - k0, 0:ncols]), r=[Bstg], w=[self.Bwb])
                return True

            def finish(self):
                while self.step():
                    pass
                return self.wb, self.Bwb

        class WStream:
            def __init__(self, nk, ncols, nslots=2, kp=4, nstg=4, name="w"):
                self.nk, self.ncols, self.kp = nk, ncols, kp
                assert kp * ncols <= 2048
                self.stg = [PST[i][:, 0:kp * ncols].rearrange("p (k n) -> p k n", k=kp) for i in range(len(PST))]
                self.Bstg = BPST
                self.wb = [A.alloc([128, nk, ncols], BF16) for _ in range(nslots)]
                self.Bwb = [P.buf("%s_wb%d" % (name, i)) for i in range(nslots)]
                self.si = 0
                self.wi = 0

            def tile(self, src_fn=None, srcs=None, ncols=None, key=None):
                ncols = ncols or self.ncols
                if srcs is None:
                    srcs = [(0, ncols, src_fn)]
                return WTile(self, srcs, ncols, key=key)

        PREF = {}
        PREF_DONE = set()
        GSI = [0]

        def prefetch(d):
            if d is None or d["key"] in PREF_DONE:
                return
            PREF_DONE.add(d["key"])
            kp, nk, ncols = d["kp"], d["nk"], d["ncols"]
            slots = []
            for p in range(min(d.get("npieces", 2), (nk + kp - 1) // kp)):
                k0, k1 = p * kp, min(nk, (p + 1) * kp)
                si = GSI[0] % len(PST)
                GSI[0] += 1
                stg = PST[si][:, 0:kp * ncols].rearrange("p (k n) -> p k n", k=kp)
                for (c0, c1, fn) in d["srcs"]:
                    P.add("sp", lambda e, stg=stg, k0=k0, k1=k1, c0=c0, c1=c1, fn=fn: e.dma_start(
                        out=stg[:, 0:k1 - k0, c0:c1], in_=fn(k0, k1)), r=[B_w], w=[BPST[si]], dma=True)
                slots.append(si)
            PREF[d["key"]] = slots

        def wsl(w, l, c0, c1):
            return lambda k0, k1: w[l, k0 * 128:k1 * 128, c0:c1].rearrange("(c p) n -> p c n", p=128)

        def d_p1(l, j):
            return dict(key=("p1", l, j), srcs=[(0, 512, wsl(w_in, l, 2048, 2560))], nk=KC, kp=4, ncols=512)

        def d_p4a(l, j):
            return dict(key=("p4a", l, j), srcs=[(0, 512, wsl(w_pa, l, 0, 512))], nk=8, kp=4, ncols=512)

        def d_p4b(l, j):
            return dict(key=("p4b", l, j), srcs=[(0, 512, wsl(w_out, l, 0, 512))], nk=KC, kp=4, ncols=512)

        def d_p5b(l, j):
            return dict(key=("p5b", l, j), srcs=[(0, 128, wsl(w_up, l, 0, 128)), (128, 256, wsl(w_up, l, DFF, DFF + 128))],
                        nk=KC, kp=4, ncols=256)

        def d_p5c(l, j):
            return dict(key=("p5c", l, j), srcs=[(0, 512, wsl(w_down, l, 0, 512))], nk=FC, kp=4, ncols=512, npieces=4)

        def pipelined(ws, tiles_srcs, nunits, first=None):
            cur = first
            if cur is None:
                cur = ws.tile(**tiles_srcs[0])
                cur.finish()
            for n in range(len(tiles_srcs)):
                nxt = ws.tile(**tiles_srcs[n + 1]) if n + 1 < len(tiles_srcs) else None
                npieces = len(nxt.pieces) if nxt is not None else 0
                state = {"done": 0}

                def tick(u, nxt=nxt, npieces=npieces, state=state):
                    if nxt is None:
                        return
                    want = min(npieces, (u + 1) * npieces // nunits + 1)
                    while state["done"] < want and nxt.step():
                        state["done"] += 1
                yield n, cur.wb, cur.Bwb, tick
                if nxt is not None:
                    nxt.finish()
                cur = nxt

        def mm_acc(ps_ap, Bp, pairs, rbufs):
            n = len(pairs)
            for i, (l, r_) in enumerate(pairs):
                P.add("pe", lambda e, l=l, r_=r_, i=i: e.matmul(ps_ap, l, r_, start=(i == 0), stop=(i == n - 1)),
                      r=rbufs, w=[Bp])

        def phase_norm(src_ap, Bsrc, r0, hT, BhT, g_row):
            gbc = A.alloc([128, D], F32)
            Bgbc = P.buf("gbc")
            P.add("sp", lambda e: e.dma_start(out=gbc, in_=g_row.partition_broadcast(128)), r=[B_w], w=[Bgbc], dma=True)
            xt = [A.alloc([128, D], F32) for _ in range(2)]
            Bxt = [P.buf("xt%d" % i) for i in range(2)]
            hn = [A.alloc([128, D], BF16) for _ in range(2)]
            Bhn = [P.buf("hn%d" % i) for i in range(2)]
            junk = A.alloc([128, D], BF16)
            Bjunk = P.buf("junk")
            st = A.alloc([128, 8], F32)
            Bst = P.buf("st")
            for i in range(HT):
                s = i % 2
                rows = slice(r0 + i * 128, r0 + (i + 1) * 128)
                P.add("sp", lambda e, s=s, rows=rows: e.dma_start(out=xt[s], in_=src_ap[rows, :]),
                      r=[Bsrc], w=[Bxt[s]], dma=True)
                P.add("act", lambda e, s=s: e.activation(out=junk, in_=xt[s], func=AF.Square, accum_out=st[:, 0:1]),
                      r=[Bxt[s]], w=[Bjunk, Bst])
                P.add("act", lambda e: e.activation(out=st[:, 1:2], in_=st[:, 0:1], func=AF.Sqrt,
                                                    scale=1.0 / D, bias=eps_t[:, 0:1]), r=[Bst], w=[Bst])
                P.add("dve", lambda e: e.reciprocal(out=st[:, 2:3], in_=st[:, 1:2]), r=[Bst], w=[Bst])
                P.add("dve", lambda e, s=s: e.scalar_tensor_tensor(out=hn[s], in0=xt[s], scalar=st[:, 2:3], in1=gbc,
                                                                   op0=ALU.mult, op1=ALU.mult),
                      r=[Bst, Bxt[s], Bgbc], w=[Bhn[s]])
                for q in range(4):
                    pb = 4 + (q % 2)
                    pv = ps_bf(pb)
                    for c4 in range(4):
                        c = q * 4 + c4
                        P.add("pe", lambda e, s=s, c=c, c4=c4, pv=pv: e.transpose(
                            out=pv[:, c4 * 128:(c4 + 1) * 128], in_=hn[s][:, c * 128:(c + 1) * 128],
                            identity=ident_b), r=[Bhn[s], B_cst], w=[Bps[pb]])
                    ce = "act" if q % 2 == 0 else "dve"
                    dst = hT[:, q * 4:(q + 1) * 4, i * 128:(i + 1) * 128]
                    srcv = pv[:, 0:512].rearrange("p (a b) -> p a b", a=4)
                    if ce == "act":
                        P.add("act", lambda e, dst=dst, srcv=srcv: e.copy(out=dst, in_=srcv), r=[Bps[pb]], w=[BhT[i]])
                    else:
                        P.add("dve", lambda e, dst=dst, srcv=srcv: e.tensor_copy(out=dst, in_=srcv),
                              r=[Bps[pb]], w=[BhT[i]])
                yield

        eps_t = None

        def phase1(l, j, src_ap, Bsrc, trim=False, nextw=None):
            A.reset(KEEP2)
            r0 = j * T
            hT = A.alloc([128, KC, T], BF16)
            BhT = [P.buf("hT%d" % i) for i in range(HT)]
            mark = A.off
            gb = A.alloc([128, 16], F32)
            qkg = A.alloc([128, 4], F32)
            Bpar = P.buf("par1")
            P.add("sp", lambda e: e.dma_start(out=gb, in_=gbias_d[l]), r=[B_w], w=[Bpar], dma=True)
            P.add("sp", lambda e: e.dma_start(out=qkg[:, 0:2], in_=qkg_d[l]), r=[B_w], w=[Bpar], dma=True)
            P.add("dve", lambda e: e.scalar_tensor_tensor(out=qkg[:, 2:3], in0=qkg[:, 0:1], scalar=float(128 ** -0.5),
                                                          in1=qkg[:, 1:2], op0=ALU.mult, op1=ALU.mult),
                  r=[Bpar], w=[Bpar])
            E_all = A.alloc([128, HT, 8], F32)
            QS_all = A.alloc([128, HT, 4], F32)
            KS_all = A.alloc([128, HT, 4], F32)
            Bgate = P.buf("gates")
            ws = WStream(KC, 512, name="win")
            wsm_stg = A.alloc([128, KC, 16], F32)
            wsm = A.alloc([128, KC, 16], BF16)
            Bwsm_stg, Bwsm = P.buf("wsm_stg"), P.buf("wsm")
            NST = 3
            st_bf = [A.alloc([128, 4, 132], BF16) for _ in range(NST)]
            Bst_bf = [P.buf("st_bf%d" % i) for i in range(NST)]
            st2_bf = [A.alloc([128, 512], BF16) for _ in range(NST)]
            Bst2_bf = [P.buf("st2_bf%d" % i) for i in range(NST)]
            mvst = [A.alloc([128, 2, 257], BF16) for _ in range(2)]
            Bmvst = [P.buf("mvst%d" % i) for i in range(2)]
            tmp_f = [A.alloc([128, 512], F32) for _ in range(2)]
            Btmp_f = [P.buf("tmp_f%d" % i) for i in range(2)]
            sm = A.alloc([128, 64], F32)
            Bsm = P.buf("sm")
            tsb = A.alloc([128, 16], F32)
            Btsb = P.buf("tsb")
            for s in range(2):
                P.add("pool", lambda e, s=s: e.memset(mvst[s][:, :, 256:257], 1.0), w=[Bmvst[s]])
            cnt = {"st": 0, "st2": 0, "mv": 0, "tf": 0, "ps": 0}

            def nxt(k, n):
                v = cnt[k] % n
                cnt[k] += 1
                return v

            groups = []
            groups += [("fv", 2048 + 512 * g, g) for g in range(2)]
            groups += [("mq", 3080, 0), ("mk", 3592, 0)]
            groups += [("mv", 4104 + 512 * g, g) for g in range(2)]
            groups += [("mo", 5128 + 512 * g, g) for g in range(2)]
            groups += [("fq", 512 * g, g) for g in range(2)]
            groups += [("fk", 1024 + 512 * g, g) for g in range(2)]
            groups += [("ga", 6160 + 512 * g, g) for g in range(4)]
            groups += [("gm", 8208 + 512 * g, g) for g in range(4)]

            def sm_src(c0, c1, lo, hi):
                return w_in[l, :, lo:hi].rearrange("(c p) n -> p c n", p=128)
            with nc.allow_non_contiguous_dma(reason="tiny gate columns"):
                for (dlo, lo, hi) in ((0, 3072, 3080), (8, 6156, 6160), (12, 6152, 6156)):
                    P.add("sp", lambda e, dlo=dlo, lo=lo, hi=hi: e.dma_start(
                        out=wsm_stg[:, :, dlo:dlo + hi - lo],
                        in_=w_in[l, :, lo:hi].rearrange("(c p) n -> p c n", p=128)),
                        r=[B_w], w=[Bwsm_stg], dma=True)
            for k in range(KC):
                P.add("dve", lambda e, k=k: e.tensor_copy(out=wsm[:, k, :], in_=wsm_stg[:, k, :]),
                      r=[Bwsm_stg], w=[Bwsm])
            gsrcs = [dict(src_fn=lambda k0, k1, c0=c0: w_in[l, k0 * 128:k1 * 128, c0:c0 + 512].rearrange(
                "(c p) n -> p c n", p=128)) for (kind, c0, g) in groups]
            first_tile = ws.tile(key=("p1", l, j), **gsrcs[0])
            first_tile.finish()
            norm_it = phase_norm(src_ap, Bsrc, r0, hT, BhT, g1_d[l:l + 1, :])
            zb = sm[:, 0:16]
            e1 = sm[:, 16:32]
            sp_ = sm[:, 32:48]
            t4 = sm[:, 48:52]

            def smA(i):
                tok = slice(i * 128, (i + 1) * 128)
                mm_acc(psum[7][:, 0:16], Bps[7], [(hT[:, c, tok], wsm[:, c, :]) for c in range(KC)], [BhT[i], Bwsm])
                P.add("dve", lambda e: e.tensor_tensor(out=zb, in0=psum[7][:, 0:16], in1=gb, op=ALU.add),
                      r=[Bps[7], Bpar], w=[Bsm])
                P.add("act", lambda e: e.activation(out=e1, in_=zb, func=AF.Exp, scale=-1.0), r=[Bsm], w=[Bsm])
                P.add("act", lambda e: e.activation(out=sp_, in_=e1, func=AF.Ln, bias=one_t[:, 0:1]), r=[Bsm], w=[Bsm])

            def smB(i):
                gi = j * HT + i
                P.add("pe", lambda e: e.matmul(psum[7][:, 32:44], tri_f, sp_[:, 0:12], start=True, stop=True),
                      r=[Bsm, B_cst], w=[Bps[7]])
                P.add("pe", lambda e: e.matmul(psum[7][:, 64:76], ones_f, sp_[:, 0:12], start=True, stop=True),
                      r=[Bsm, B_cst], w=[Bps[7]])
                cum = psum[7][:, 32:44]
                P.add("act", lambda e, i=i: e.activation(out=E_all[:, i, :], in_=cum[:, 0:8], func=AF.Exp),
                      r=[Bps[7]], w=[Bgate])
                P.add("act", lambda e, i=i: e.activation(out=QS_all[:, i, :], in_=cum[:, 8:12], func=AF.Exp,
                                                         scale=-1.0, bias=lnsc_t[:, 0:1]), r=[Bps[7]], w=[Bgate])
                P.add("dve", lambda e: e.tensor_tensor(out=t4, in0=cum[:, 8:12], in1=zb[:, 12:16], op=ALU.add),
                      r=[Bps[7], Bsm], w=[Bsm])
                P.add("act", lambda e, i=i: e.activation(out=KS_all[:, i, :], in_=t4, func=AF.Exp),
                      r=[Bsm], w=[Bgate])
                P.add("act", lambda e: e.copy(out=tsb[:, 0:12], in_=psum[7][:, 64:76]), r=[Bps[7]], w=[Btsb])
                P.add("sp", lambda e, gi=gi: e.dma_start(out=tot_s[gi:gi + 1, :], in_=tsb[0:1, 0:12]),
                      r=[Btsb], w=[B["tot_s"]], dma=True)

            for _ in norm_it:
                pass
            for i in range(HT):
                smA(i)
                smB(i)
            for gn, wb, Bwb, tick in pipelined(ws, gsrcs, 16, first=first_tile):
                kind, c0, g = groups[gn]
                if kind in ("fv", "mq", "mk", "mv", "mo"):
                  def tm_unit(i, wb=wb, Bwb=Bwb, kind=kind, g=g):
                    if True:
                        gi = j * HT + i
                        tok = slice(i * 128, (i + 1) * 128)
                        pb = nxt("ps", 4)
                        ps = psum[pb]
                        mm_acc(ps[:, :], Bps[pb], [(hT[:, c, tok], wb[:, c, :]) for c in range(KC)], [BhT[i], Bwb])
                        if kind == "fv":
                            s = nxt("st", NST)
                            h0 = g * 4
                            P.add("dve", lambda e, s=s, ps=ps, i=i, h0=h0: e.tensor_tensor(
                                out=st_bf[s][:, :, 0:128], in0=ps[:, :].rearrange("p (h d) -> p h d", h=4),
                                in1=E_all[:, i, h0:h0 + 4].unsqueeze(2).to_broadcast([128, 4, 128]), op=ALU.mult),
                                r=[Bps[pb], Bgate], w=[Bst_bf[s]])
                            P.add("dve", lambda e, s=s, i=i, h0=h0: e.tensor_copy(
                                out=st_bf[s][:, :, 128:129], in_=E_all[:, i, h0:h0 + 4].unsqueeze(2)),
                                r=[Bgate], w=[Bst_bf[s]])
                            P.add("sp", lambda e, s=s, h0=h0, gi=gi: e.dma_start(
                                out=vp_s[h0:h0 + 4, :, gi, :].rearrange("h p c -> p h c"),
                                in_=st_bf[s][:, :, 0:129]), r=[Bst_bf[s]], w=[B["vp_s"]], dma=True)
                        elif kind in ("mq", "mk"):
                            s = nxt("st2", NST)
                            SC = QS_all if kind == "mq" else KS_all
                            P.add("dve", lambda e, s=s, ps=ps, i=i, SC=SC: e.tensor_tensor(
                                out=st2_bf[s][:, :].rearrange("p (h d) -> p h d", h=4),
                                in0=ps[:, :].rearrange("p (h d) -> p h d", h=4),
                                in1=SC[:, i, 0:4].unsqueeze(2).to_broadcast([128, 4, 128]), op=ALU.mult),
                                r=[Bps[pb], Bgate], w=[Bst2_bf[s]])
                            if kind == "mk":
                                P.add("sp", lambda e, s=s, gi=gi: e.dma_start(out=mk_s[gi], in_=st2_bf[s]),
                                      r=[Bst2_bf[s]], w=[B["mk_s"]], dma=True)
                            tb = 4 + (i % 2)
                            pv = ps_bf(tb)
                            for hh in range(4):
                                P.add("pe", lambda e, s=s, hh=hh, pv=pv: e.transpose(
                                    out=pv[:, hh * 128:(hh + 1) * 128], in_=st2_bf[s][:, hh * 128:(hh + 1) * 128],
                                    identity=ident_b), r=[Bst2_bf[s], B_cst], w=[Bps[tb]])
                            s2 = nxt("st2", NST)
                            P.add("act", lambda e, s2=s2, pv=pv: e.copy(out=st2_bf[s2], in_=pv[:, 0:512]),
                                  r=[Bps[tb]], w=[Bst2_bf[s2]])
                            dstT = mqT_s if kind == "mq" else mkT_s
                            Bd = B["mqT_s"] if kind == "mq" else B["mkT_s"]
                            P.add("sp", lambda e, s2=s2, dstT=dstT, gi=gi: e.dma_start(
                                out=dstT[:, :, gi * 128:(gi + 1) * 128].rearrange("h p t -> p h t"),
                                in_=st2_bf[s2][:, :].rearrange("p (h t) -> p h t", h=4)),
                                r=[Bst2_bf[s2]], w=[Bd], dma=True)
                        elif kind == "mv":
                            s = nxt("mv", 2)
                            P.add("act", lambda e, s=s, ps=ps: e.copy(
                                out=mvst[s][:, :, 0:256], in_=ps[:, :].rearrange("p (h d) -> p h d", h=2)),
                                r=[Bps[pb]], w=[Bmvst[s]])
                            P.add("sp", lambda e, s=s, gi=gi, g=g: e.dma_start(
                                out=mv_s[gi, :, 2 * g:2 * g + 2, :], in_=mvst[s]),
                                r=[Bmvst[s]], w=[B["mv_s"]], dma=True)
                        else:
                            s = nxt("st2", NST)
                            P.add("act", lambda e, s=s, ps=ps: e.activation(out=st2_bf[s], in_=ps[:, :],
                                                                            func=AF.Sigmoid),
                                  r=[Bps[pb]], w=[Bst2_bf[s]])
                            P.add("sp", lambda e, s=s, gi=gi, g=g: e.dma_start(
                                out=mo_s[gi, :, 512 * g:512 * (g + 1)], in_=st2_bf[s]),
                                r=[Bst2_bf[s]], w=[B["mo_s"]], dma=True)
                  for i in ([HT - 1] if (trim and kind in ("mq", "mo")) else range(HT)):
                      tick(i)
                      tm_unit(i)
                else:
                    for q in range(4):
                        ch = g * 4 + q
                        for tg in ([TG - 1] if (trim and kind in ("fq", "ga", "gm")) else range(TG)):
                            tick(q * TG + tg)
                            tks = slice(tg * 512, (tg + 1) * 512)
                            gt0 = r0 + tg * 512
                            pb = nxt("ps", 4)
                            ps = psum[pb]
                            mm_acc(ps[:, :], Bps[pb], [(wb[:, c, q * 128:(q + 1) * 128], hT[:, c, tks])
                                                       for c in range(KC)], BhT[4 * tg:4 * tg + 4] + [Bwb])
                            if kind in ("ga", "gm"):
                                s = nxt("st2", NST)
                                P.add("act", lambda e, s=s, ps=ps: e.activation(out=st2_bf[s], in_=ps[:, :],
                                                                                func=AF.Sigmoid),
                                      r=[Bps[pb]], w=[Bst2_bf[s]])
                                dstT = gaT_s if kind == "ga" else gmT_s
                                Bd = B["gaT_s"] if kind == "ga" else B["gmT_s"]
                                P.add("sp", lambda e, s=s, dstT=dstT, ch=ch, gt0=gt0: e.dma_start(
                                    out=dstT[ch, :, gt0:gt0 + 512], in_=st2_bf[s]),
                                    r=[Bst2_bf[s]], w=[Bd], dma=True)
                            else:
                                s = nxt("st2", NST)
                                tf = nxt("tf", 2)
                                P.add("act", lambda e, s=s, ps=ps: e.activation(out=st2_bf[s], in_=ps[:, :],
                                                                                func=AF.Square),
                                      r=[Bps[pb]], w=[Bst2_bf[s]])
                                P.add("pe", lambda e, s=s: e.matmul(psum[6][:, :], ones_b, st2_bf[s],
                                                                    start=True, stop=True),
                                      r=[Bst2_bf[s], B_cst], w=[Bps[6]])
                                P.add("act", lambda e, tf=tf: e.activation(out=tmp_f[tf], in_=psum[6][:, :],
                                                                           func=AF.Sqrt, scale=1.0 / 128,
                                                                           bias=eps_t[:, 0:1]),
                                      r=[Bps[6]], w=[Btmp_f[tf]])
                                P.add("dve", lambda e, tf=tf: e.reciprocal(out=tmp_f[tf], in_=tmp_f[tf]),
                                      r=[Btmp_f[tf]], w=[Btmp_f[tf]])
                                s2 = nxt("st2", NST)
                                if kind == "fq":
                                    P.add("dve", lambda e, s2=s2, ps=ps, tf=tf: e.scalar_tensor_tensor(
                                        out=st2_bf[s2], in0=ps[:, :], scalar=qkg[:, 2:3], in1=tmp_f[tf],
                                        op0=ALU.mult, op1=ALU.mult), r=[Bps[pb], Btmp_f[tf], Bpar], w=[Bst2_bf[s2]])
                                else:
                                    P.add("dve", lambda e, s2=s2, ps=ps, tf=tf: e.tensor_tensor(
                                        out=st2_bf[s2], in0=ps[:, :], in1=tmp_f[tf], op=ALU.mult),
                                        r=[Bps[pb], Btmp_f[tf]], w=[Bst2_bf[s2]])
                                dstT = qT_s if kind == "fq" else kT_s
                                Bd = B["qT_s"] if kind == "fq" else B["kT_s"]
                                P.add("sp", lambda e, s2=s2, dstT=dstT, ch=ch, gt0=gt0: e.dma_start(
                                    out=dstT[ch, :, gt0:gt0 + 512], in_=st2_bf[s2]),
                                    r=[Bst2_bf[s2]], w=[Bd], dma=True)
            prefetch(nextw)
            P.barrier()


        def phase2(l, trim=False, nextw=None):
            A.reset(KEEP2)
            QG = S // 512
            totb = A.alloc([128, NT, 12], F32)
            fcn = A.alloc([128, NT, 8], F32)
            Btot = P.buf("totb")
            qkg = A.alloc([128, 4], F32)
            bnd = A.alloc([128, 4], F32)
            Bpar = P.buf("par2")
            P.add("sp", lambda e: e.dma_start(out=totb, in_=tot_s.partition_broadcast(128)),
                  r=[B["tot_s"]], w=[Btot], dma=True)
            P.add("sp", lambda e: e.dma_start(out=qkg[:, 0:2], in_=qkg_d[l]), r=[B_w], w=[Bpar], dma=True)
            P.add("dve", lambda e: e.scalar_tensor_tensor(out=qkg[:, 2:3], in0=qkg[:, 0:1], scalar=float(128 ** -0.5),
                                                          in1=qkg[:, 1:2], op0=ALU.mult, op1=ALU.mult),
                  r=[Bpar], w=[Bpar])
            P.add("pe", lambda e: e.transpose(out=psum[6][0:1, 0:128], in_=qkg[:, 2:3], identity=cst[:, 0, :]),
                  r=[Bpar, B_cst], w=[Bps[6]])
            P.add("dve", lambda e: e.tensor_reduce(out=bnd[0:1, 0:1], in_=psum[6][0:1, 0:128], axis=AX.X, op=ALU.max,
                                                   apply_absolute_value=True), r=[Bps[6]], w=[Bpar])
            P.add("dve", lambda e: e.tensor_scalar(out=bnd[0:1, 1:2], in0=bnd[0:1, 0:1], scalar1=128.0, scalar2=None,
                                                   op0=ALU.mult), r=[Bpar], w=[Bpar])
            P.add("pe", lambda e: e.matmul(psum[6][:, 128:129], ones_f[0:1, :], bnd[0:1, 1:2], start=True, stop=True),
                  r=[Bpar, B_cst], w=[Bps[6]])
            P.add("dve", lambda e: e.tensor_copy(out=bnd[:, 2:3], in_=psum[6][:, 128:129]), r=[Bps[6]], w=[Bpar])
            P.add("pool", lambda e: e.memset(fcn[:, 0, :], 0.0), w=[Btot])
            for i in range(1, NT):
                P.add("dve", lambda e, i=i: e.tensor_tensor(out=fcn[:, i, :], in0=fcn[:, i - 1, :],
                                                            in1=totb[:, i - 1, 0:8], op=ALU.add), r=[Btot], w=[Btot])
            NS = 2
            kT = [A.alloc([128, S], BF16) for _ in range(NS)]
            qT = [A.alloc([128, S], BF16) for _ in range(NS)]
            vp = [A.alloc([128, NT, 129], BF16) for _ in range(NS)]
            vpm = [A.alloc([128, NT // 2, 129], BF16) for _ in range(NS)]
            Bvpm = [P.buf("a_vpm%d" % i) for i in range(NS)]
            Dt = [A.alloc([128, NT, NT], F32) for _ in range(NS)]
            Dhi = [A.alloc([128, NT, NT], BF16) for _ in range(NS)]
            Dlo = [A.alloc([128, NT, NT], F32) for _ in range(NS)]
            DX = [A.alloc([128, NT, NT], BF16) for _ in range(NS)]
            BkT = [P.buf("a_kT%d" % i) for i in range(NS)]
            BqT = [P.buf("a_qT%d" % i) for i in range(NS)]
            Bvp = [P.buf("a_vp%d" % i) for i in range(NS)]
            BDt = [P.buf("a_D%d" % i) for i in range(NS)]
            NPT = 3
            pt = [A.alloc([128, 512], BF16) for _ in range(NPT)]
            Bpt = [P.buf("a_pt%d" % i) for i in range(NPT)]
            ao = [A.alloc([128, 512], BF16) for _ in range(2)]
            Bao = [P.buf("a_ao%d" % i) for i in range(2)]
            aoT = [A.alloc([128, 512], BF16) for _ in range(2)]
            BaoT = [P.buf("a_aoT%d" % i) for i in range(2)]
            rc = A.alloc([128, 8], F32)
            Brc = P.buf("a_rc")
            ptc = 0
            stc = 0
            for h in range(NH):
                s = h % NS
                P.add("sp", lambda e, s=s, h=h: e.dma_start(out=kT[s], in_=kT_s[h]), r=[B["kT_s"]], w=[BkT[s]], dma=True)
                P.add("sp", lambda e, s=s, h=h: e.dma_start(out=qT[s], in_=qT_s[h]), r=[B["qT_s"]], w=[BqT[s]], dma=True)
                P.add("sp", lambda e, s=s, h=h: e.dma_start(out=vp[s], in_=vp_s[h]), r=[B["vp_s"]], w=[Bvp[s]], dma=True)
                P.add("dve", lambda e, s=s: e.tensor_scalar(out=vpm[s], in0=vp[s][:, 0:NT // 2, :], scalar1=flg[:, 0:1],
                                                            scalar2=None, op0=ALU.mult),
                      r=[Bvp[s], B_cst], w=[Bvpm[s]])
                P.add("dve", lambda e, s=s, h=h: e.scalar_tensor_tensor(
                    out=Dt[s], in0=fcn[:, :, h].unsqueeze(1).to_broadcast([128, NT, NT]), scalar=bnd[:, 2:3],
                    in1=fcn[:, :, h].unsqueeze(2).to_broadcast([128, NT, NT]), op0=ALU.subtract, op1=ALU.subtract),
                    r=[Btot, Bpar], w=[BDt[s]])
                P.add("dve", lambda e, s=s: e.tensor_copy(out=Dhi[s], in_=Dt[s]), r=[BDt[s]], w=[BDt[s]])
                P.add("dve", lambda e, s=s: e.tensor_tensor(out=Dlo[s], in0=Dt[s], in1=Dhi[s], op=ALU.subtract),
                      r=[BDt[s]], w=[BDt[s]])
                P.add("dve", lambda e, s=s: e.tensor_scalar(out=Dlo[s], in0=Dlo[s], scalar1=cst[:, 0, 1:2],
                                                            scalar2=None, op0=ALU.mult), r=[BDt[s], B_cst], w=[BDt[s]])
                P.add("dve", lambda e, s=s: e.scalar_tensor_tensor(
                    out=DX[s], in0=Dhi[s], scalar=cst[:, 0, 0:1], in1=Dlo[s], op0=ALU.mult, op1=ALU.add),
                    r=[BDt[s], B_cst], w=[BDt[s]])
                steps = [(g, j) for g in range(QG) for j in range(4 * g + 4) if not (trim and g < QG // 2 - 1)]

                def emit_qk(st):
                    g, j = steps[st]
                    i0 = 4 * g
                    ilo = max(i0, j)
                    ncol = (i0 + 4 - ilo) * 128
                    qlo = ilo * 128
                    sb = 4 + (st % 2)
                    P.add("pe", lambda e, s=s, j=j, sb=sb, qlo=qlo, ncol=ncol: e.matmul(
                        psum[sb][:, 0:ncol], kT[s][:, j * 128:(j + 1) * 128], qT[s][:, qlo:qlo + ncol],
                        start=True, stop=False), r=[BkT[s], BqT[s]], w=[Bps[sb]])
                    nseg = i0 + 4 - ilo
                    P.add("pe", lambda e, s=s, j=j, sb=sb, ilo=ilo, nseg=nseg, ncol=ncol: e.matmul(
                        psum[sb][:, 0:ncol], ones_b,
                        DX[s][:, ilo:ilo + nseg, j:j + 1].to_broadcast([128, nseg, 128]),
                        start=False, stop=True), r=[BDt[s], B_cst], w=[Bps[sb]])

                emit_qk(0)
                for st, (g, j) in enumerate(steps):
                    if True:
                        i0 = 4 * g
                        ilo = max(i0, j)
                        sb = 4 + (st % 2)
                        p = ptc % NPT
                        ptc += 1
                        if st + 1 < len(steps):
                            emit_qk(st + 1)
                        ncol = (i0 + 4 - ilo) * 128
                        P.add("act", lambda e, p=p, sb=sb, ncol=ncol: e.activation(
                            out=pt[p][:, 0:ncol], in_=psum[sb][:, 0:ncol], func=AF.Exp),
                            r=[Bps[sb]], w=[Bpt[p]])
                        for i in range(ilo, i0 + 4):
                            c0 = (i - ilo) * 128
                            if i == j:
                                P.add("dve", lambda e, p=p, c0=c0: e.tensor_tensor(
                                    out=pt[p][:, c0:c0 + 128], in0=pt[p][:, c0:c0 + 128], in1=tri_b, op=ALU.mult),
                                    r=[Bpt[p], B_cst], w=[Bpt[p]])
                        for i in range(ilo, i0 + 4):
                            c0 = (i - ilo) * 128
                            ab = i - i0
                            msk = (i >= NT // 2 and j < NT // 2)
                            vsrc = vpm[s] if msk else vp[s]
                            Bv = Bvpm[s] if msk else Bvp[s]
                            P.add("pe", lambda e, vsrc=vsrc, p=p, c0=c0, ab=ab, i=i, j=j: e.matmul(
                                psum[ab][:, 0:129], pt[p][:, c0:c0 + 128], vsrc[:, j, :],
                                start=(j == 0), stop=(j == i)), r=[Bpt[p], Bv], w=[Bps[ab]])
                    if j != 4 * g + 3:
                        continue
                    a = stc % 2
                    stc += 1
                    for m in range(4):
                        P.add("dve", lambda e, m=m: e.reciprocal(out=rc[:, m:m + 1], in_=psum[m][:, 128:129]),
                              r=[Bps[m]], w=[Brc])
                        P.add("dve", lambda e, m=m, a=a: e.tensor_scalar(
                            out=ao[a][:, m * 128:(m + 1) * 128], in0=psum[m][:, 0:128], scalar1=rc[:, m:m + 1],
                            scalar2=None, op0=ALU.mult), r=[Bps[m], Brc], w=[Bao[a]])
                    pv = ps_bf(6)
                    for m in range(4):
                        P.add("pe", lambda e, m=m, a=a, pv=pv: e.transpose(
                            out=pv[:, m * 128:(m + 1) * 128], in_=ao[a][:, m * 128:(m + 1) * 128], identity=ident_b),
                            r=[Bao[a], B_cst], w=[Bps[6]])
                    P.add("dve", lambda e, a=a, pv=pv: e.tensor_copy(out=aoT[a], in_=pv[:, 0:512]),
                          r=[Bps[6]], w=[BaoT[a]])
                    P.add("sp", lambda e, a=a, h=h, g=g: e.dma_start(out=aoT_s[h, :, g * 512:(g + 1) * 512],
                                                                     in_=aoT[a]),
                          r=[BaoT[a]], w=[B["aoT_s"]], dma=True)
            prefetch(nextw)
            P.barrier()

        def phase3(l, trim=False, nextw=None):
            A.reset(KEEP2)
            totb = A.alloc([128, NT, 12], F32)
            eL = A.alloc([128, NT, 4], F32)
            Btot = P.buf("m_tot")
            gbc = A.alloc([128, 1024], F32)
            Bg = P.buf("m_g")
            P.add("sp", lambda e: e.dma_start(out=totb, in_=tot_s.partition_broadcast(128)),
                  r=[B["tot_s"]], w=[Btot], dma=True)
            P.add("sp", lambda e: e.dma_start(out=gbc, in_=mng_d[l]), r=[B_w], w=[Bg], dma=True)
            P.add("act", lambda e: e.activation(out=eL, in_=totb[:, :, 8:12], func=AF.Exp, scale=-1.0),
                  r=[Btot], w=[Btot])
            C32 = A.alloc([128, MH, 257], F32)
            Cb = A.alloc([128, MH, 257], BF16)
            BC32, BCb = P.buf("m_C32"), P.buf("m_Cb")
            NS = 2
            qT4 = [A.alloc([128, MH, 128], BF16) for _ in range(NS)]
            kT4 = [A.alloc([128, MH, 128], BF16) for _ in range(NS)]
            k4 = [A.alloc([128, 512], BF16) for _ in range(NS)]
            v4 = [A.alloc([128, MH, 257], BF16) for _ in range(NS)]
            o4 = [A.alloc([128, 1024], BF16) for _ in range(NS)]
            Bin = [P.buf("m_in%d" % i) for i in range(NS)]
            wt = [A.alloc([128, MH, 128], BF16) for _ in range(2)]
            Bwt = [P.buf("m_wt%d" % i) for i in range(2)]
            tmpU = [A.alloc([128, 257], F32) for _ in range(2)]
            BtmpU = [P.buf("m_tmpU%d" % i) for i in range(2)]
            sm = A.alloc([128, 16], F32)
            Bsm = P.buf("m_sm")
            junk = A.alloc([128, 256], BF16)
            Bjunk = P.buf("m_junk")
            mo = [A.alloc([128, 1024], BF16) for _ in range(2)]
            Bmo = [P.buf("m_mo%d" % i) for i in range(2)]
            moT = [A.alloc([128, 8, 128], BF16) for _ in range(2)]
            BmoT = [P.buf("m_moT%d" % i) for i in range(2)]
            uc = 0
            BC32h = [P.buf("m_C32_%d" % h) for h in range(MH)]
            BCbh = [P.buf("m_Cb_%d" % h) for h in range(MH)]
            for h in range(MH):
                P.add("pool", lambda e, h=h: e.memset(C32[:, h, :], 0.0), w=[BC32h[h]])
                P.add("pool", lambda e, h=h: e.memset(Cb[:, h, :], 0.0), w=[BCbh[h]])
            dn = A.alloc([128, 32], F32)
            Bdn, Bss = P.buf("m_dn"), P.buf("m_ss")
            t2 = [A.alloc([128, 256], F32) for _ in range(4)]
            Bt2 = [P.buf("m_t2%d" % i) for i in range(4)]
            pending = []
            for c in range(NT):
                s = c % NS
                tok = slice(c * 128, (c + 1) * 128)
                if c == NT // 2:
                    for h in range(MH):
                        P.add("dve", lambda e, h=h: e.tensor_scalar(out=C32[:, h, :], in0=C32[:, h, :],
                                                                    scalar1=flg[:, 0:1], scalar2=None, op0=ALU.mult),
                              r=[BC32h[h], B_cst], w=[BC32h[h]])
                        P.add("act", lambda e, h=h: e.copy(out=Cb[:, h, :], in_=C32[:, h, :]),
                              r=[BC32h[h]], w=[BCbh[h]])
                P.add("sp", lambda e, s=s, tok=tok: e.dma_start(
                    out=qT4[s], in_=mqT_s[:, :, tok].rearrange("h p t -> p h t")), r=[B["mqT_s"]], w=[Bin[s]], dma=True)
                P.add("sp", lambda e, s=s, tok=tok: e.dma_start(
                    out=kT4[s], in_=mkT_s[:, :, tok].rearrange("h p t -> p h t")), r=[B["mkT_s"]], w=[Bin[s]], dma=True)
                P.add("sp", lambda e, s=s, c=c: e.dma_start(out=k4[s], in_=mk_s[c]), r=[B["mk_s"]], w=[Bin[s]], dma=True)
                P.add("sp", lambda e, s=s, c=c: e.dma_start(out=v4[s], in_=mv_s[c]), r=[B["mv_s"]], w=[Bin[s]], dma=True)
                P.add("sp", lambda e, s=s, c=c: e.dma_start(out=o4[s], in_=mo_s[c]), r=[B["mo_s"]], w=[Bin[s]], dma=True)
                w_ = c % 2
                need_out = not (trim and c < NT // 2 - 1)
                if need_out:
                    for h in range(MH):
                        P.add("pe", lambda e, s=s, h=h: e.matmul(psum[4][:, h * 128:(h + 1) * 128], kT4[s][:, h, :],
                                                                 qT4[s][:, h, :], start=True, stop=True),
                              r=[Bin[s]], w=[Bps[4]])
                    P.add("dve", lambda e, w_=w_: e.tensor_tensor(
                        out=wt[w_], in0=psum[4][:, :].rearrange("p (h t) -> p h t", h=MH),
                        in1=tri_b.unsqueeze(1).to_broadcast([128, MH, 128]), op=ALU.mult),
                        r=[Bps[4], B_cst], w=[Bwt[w_]])
                    for h in range(MH):
                        hb = h // 2
                        hc = (h % 2) * 256
                        P.add("pe", lambda e, s=s, h=h, w_=w_, hb=hb, hc=hc: e.matmul(
                            psum[hb][:, hc:hc + 256], wt[w_][:, h, :], v4[s][:, h, 0:256], start=True, stop=False),
                            r=[Bwt[w_], Bin[s]], w=[Bps[hb]])
                        P.add("pe", lambda e, s=s, h=h, hb=hb, hc=hc: e.matmul(
                            psum[hb][:, hc:hc + 256], qT4[s][:, h, :], Cb[:, h, 0:256], start=False, stop=True),
                            r=[Bin[s], BCbh[h]], w=[Bps[hb]])
                    for h in range(MH):
                        P.add("pe", lambda e, s=s, h=h, w_=w_: e.matmul(
                            psum[2][:, h:h + 1], wt[w_][:, h, :], v4[s][:, h, 256:257], start=True, stop=False),
                            r=[Bwt[w_], Bin[s]], w=[Bps[2]])
                        P.add("pe", lambda e, s=s, h=h: e.matmul(
                            psum[2][:, h:h + 1], qT4[s][:, h, :], Cb[:, h, 256:257], start=False, stop=True),
                            r=[Bin[s], BCbh[h]], w=[Bps[2]])
                for h in range(MH):
                    ub = 5 + (uc % 2)
                    u = uc % 2
                    uc += 1
                    P.add("pe", lambda e, s=s, h=h, ub=ub: e.matmul(psum[ub][:, 0:257], k4[s][:, h * 128:(h + 1) * 128],
                                                                    v4[s][:, h, :], start=True, stop=True),
                          r=[Bin[s]], w=[Bps[ub]])
                    P.add("dve", lambda e, u=u, ub=ub, h=h: e.tensor_tensor(
                        out=tmpU[u], in0=psum[ub][:, 0:257], in1=C32[:, h, :], op=ALU.add),
                        r=[Bps[ub], BC32h[h]], w=[BtmpU[u]])
                    P.add("act", lambda e, u=u, c=c, h=h: e.activation(
                        out=Cb[:, h, :], in_=tmpU[u], func=AF.Copy, scale=eL[:, c, h:h + 1]),
                        r=[BtmpU[u], Btot], w=[BCbh[h]])
                    P.add("dve", lambda e, u=u, c=c, h=h: e.tensor_scalar(
                        out=C32[:, h, :], in0=tmpU[u], scalar1=eL[:, c, h:h + 1], scalar2=None, op0=ALU.mult),
                        r=[BtmpU[u], Btot], w=[BC32h[h]])
                for fn in pending:
                    fn()
                pending = []
                if need_out:
                    P.add("dve", lambda e: e.tensor_reduce(out=dn[:, 0:4], in_=psum[2][:, 0:4].unsqueeze(2), axis=AX.X,
                                                           op=ALU.max, apply_absolute_value=True), r=[Bps[2]], w=[Bdn])
                    P.add("dve", lambda e: e.tensor_scalar(out=dn[:, 4:8], in0=dn[:, 0:4], scalar1=1.0, scalar2=None,
                                                           op0=ALU.max), r=[Bdn], w=[Bdn])
                    P.add("dve", lambda e: e.reciprocal(out=dn[:, 8:12], in_=dn[:, 4:8]), r=[Bdn], w=[Bdn])
                    for h in range(MH):
                        hb = h // 2
                        hc = (h % 2) * 256
                        P.add("act", lambda e, h=h, hb=hb, hc=hc: e.activation(
                            out=junk, in_=psum[hb][:, hc:hc + 256], func=AF.Square, scale=dn[:, 8 + h:9 + h],
                            accum_out=dn[:, 12 + h:13 + h]), r=[Bps[hb], Bdn], w=[Bjunk, Bss])
                    P.add("act", lambda e: e.activation(out=dn[:, 16:20], in_=dn[:, 12:16], func=AF.Sqrt, scale=1.0 / 256,
                                                        bias=eps_t[:, 0:1]), r=[Bss], w=[Bss])
                    P.add("dve", lambda e: e.reciprocal(out=dn[:, 20:24], in_=dn[:, 16:20]), r=[Bss], w=[Bss])
                    P.add("dve", lambda e: e.tensor_tensor(out=dn[:, 24:28], in0=dn[:, 20:24], in1=dn[:, 8:12], op=ALU.mult),
                          r=[Bss, Bdn], w=[Bss])
                    for h in range(MH):
                        hb = h // 2
                        hc = (h % 2) * 256
                        P.add("dve", lambda e, h=h, hb=hb, hc=hc: e.scalar_tensor_tensor(
                            out=t2[h], in0=psum[hb][:, hc:hc + 256], scalar=dn[:, 24 + h:25 + h],
                            in1=gbc[:, h * 256:(h + 1) * 256], op0=ALU.mult, op1=ALU.mult),
                            r=[Bps[hb], Bss, Bg], w=[Bt2[h]])
                        P.add("pool", lambda e, h=h, w_=w_, s=s: e.tensor_tensor(
                            out=mo[w_][:, h * 256:(h + 1) * 256], in0=t2[h], in1=o4[s][:, h * 256:(h + 1) * 256],
                            op=ALU.mult), r=[Bt2[h], Bin[s]], w=[Bmo[w_]])

                    def do_transposes(w_=w_, tok=tok):
                        pv = ps_bf(7)
                        for q in range(8):
                            P.add("pe", lambda e, q=q, pv=pv: e.transpose(
                                out=pv[:, q * 128:(q + 1) * 128], in_=mo[w_][:, q * 128:(q + 1) * 128], identity=ident_b),
                                r=[Bmo[w_], B_cst], w=[Bps[7]])
                        P.add("act", lambda e, pv=pv: e.copy(out=moT[w_], in_=pv[:, 0:1024].rearrange(
                            "p (q t) -> p q t", q=8)), r=[Bps[7]], w=[BmoT[w_]])
                        P.add("sp", lambda e: e.dma_start(
                            out=moT_s[:, :, tok].rearrange("q p t -> p q t"), in_=moT[w_]),
                            r=[BmoT[w_]], w=[B["moT_s"]], dma=True)
                    pending.append(do_transposes)
            for fn in pending:
                fn()
            prefetch(nextw)
            P.barrier()


        def phase4(l, j, src_ap, Bsrc, trim=False, nextw=None):
            A.reset(KEEP2)
            r0 = j * T
            mT = A.alloc([128, KC, T], BF16)
            BmT = P.buf("mT")
            mark = A.off
            TH = T // 2
            aoT = A.alloc([128, NH, TH], BF16)
            moT = A.alloc([128, NH, TH], BF16)
            Bao, Bmo = P.buf("p4_ao"), P.buf("p4_mo")
            ws = WStream(8, 512, nslots=4, name="p4w")
            gat = [A.alloc([128, TH], BF16) for _ in range(2)]
            gmt = [A.alloc([128, TH], BF16) for _ in range(2)]
            Bgat = [P.buf("p4_ga%d" % i) for i in range(2)]
            Bgmt = [P.buf("p4_gm%d" % i) for i in range(2)]
            t1 = [A.alloc([128, 512], F32) for _ in range(2)]
            t2 = [A.alloc([128, 512], F32) for _ in range(2)]
            Bt1 = [P.buf("p4_t1%d" % i) for i in range(2)]
            Bt2 = [P.buf("p4_t2%d" % i) for i in range(2)]
            seq = [(th, n) for th in ([1] if trim else range(2)) for n in range(4)]

            def mk(idx):
                th, n = seq[idx]
                ta = ws.tile(key=(("p4a", l, j) if idx == 0 else None), src_fn=lambda k0, k1, n=n: w_pa[l, k0 * 128:k1 * 128, n * 512:(n + 1) * 512].rearrange(
                    "(c p) n -> p c n", p=128))
                tm = ws.tile(src_fn=lambda k0, k1, n=n: w_pm[l, k0 * 128:k1 * 128, n * 512:(n + 1) * 512].rearrange(
                    "(c p) n -> p c n", p=128))
                return ta, tm

            def load_gates(th, dc):
                gs = dc % 2
                c0 = r0 + th * TH
                P.add("sp", lambda e, gs=gs, dc=dc, c0=c0: e.dma_start(out=gat[gs], in_=gaT_s[dc, :, c0:c0 + TH]),
                      r=[B["gaT_s"]], w=[Bgat[gs]], dma=True)
                P.add("sp", lambda e, gs=gs, dc=dc, c0=c0: e.dma_start(out=gmt[gs], in_=gmT_s[dc, :, c0:c0 + TH]),
                      r=[B["gmT_s"]], w=[Bgmt[gs]], dma=True)

            cur = mk(0)
            cur[0].finish()
            cur[1].finish()
            load_gates(seq[0][0], seq[0][1] * 4)
            k = 0
            for idx, (th, n) in enumerate(seq):
                if n == 0:
                    c0 = r0 + th * TH
                    P.add("sp", lambda e, c0=c0: e.dma_start(
                        out=aoT, in_=aoT_s[:, :, c0:c0 + TH].rearrange("h p t -> p h t")),
                        r=[B["aoT_s"]], w=[Bao], dma=True)
                    P.add("sp", lambda e, c0=c0: e.dma_start(
                        out=moT, in_=moT_s[:, :, c0:c0 + TH].rearrange("h p t -> p h t")),
                        r=[B["moT_s"]], w=[Bmo], dma=True)
                nxt_t = mk(idx + 1) if idx + 1 < len(seq) else None
                wa, Bwa = cur[0].wb, cur[0].Bwb
                wm, Bwm = cur[1].wb, cur[1].Bwb
                un = 0
                for q in range(4):
                    dc = n * 4 + q
                    gs = dc % 2
                    if q < 3:
                        load_gates(th, dc + 1)
                    elif idx + 1 < len(seq):
                        load_gates(seq[idx + 1][0], seq[idx + 1][1] * 4)
                    for tg in ([TH // 512 - 1] if trim else range(TH // 512)):
                        if nxt_t is not None and un % 2 == 0:
                            nxt_t[(un // 2) % 2].step()
                        un += 1
                        tks = slice(tg * 512, (tg + 1) * 512)
                        otk = slice(th * TH + tg * 512, th * TH + (tg + 1) * 512)
                        pa = (k % 3)
                        pm = 3 + (k % 3)
                        u = k % 2
                        k += 1
                        mm_acc(psum[pa][:, :], Bps[pa], [(wa[:, c, q * 128:(q + 1) * 128], aoT[:, c, tks])
                                                         for c in range(8)], [Bwa, Bao])
                        mm_acc(psum[pm][:, :], Bps[pm], [(wm[:, c, q * 128:(q + 1) * 128], moT[:, c, tks])
                                                         for c in range(8)], [Bwm, Bmo])
                        P.add("dve", lambda e, pa=pa, u=u, gs=gs, tks=tks: e.tensor_tensor(
                            out=t1[u], in0=psum[pa][:, :], in1=gat[gs][:, tks], op=ALU.mult),
                            r=[Bps[pa], Bgat[gs]], w=[Bt1[u]])
                        P.add("dve", lambda e, pm=pm, u=u, gs=gs, tks=tks: e.tensor_tensor(
                            out=t2[u], in0=psum[pm][:, :], in1=gmt[gs][:, tks], op=ALU.mult),
                            r=[Bps[pm], Bgmt[gs]], w=[Bt2[u]])
                        P.add("dve", lambda e, u=u, dc=dc, otk=otk: e.tensor_tensor(
                            out=mT[:, dc, otk], in0=t1[u], in1=t2[u], op=ALU.add),
                            r=[Bt1[u], Bt2[u]], w=[BmT])
                if nxt_t is not None:
                    nxt_t[0].finish()
                    nxt_t[1].finish()
                cur = nxt_t
            prefetch(d_p4b(l, j))
            P.barrier()
            A.reset(mark)
            ws = WStream(KC, 512, name="p4o")
            xp = [A.alloc([128, 512], F32) for _ in range(3)]
            Bxp = [P.buf("p4_xp%d" % i) for i in range(3)]
            xo = [A.alloc([128, 512], F32) for _ in range(3)]
            Bxo = [P.buf("p4_xo%d" % i) for i in range(3)]
            osrcs = [dict(src_fn=lambda k0, k1, n=n: w_out[l, k0 * 128:k1 * 128, n * 512:(n + 1) * 512].rearrange(
                "(c p) n -> p c n", p=128)) for n in range(4)]
            osrcs[0]["key"] = ("p4b", l, j)

            tiles_i = [HT - 1] if trim else list(range(HT))
            units = [(n, i) for n in range(4) for i in tiles_i]

            def ld_x(kk):
                n, i = units[kk]
                u = kk % 3
                rows = slice(r0 + i * 128, r0 + (i + 1) * 128)
                cols = slice(n * 512, (n + 1) * 512)
                P.add("sp", lambda e, u=u, rows=rows, cols=cols: e.dma_start(out=xp[u], in_=src_ap[rows, cols]),
                      r=[Bsrc], w=[Bxp[u]], dma=True)

            ld_x(0)
            k = 0
            for n, wb, Bwb, tick in pipelined(ws, osrcs, len(tiles_i)):
                cols = slice(n * 512, (n + 1) * 512)
                for ui, i in enumerate(tiles_i):
                    tick(ui)
                    rows = slice(r0 + i * 128, r0 + (i + 1) * 128)
                    tok = slice(i * 128, (i + 1) * 128)
                    pb = k % 4
                    u = k % 3
                    if k + 1 < len(units):
                        ld_x(k + 1)
                    k += 1
                    mm_acc(psum[pb][:, :], Bps[pb], [(mT[:, c, tok], wb[:, c, :]) for c in range(KC)], [BmT, Bwb])
                    P.add("dve", lambda e, u=u, pb=pb: e.tensor_tensor(out=xo[u], in0=psum[pb][:, :], in1=xp[u],
                                                                        op=ALU.add), r=[Bps[pb], Bxp[u]], w=[Bxo[u]])
                    P.add("sp", lambda e, u=u, rows=rows, cols=cols: e.dma_start(out=x_mid[rows, cols], in_=xo[u]),
                          r=[Bxo[u]], w=[B["x_mid"]], dma=True)
            prefetch(nextw)
            P.barrier()

        def phase5(l, j, dst_ap, Bdst, dst_off=0, halo=False, nextw=None):
            A.reset(KEEP2)
            r0 = j * T
            h2T = A.alloc([128, KC, T], BF16)
            Bh2T = [P.buf("h2T%d" % i) for i in range(HT)]
            mark = A.off
            cw = A.alloc([128, 2 * FC, 3], F32)
            cb = A.alloc([128, 2 * FC], F32)
            Bcw = P.buf("p5_cw")
            P.add("sp", lambda e: e.dma_start(out=cw, in_=cw_d[l]), r=[B_w], w=[Bcw], dma=True)
            P.add("sp", lambda e: e.dma_start(out=cb, in_=cb_d[l]), r=[B_w], w=[Bcw], dma=True)
            if j == 0:
                P.add("pool", lambda e: e.memset(hist, 0.0), w=[Bhist])
            elif not halo:
                P.add("dve", lambda e: e.tensor_scalar(out=hist, in0=hist, scalar1=flg[:, 0:1], scalar2=None,
                                                       op0=ALU.mult), r=[Bhist, B_cst], w=[Bhist])
            if halo:
                xh = A.alloc([2, D], F32)
                hh = A.alloc([2, D], BF16)
                hjunk = A.alloc([2, D], BF16)
                g2b = A.alloc([2, D], F32)
                hst = A.alloc([2, 4], F32)
                h2h = A.alloc([128, KC, 2], BF16)
                Bhalo = P.buf("p5_halo")
                P.add("sp", lambda e: e.dma_start(out=xh, in_=x_mid[r0 - 2:r0, :]), r=[B["x_mid"]], w=[Bhalo], dma=True)
                P.add("sp", lambda e: e.dma_start(out=g2b, in_=g2_d[l:l + 1, :].partition_broadcast(2)),
                      r=[B_w], w=[Bhalo], dma=True)
                P.add("act", lambda e: e.activation(out=hjunk, in_=xh, func=AF.Square, accum_out=hst[:, 0:1]),
                      r=[Bhalo], w=[Bhalo])
                P.add("act", lambda e: e.activation(out=hst[:, 1:2], in_=hst[:, 0:1], func=AF.Sqrt, scale=1.0 / D,
                                                    bias=eps_t[0:2, 0:1]), r=[Bhalo], w=[Bhalo])
                P.add("dve", lambda e: e.reciprocal(out=hst[:, 2:3], in_=hst[:, 1:2]), r=[Bhalo], w=[Bhalo])
                P.add("dve", lambda e: e.scalar_tensor_tensor(out=hh, in0=xh, scalar=hst[:, 2:3], in1=g2b,
                                                              op0=ALU.mult, op1=ALU.mult), r=[Bhalo], w=[Bhalo])
                pvh = ps_bf(6)
                for c in range(KC):
                    P.add("pe", lambda e, c=c: e.transpose(out=pvh[:, 2 * c:2 * c + 2],
                                                           in_=hh[0:2, c * 128:(c + 1) * 128],
                                                           identity=ident_b[0:2, 0:2]), r=[Bhalo, B_cst], w=[Bps[6]])
                P.add("dve", lambda e: e.tensor_copy(out=h2h, in_=pvh[:, 0:2 * KC].rearrange("p (c t) -> p c t", t=2)),
                      r=[Bps[6]], w=[Bhalo])
            ws = WStream(KC, 256, name="p5u")
            av = [[A.alloc([128, 512], F32) for _ in range(3)] for _ in range(2)]
            Bav = [[P.buf("p5_a%d%d" % (x, i)) for i in range(3)] for x in range(2)]
            sg = [A.alloc([128, 512], F32) for _ in range(3)]
            Bsg = [P.buf("p5_sg%d" % i) for i in range(3)]
            ast = [A.alloc([128, 512], BF16) for _ in range(3)]
            Bast = [P.buf("p5_ast%d" % i) for i in range(3)]
            k = 0
            usrcs = [dict(srcs=[
                (0, 128, lambda k0, k1, fc=fc: w_up[l, k0 * 128:k1 * 128, fc * 128:(fc + 1) * 128].rearrange(
                    "(c p) n -> p c n", p=128)),
                (128, 256, lambda k0, k1, fc=fc: w_up[l, k0 * 128:k1 * 128, DFF + fc * 128:DFF + (fc + 1) * 128].rearrange(
                    "(c p) n -> p c n", p=128))]) for fc in range(FC)]
            first_tile = ws.tile(key=("p5b", l, j), **usrcs[0])
            first_tile.finish()
            norm_it = phase_norm(x_mid, B["x_mid"], r0, h2T, Bh2T, g2_d[l:l + 1, :])
            for _ in norm_it:
                pass
            for fc, wb, Bwb, tick in pipelined(ws, usrcs, TG, first=first_tile):
                if halo:
                    for x in range(2):
                        hb = 6 + x
                        fcx = x * FC + fc
                        mm_acc(psum[hb][:, 0:2], Bps[hb], [(wb[:, c, x * 128:(x + 1) * 128], h2h[:, c, :])
                                                           for c in range(KC)], [Bwb, Bhalo])
                        P.add("dve", lambda e, hb=hb, fcx=fcx: e.tensor_scalar(
                            out=hist[:, fcx, 0:2], in0=psum[hb][:, 0:2], scalar1=flg[:, 0:1], scalar2=None,
                            op0=ALU.mult), r=[Bps[hb], B_cst, Bhist], w=[Bhist])
                for tg in range(TG):
                    tick(tg)
                    tks = slice(tg * 512, (tg + 1) * 512)
                    u = k % 3
                    k3 = k
                    k += 1
                    first = (j == 0 and tg == 0)
                    for x in range(2):
                        pb = 3 * x + (k3 % 3)
                        fcx = x * FC + fc
                        a = av[x][u]
                        Ba = Bav[x][u]
                        mm_acc(psum[pb][:, :], Bps[pb], [(wb[:, c, x * 128:(x + 1) * 128], h2T[:, c, tks])
                                                         for c in range(KC)], [Bwb] + Bh2T[4 * tg:4 * tg + 4])
                        P.add("act", lambda e, a=a, pb=pb, fcx=fcx: e.activation(
                            out=a, in_=psum[pb][:, :], func=AF.Identity, scale=cw[:, fcx, 2:3],
                            bias=cb[:, fcx:fcx + 1]), r=[Bps[pb], Bcw], w=[Ba])
                        P.add("dve", lambda e, a=a, pb=pb, fcx=fcx: e.scalar_tensor_tensor(
                            out=a[:, 1:512], in0=psum[pb][:, 0:511], scalar=cw[:, fcx, 1:2], in1=a[:, 1:512],
                            op0=ALU.mult, op1=ALU.add), r=[Bps[pb], Bcw, Ba], w=[Ba])
                        P.add("dve", lambda e, a=a, pb=pb, fcx=fcx: e.scalar_tensor_tensor(
                            out=a[:, 2:512], in0=psum[pb][:, 0:510], scalar=cw[:, fcx, 0:1], in1=a[:, 2:512],
                            op0=ALU.mult, op1=ALU.add), r=[Bps[pb], Bcw, Ba], w=[Ba])
                        if not first:
                            P.add("dve", lambda e, a=a, fcx=fcx: e.scalar_tensor_tensor(
                                out=a[:, 0:1], in0=hist[:, fcx, 1:2], scalar=cw[:, fcx, 1:2], in1=a[:, 0:1],
                                op0=ALU.mult, op1=ALU.add), r=[Bhist, Bcw, Ba], w=[Ba])
                            P.add("dve", lambda e, a=a, fcx=fcx: e.scalar_tensor_tensor(
                                out=a[:, 0:2], in0=hist[:, fcx, 0:2], scalar=cw[:, fcx, 0:1], in1=a[:, 0:2],
                                op0=ALU.mult, op1=ALU.add), r=[Bhist, Bcw, Ba], w=[Ba])
                        P.add("dve", lambda e, pb=pb, fcx=fcx: e.tensor_copy(out=hist[:, fcx, 0:2],
                                                                              in_=psum[pb][:, 510:512]),
                              r=[Bps[pb], Bhist], w=[Bhist])
                    P.add("act", lambda e, u=u: e.activation(out=sg[u], in_=av[0][u], func=AF.Silu),
                          r=[Bav[0][u]], w=[Bsg[u]])
                    o = k % 3
                    P.add("pool", lambda e, u=u, o=o: e.tensor_tensor(out=ast[o], in0=sg[u], in1=av[1][u], op=ALU.mult),
                          r=[Bsg[u], Bav[1][u]], w=[Bast[o]])
                    ti0 = (r0 + tg * 512) // 128
                    P.add("sp", lambda e, o=o, fc=fc, ti0=ti0: e.dma_start(
                        out=actT_s[ti0:ti0 + 4, :, fc, :].rearrange("i p t -> p i t"),
                        in_=ast[o][:, :].rearrange("p (i t) -> p i t", i=4)),
                        r=[Bast[o]], w=[B["actT_s"]], dma=True)
            prefetch(d_p5c(l, j))
            P.barrier()
            A.reset(KEEP2)
            ws = WStream(FC, 512, name="p5d")
            at = [A.alloc([128, FC, 128], BF16) for _ in range(2)]
            Bat = [P.buf("p5_at%d" % i) for i in range(2)]
            xp = [A.alloc([128, 512], F32) for _ in range(3)]
            Bxp = [P.buf("p5_xp%d" % i) for i in range(3)]
            xo = [A.alloc([128, 512], F32) for _ in range(3)]
            Bxo = [P.buf("p5_xo%d" % i) for i in range(3)]
            dsrcs = [dict(src_fn=lambda k0, k1, n=n: w_down[l, k0 * 128:k1 * 128, n * 512:(n + 1) * 512].rearrange(
                "(c p) n -> p c n", p=128)) for n in range(4)]
            dsrcs[0]["key"] = ("p5c", l, j)

            def ld_in(kk):
                n, i = divmod(kk, HT)
                u = kk % 3
                a = kk % 2
                rows = slice(r0 + i * 128, r0 + (i + 1) * 128)
                cols = slice(n * 512, (n + 1) * 512)
                ti = (r0 + i * 128) // 128
                P.add("sp", lambda e, a=a, ti=ti: e.dma_start(out=at[a], in_=actT_s[ti]),
                      r=[B["actT_s"]], w=[Bat[a]], dma=True)
                P.add("sp", lambda e, u=u, rows=rows, cols=cols: e.dma_start(out=xp[u], in_=x_mid[rows, cols]),
                      r=[B["x_mid"]], w=[Bxp[u]], dma=True)

            ld_in(0)
            k = 0
            for n, wb, Bwb, tick in pipelined(ws, dsrcs, HT):
                cols = slice(n * 512, (n + 1) * 512)
                for i in range(HT):
                    tick(i)
                    rows = slice(r0 + i * 128, r0 + (i + 1) * 128)
                    pb = k % 4
                    u = k % 3
                    a = k % 2
                    if k + 1 < 4 * HT:
                        ld_in(k + 1)
                    k += 1
                    mm_acc(psum[pb][:, :], Bps[pb], [(at[a][:, f, :], wb[:, f, :]) for f in range(FC)], [Bat[a], Bwb])
                    P.add("dve", lambda e, u=u, pb=pb: e.tensor_tensor(out=xo[u], in0=psum[pb][:, :], in1=xp[u],
                                                                        op=ALU.add), r=[Bps[pb], Bxp[u]], w=[Bxo[u]])
                    drows = slice(rows.start - dst_off, rows.stop - dst_off)
                    P.add("sp", lambda e, u=u, drows=drows, cols=cols: e.dma_start(out=dst_ap[drows, cols], in_=xo[u]),
                          r=[Bxo[u]], w=[Bdst], dma=True)
            prefetch(nextw)
            P.barrier()

        eps_t = A.alloc([128, 1], F32)
        one_t = A.alloc([128, 1], F32)
        lnsc_t = A.alloc([128, 1], F32)
        P.add("pool", lambda e: e.memset(eps_t, EPS), w=[B_cst])
        P.add("pool", lambda e: e.memset(one_t, 1.0), w=[B_cst])
        P.add("pool", lambda e: e.memset(lnsc_t, math.log(128 ** -0.5)), w=[B_cst])
        hist = A.alloc([128, 2 * FC, 2], F32)
        Bhist = P.buf("hist")
        flg = A.alloc([128, 1], F32)
        P.add("sp", lambda e: e.dma_start(out=flg, in_=flag_d), r=[B_w], w=[B_cst], dma=True)
        PST.extend(A.alloc([128, 2048], F32) for _ in range(4))
        for i in range(4):
            b = P.buf("pst%d" % i)
            b.persistent = True
            BPST.append(b)
        KEEP2 = A.off
        P.barrier()

        plan = []
        for l in range(n_layers):
            last = (l == n_layers - 1)
            for j in range(2):
                plan.append(("p1", l, j, d_p1(l, j)))
            plan.append(("p2", l, 0, None))
            plan.append(("p3", l, 0, None))
            for j in range(2):
                plan.append(("p4", l, j, d_p4a(l, j)))
            for j in range(2):
                if not (last and j == 0):
                    plan.append(("p5", l, j, d_p5b(l, j)))
        for idx, (ph, l, j, _) in enumerate(plan):
            nextw = None
            for later in plan[idx + 1:]:
                if later[3] is not None:
                    nextw = later[3]
                    break
            last = (l == n_layers - 1)
            src_ap, Bsrc = (x_in, B_x_in) if l == 0 else (x_l1, B["x_l1"])
            if ph == "p1":
                phase1(l, j, src_ap, Bsrc, trim=(last and j == 0), nextw=nextw)
            elif ph == "p2":
                phase2(l, trim=last, nextw=nextw)
            elif ph == "p3":
                phase3(l, trim=last, nextw=nextw)
            elif ph == "p4":
                phase4(l, j, src_ap, Bsrc, trim=(last and j == 0), nextw=nextw)
            else:
                if last:
                    phase5(l, j, y_out, B["y"], dst_off=T, halo=True, nextw=nextw)
                else:
                    phase5(l, j, x_l1, B["x_l1"], nextw=nextw)

        P.barrier(final=True)
        P.emit()
    return nc


def _consts():
    c = np.zeros((128, 3, 128), np.float32)
    c[:, 0, :] = np.eye(128, dtype=np.float32)
    c[:, 1, :] = np.triu(np.ones((128, 128), np.float32))
    c[:, 2, :] = 1.0
    return c


def prep_shared(inp):
    f = lambda a: np.ascontiguousarray(np.asarray(a, dtype=np.float32))
    sh = {}
    sh["w_in"] = f(inp["w_in"])
    sh["w_pa"] = f(inp["w_proj_a"])
    sh["w_pm"] = f(inp["w_proj_m"])
    sh["w_out"] = f(inp["w_out"])
    sh["w_up"] = f(inp["w_up"])
    sh["w_down"] = f(inp["w_down"])
    sh["g1"] = f(inp["norm1_g"])
    sh["g2"] = f(inp["norm2_g"])
    gb = np.concatenate([np.asarray(inp["fox_f_bias"]), np.asarray(inp["m_f_bias"]), np.asarray(inp["m_i_bias"])], axis=1)
    sh["gbias"] = f(np.broadcast_to(gb[:, None, :], (NL, 128, 16)))
    sh["qkg"] = f(np.stack([np.asarray(inp["q_norm_g"]), np.asarray(inp["k_norm_g"])], axis=2))
    sh["mng"] = f(np.broadcast_to(np.asarray(inp["m_norm_g"]).reshape(NL, 1, 1024), (NL, 128, 1024)))
    sh["cw"] = f(np.asarray(inp["conv_w"]).reshape(NL, 3, 2 * FC, 128).transpose(0, 3, 2, 1))
    sh["cb"] = f(np.asarray(inp["conv_b"]).reshape(NL, 2 * FC, 128).transpose(0, 2, 1))
    sh["consts"] = _consts()
    return sh


def kernel(**inputs):
    x = np.asarray(inputs["x"], dtype=np.float32)
    Bn, S, _ = x.shape
    T = S // 2
    sh = prep_shared(inputs)
    nc = build(S=S)
    in_maps = []
    for b in range(Bn):
        for r in range(2):
            m = dict(sh)
            if r == 1:
                m["x"] = np.ascontiguousarray(x[b])
            else:
                m["x"] = np.ascontiguousarray(np.concatenate([np.zeros((T, D), np.float32), x[b, :T]], axis=0))
            m["flag"] = np.full((128, 1), float(r), np.float32)
            in_maps.append(m)
    res = run_bass_kernel_spmd(nc, in_maps, core_ids=list(range(2 * Bn)))
    out = np.empty((Bn, S, D), np.float32)
    for b in range(Bn):
        out[b, :T] = np.asarray(res.results[2 * b]["y"])
        out[b, T:] = np.asarray(res.results[2 * b + 1]["y"])
    return out
```
